# Optimizing a Trainium2 kernel written in Bass

```python
import jax
import jax.numpy as jnp
from jax import lax
import numpy as np

D_MODEL = 1024
BATCH = 2
SEQ = 16384
DEPTH = 2

GRID_W = 64
CTX_LEN = 256
N_EVEN = (DEPTH + 1) // 2
N_ODD = DEPTH // 2
DEEPNORM_ALPHA = (2 * DEPTH) ** 0.25
DEEPNORM_BETA = (8 * DEPTH) ** -0.25
LN_EPS = 1e-6
RMS_EPS = 1e-6

LRU_WIDTH = D_MODEL
LRU_HEADS = 16
LRU_HEAD_DIM = LRU_WIDTH // LRU_HEADS
CONV_W = 4
LRU_C = 8.0
FNET_GROUPS = 4
FNET_WIDTH = D_MODEL // 2
FNET_GROUP_DIM = FNET_WIDTH // FNET_GROUPS
MIX0_WIDTH = LRU_WIDTH + FNET_WIDTH
IN0_WIDTH = 2 * MIX0_WIDTH

MLA_HEADS = 16
Q_LORA = 256
KV_LORA = 128
QK_NOPE = 64
QK_ROPE = 32
V_DIM = 64
QK_DIM = QK_NOPE + QK_ROPE
MLA_WIDTH = MLA_HEADS * V_DIM
IN1_WIDTH = Q_LORA + KV_LORA + QK_ROPE + MLA_WIDTH
ROPE_PAIRS = QK_ROPE // 4
ROPE_THETA = 10000.0
ATTN_SCALE = QK_DIM ** -0.5
Q_BLOCK = 128

kernel_name = 'hybrid_rglru_fnet_mla_dit'


def layer_norm(x, g, b):
    xf = x.astype(jnp.float32)
    mu = xf.mean(-1, keepdims=True)
    var = jnp.square(xf - mu).mean(-1, keepdims=True)
    return ((xf - mu) * lax.rsqrt(var + LN_EPS)).astype(x.dtype) * g + b


def rms_norm(x, g):
    xf = x.astype(jnp.float32)
    return (xf * lax.rsqrt(jnp.mean(xf * xf, -1, keepdims=True) + RMS_EPS)).astype(x.dtype) * g


def adaln(cond, w, b):
    return jnp.split(jax.nn.silu(cond) @ w + b, 3, axis=-1)


def centred_dwconv(x, w, b):
    length = x.shape[1]
    left = (CONV_W - 1) // 2
    xp = jnp.pad(x, ((0, 0), (left, CONV_W - 1 - left), (0, 0)))
    return sum(xp[:, k:k + length] * w[k] for k in range(CONV_W)) + b


def rglru_coeffs(xc, gate_w, gate_b, lam):
    b_, length, _ = xc.shape
    xh = xc.reshape(b_, length, LRU_HEADS, LRU_HEAD_DIM)
    g = jnp.einsum('blhi,ghij->gblhj', xh, gate_w).reshape(2, b_, length, LRU_WIDTH)
    g = (g + gate_b[:, None, None, :]).astype(jnp.float32)
    r = jax.nn.sigmoid(g[0])
    i = jax.nn.sigmoid(g[1])
    log_a = -LRU_C * r * jax.nn.softplus(-lam.astype(jnp.float32))
    a = jnp.exp(log_a)
    b = jnp.sqrt(-jnp.expm1(2.0 * log_a)) * (i * xc.astype(jnp.float32))
    return a, b


def linear_scan(a, b, h0):
    def combine(e1, e2):
        a1, b1 = e1
        a2, b2 = e2
        return a1 * a2, a2 * b1 + b2
    a_cum, b_cum = lax.associative_scan(combine, (a, b), axis=1)
    if h0 is None:
        return b_cum
    return a_cum * h0[:, None, :] + b_cum


def rglru_bidirectional(x_ctx, x_lat, gate_w, gate_b, lam, ctx_out):
    y_lat, y_ctx = [], []
    for d in range(2):
        rev = d == 1
        xc = jnp.flip(x_ctx, axis=1) if rev else x_ctx
        xl = jnp.flip(x_lat, axis=1) if rev else x_lat
        a_c, b_c = rglru_coeffs(xc, gate_w[d], gate_b[d], lam[d])
        h_c = linear_scan(a_c, b_c, None)
        a_l, b_l = rglru_coeffs(xl, gate_w[d], gate_b[d], lam[d])
        h_l = linear_scan(a_l, b_l, h_c[:, -1])
        y_lat.append(jnp.flip(h_l, axis=1) if rev else h_l)
        y_ctx.append(jnp.flip(h_c, axis=1) if rev else h_c)
    out_lat = (y_lat[0] + y_lat[1]).astype(x_lat.dtype)
    out_ctx = (y_ctx[0] + y_ctx[1]).astype(x_ctx.dtype) if ctx_out else None
    return out_lat, out_ctx


def fourier_mix(u, w, b):
    b_, length, _ = u.shape
    ug = u.reshape(b_, length, FNET_GROUPS, FNET_GROUP_DIM).astype(jnp.float32)
    f = jnp.fft.fft2(ug, axes=(1, 3), norm='ortho').real.astype(u.dtype)
    return jnp.einsum('blgi,gij->blgj', f, w).reshape(b_, length, FNET_WIDTH) + b


def recurrent_fourier_mixer(u_lat, u_ctx, w_in, conv_w, conv_b, gate_w, gate_b, lam,
                            fnet_w, fnet_b, w_out, ctx_out):
    z_lat = u_lat @ w_in
    z_ctx = u_ctx @ w_in
    xl_lat = centred_dwconv(z_lat[..., :LRU_WIDTH], conv_w, conv_b)
    xl_ctx = centred_dwconv(z_ctx[..., :LRU_WIDTH], conv_w, conv_b)
    r_lat, r_ctx = rglru_bidirectional(xl_ctx, xl_lat, gate_w, gate_b, lam, ctx_out)

    def merge(z, r):
        f = fourier_mix(z[..., LRU_WIDTH:MIX0_WIDTH], fnet_w, fnet_b)
        return (jnp.concatenate([r, f], axis=-1) * jax.nn.silu(z[..., MIX0_WIDTH:])) @ w_out

    y_lat = merge(z_lat, r_lat)
    y_ctx = merge(z_ctx, r_ctx) if ctx_out else None
    return y_lat, y_ctx


def axial_rope_tables(rows):
    inv = ROPE_THETA ** (-jnp.arange(ROPE_PAIRS, dtype=jnp.float32) / ROPE_PAIRS)
    row = jnp.broadcast_to(jnp.arange(rows, dtype=jnp.float32)[:, None], (rows, GRID_W)).reshape(-1)
    col = jnp.broadcast_to(jnp.arange(GRID_W, dtype=jnp.float32)[None, :], (rows, GRID_W)).reshape(-1)
    ang = jnp.stack([row[:, None] * inv, col[:, None] * inv], axis=1)
    return jnp.cos(ang), jnp.sin(ang)


def apply_axial_rope(x, cos, sin):
    xs = x.reshape(*x.shape[:-1], 2, 2, ROPE_PAIRS)
    x1, x2 = xs[..., 0, :], xs[..., 1, :]
    c = cos[:, None].astype(x.dtype)
    s = sin[:, None].astype(x.dtype)
    return jnp.stack([x1 * c - x2 * s, x2 * c + x1 * s], axis=-2).reshape(x.shape)


def attend(q, k, v):
    s = jnp.einsum('bqhd,bkhd->bhqk', q, k, preferred_element_type=jnp.float32) * ATTN_SCALE
    p = jax.nn.softmax(s, axis=-1).astype(v.dtype)
    return jnp.einsum('bhqk,bkhd->bqhd', p, v)


def blocked_attention(q, k, v):
    b_, s_, h_, dq = q.shape
    nb = s_ // Q_BLOCK
    qb = jnp.moveaxis(q.reshape(b_, nb, Q_BLOCK, h_, dq), 1, 0)
    ob = lax.map(lambda qi: attend(qi, k, v), qb)
    return jnp.moveaxis(ob, 0, 1).reshape(b_, s_, h_, V_DIM)


def mla_mixer(u_lat, u_ctx, w_in, q_norm_g, kv_norm_g, w_uq, w_ukv, w_out, cos, sin, ctx_out):
    bounds = [Q_LORA, Q_LORA + KV_LORA, Q_LORA + KV_LORA + QK_ROPE]
    qc_l, kvc_l, kr_l, g_l = jnp.split(u_lat @ w_in, bounds, axis=-1)
    qc_c, kvc_c, kr_c, g_c = jnp.split(u_ctx @ w_in, bounds, axis=-1)

    def expand_q(qc):
        b_, length, _ = qc.shape
        return (rms_norm(qc, q_norm_g) @ w_uq).reshape(b_, length, MLA_HEADS, QK_DIM)

    def expand_kv(kvc, k_rope):
        b_, length, _ = kvc.shape
        kv = (rms_norm(kvc, kv_norm_g) @ w_ukv).reshape(b_, length, MLA_HEADS, QK_NOPE + V_DIM)
        k_rope = jnp.broadcast_to(k_rope[:, :, None, :], (b_, length, MLA_HEADS, QK_ROPE))
        return jnp.concatenate([kv[..., :QK_NOPE], k_rope], axis=-1), kv[..., QK_NOPE:]

    b_, s_, _ = u_lat.shape
    q_l = expand_q(qc_l)
    q_l = jnp.concatenate([q_l[..., :QK_NOPE], apply_axial_rope(q_l[..., QK_NOPE:], cos, sin)], axis=-1)
    kr_l = apply_axial_rope(kr_l[:, :, None, :], cos, sin)[:, :, 0]
    k_l, v_l = expand_kv(kvc_l, kr_l)
    k_c, v_c = expand_kv(kvc_c, kr_c)
    k_all = jnp.concatenate([k_c, k_l], axis=1)
    v_all = jnp.concatenate([v_c, v_l], axis=1)
    o_l = blocked_attention(q_l, k_all, v_all).reshape(b_, s_, MLA_WIDTH)
    y_lat = (o_l * jax.nn.silu(g_l)) @ w_out
    y_ctx = None
    if ctx_out:
        o_c = attend(expand_q(qc_c), k_c, v_c).reshape(b_, u_ctx.shape[1], MLA_WIDTH)
        y_ctx = (o_c * jax.nn.silu(g_c)) @ w_out
    return y_lat, y_ctx


def setup_inputs(seed: int = 0) -> dict:
    key = jax.random.key(seed)
    keys = iter(jax.random.split(key, 40))

    def normal(shape, scale):
        return jax.random.normal(next(keys), shape, jnp.float32) * scale

    D = D_MODEL
    a0 = jax.random.uniform(next(keys), (N_EVEN, 2, LRU_WIDTH), jnp.float32, 0.9, 0.999)
    s = a0 ** (1.0 / LRU_C)
    lru_lambda = jnp.log(s) - jnp.log1p(-s)
    return {
        'x': normal((BATCH, SEQ, D), 1.0),
        'c': normal((BATCH, D), 1.0),
        'ctx': normal((BATCH, CTX_LEN, D), 1.0),
        'c_ctx': normal((D,), 1.0),
        'ada_w': normal((DEPTH, D, 3 * D), D ** -0.5),
        'ada_b': normal((DEPTH, 3 * D), 0.02),
        'ln_g': 1.0 + normal((DEPTH, D), 0.02),
        'ln_b': normal((DEPTH, D), 0.02),
        'w_in_rf': normal((N_EVEN, D, IN0_WIDTH), D ** -0.5),
        'conv_w': normal((N_EVEN, CONV_W, LRU_WIDTH), CONV_W ** -0.5),
        'conv_b': normal((N_EVEN, LRU_WIDTH), 0.02),
        'lru_gate_w': normal((N_EVEN, 2, 2, LRU_HEADS, LRU_HEAD_DIM, LRU_HEAD_DIM), LRU_HEAD_DIM ** -0.5),
        'lru_gate_b': normal((N_EVEN, 2, 2, LRU_WIDTH), 0.02),
        'lru_lambda': lru_lambda,
        'fnet_w': normal((N_EVEN, FNET_GROUPS, FNET_GROUP_DIM, FNET_GROUP_DIM), FNET_GROUP_DIM ** -0.5),
        'fnet_b': normal((N_EVEN, FNET_WIDTH), 0.02),
        'w_out_rf': normal((N_EVEN, MIX0_WIDTH, D), MIX0_WIDTH ** -0.5 * DEEPNORM_BETA),
        'w_in_mla': normal((N_ODD, D, IN1_WIDTH), D ** -0.5),
        'q_norm_g': 1.0 + normal((N_ODD, Q_LORA), 0.02),
        'kv_norm_g': 1.0 + normal((N_ODD, KV_LORA), 0.02),
        'w_uq': normal((N_ODD, Q_LORA, MLA_HEADS * QK_DIM), Q_LORA ** -0.5),
        'w_ukv': normal((N_ODD, KV_LORA, MLA_HEADS * (QK_NOPE + V_DIM)), KV_LORA ** -0.5),
        'w_out_mla': normal((N_ODD, MLA_WIDTH, D), MLA_WIDTH ** -0.5 * DEEPNORM_BETA),
    }


def reference(x, c, ctx, c_ctx, ada_w, ada_b, ln_g, ln_b,
              w_in_rf, conv_w, conv_b, lru_gate_w, lru_gate_b, lru_lambda, fnet_w, fnet_b, w_out_rf,
              w_in_mla, q_norm_g, kv_norm_g, w_uq, w_ukv, w_out_mla):
    rows = x.shape[1] // GRID_W
    cos, sin = axial_rope_tables(rows)
    h_lat, h_ctx = x, ctx
    for layer in range(DEPTH):
        j = layer // 2
        ctx_out = layer < DEPTH - 1
        shift, scale, gate = adaln(c, ada_w[layer], ada_b[layer])
        shift_c, scale_c, gate_c = adaln(c_ctx, ada_w[layer], ada_b[layer])
        u_lat = h_lat * (1.0 + scale[:, None, :]) + shift[:, None, :]
        u_ctx = h_ctx * (1.0 + scale_c) + shift_c
        if layer % 2 == 0:
            y_lat, y_ctx = recurrent_fourier_mixer(
                u_lat, u_ctx, w_in_rf[j], conv_w[j], conv_b[j], lru_gate_w[j], lru_gate_b[j],
                lru_lambda[j], fnet_w[j], fnet_b[j], w_out_rf[j], ctx_out)
        else:
            y_lat, y_ctx = mla_mixer(
                u_lat, u_ctx, w_in_mla[j], q_norm_g[j], kv_norm_g[j], w_uq[j], w_ukv[j],
                w_out_mla[j], cos, sin, ctx_out)
        h_lat = layer_norm(DEEPNORM_ALPHA * h_lat + gate[:, None, :] * y_lat, ln_g[layer], ln_b[layer])
        if ctx_out:
            h_ctx = layer_norm(DEEPNORM_ALPHA * h_ctx + gate_c * y_ctx, ln_g[layer], ln_b[layer])
    return h_lat
```

```python
import contextlib
import numpy as np
import ml_dtypes
import concourse.bass as bass
import concourse.mybir as mybir
from concourse.bass_utils import run_bass_kernel_spmd

F32 = mybir.dt.float32
BF16 = mybir.dt.bfloat16
I32 = mybir.dt.int32
AF = mybir.ActivationFunctionType
ALU = mybir.AluOpType

D = 1024
SL = 16384
SC = 256
TA = SL + SC
TOK = 4096
NH = 16
ALPHA = 4.0 ** 0.25
LN_EPS = 1e-6
RMS_EPS = 1e-6
ATTN_SCALE = 96.0 ** -0.5


class Buf:
    __slots__ = ("t", "lw", "rd", "dsem", "dcnt", "name")

    def __init__(self, t, name=""):
        self.t = t
        self.lw = {}
        self.rd = {}
        self.dsem = None
        self.dcnt = 0
        self.name = name

    def __getitem__(self, idx):
        return self.t[idx]


class Sync:
    ENG = ("pe", "act", "dve", "pool", "sp")
    NPOOL = 64

    def __init__(self, nc, stack, same_engine_wait=True):
        self.nc = nc
        self.stack = stack
        self.sems = {}
        self.cnt = {}
        for e in ("pe", "act", "dve", "pool"):
            self.sems[e] = stack.enter_context(nc.semaphore("c_" + e))
            self.cnt[e] = 0
        self.free = {"hw": [], "sw": []}
        for i in range(self.NPOOL):
            k = "d%d" % i
            self.sems[k] = stack.enter_context(nc.semaphore(k))
            self.cnt[k] = 0
            self.free["hw" if i < 38 else "sw"].append(k)
        self.known = {e: {} for e in self.ENG}
        self.same = same_engine_wait
        self.nwaits = 0
        self.nins = 0
        self.nalloc = 0
        self.owners = []
        self.group = None
        self.prog = {e: [] for e in self.ENG}

    def sbuf(self, name, shape, dt):
        self.nalloc += 1
        t = self.stack.enter_context(self.nc.sbuf_tensor("s%d_%s" % (self.nalloc, name), list(shape), dt))
        return Buf(t, name)

    def psum(self, name, shape, dt):
        self.nalloc += 1
        t = self.stack.enter_context(self.nc.psum_tensor("p%d_%s" % (self.nalloc, name), list(shape), dt))
        return Buf(t, name)

    def view(self, t, name=""):
        return Buf(t, name)

    def _dsem(self, b, q):
        kind = "sw" if q == "pool" else "hw"
        if b.dsem is None:
            b.dsem = self.free[kind].pop()
            self.owners.append(b)
        elif (int(b.dsem[1:]) >= 38) != (kind == "sw"):
            raise RuntimeError("buffer %s mixes software- and hardware-DGE DMAs on one semaphore" % b.name)
        return b.dsem

    def release(self):
        for b in self.owners:
            self.free["sw" if int(b.dsem[1:]) >= 38 else "hw"].append(b.dsem)
            b.dsem = None
        self.owners = []

    def group_begin(self):
        self.group = (Buf(None, "grp"), [])

    def group_end(self):
        g, bufs = self.group
        self.group = None
        if g.dsem is None:
            return
        k = g.dsem
        for b in bufs:
            b.lw = {k: self.cnt[k]}

    def _waits(self, q, R, W):
        need = {}
        for b in R:
            for k, v in b.lw.items():
                if need.get(k, 0) < v:
                    need[k] = v
        for b in W:
            for d in (b.lw, b.rd):
                for k, v in d.items():
                    if need.get(k, 0) < v:
                        need[k] = v
        kn = self.known[q]
        for k, v in need.items():
            if k == q and (q == "pe" or not self.same):
                continue
            if kn.get(k, 0) >= v:
                continue
            self.prog[q].append(("w", self.sems[k], v))
            kn[k] = v
            self.nwaits += 1

    def _post(self, ev, R, W):
        k, v = ev
        for b in W:
            b.lw = {k: v}
            b.rd = {}
        for b in R:
            if b not in W:
                b.rd[k] = v

    def op(self, q, fn, R=(), W=()):
        self._waits(q, R, W)
        self.cnt[q] += 1
        self.prog[q].append(("i", fn, self.sems[q], 1))
        self._post((q, self.cnt[q]), R, W)
        self.nins += 1

    def dma(self, q, pairs, R=(), W=(), owner=None):
        if self.group is not None and owner is None:
            owner = self.group[0]
            self.group[1].extend(W)
        if owner is None:
            owner = W[0] if W else R[0]
        k = self._dsem(owner, q)
        self._waits(q, R, W)
        for (o, i) in pairs:
            self.prog[q].append(("i", (lambda e, o=o, i=i: e.dma_start(out=o, in_=i)), self.sems[k], 16))
            self.cnt[k] += 16
        self._post((k, self.cnt[k]), R, W)
        self.nins += len(pairs)

    def special(self, q, fn, inc, R=(), W=(), owner=None):
        if owner is None:
            owner = W[0]
        k = self._dsem(owner, q)
        self._waits(q, R, W)
        self.prog[q].append(("i", fn, self.sems[k], inc))
        self.cnt[k] += inc
        self._post((k, self.cnt[k]), R, W)
        self.nins += 1

    def barrier(self):
        for q in self.ENG:
            kn = self.known[q]
            for k, v in self.cnt.items():
                if v and kn.get(k, 0) < v:
                    self.prog[q].append(("w", self.sems[k], v))
                    kn[k] = v
                    self.nwaits += 1
        self.release()

    def flush(self):
        prog = self.prog
        self.prog = {e: [] for e in self.ENG}

        def run(eng, items):
            for it in items:
                if it[0] == "w":
                    eng.wait_ge(it[1], it[2])
                else:
                    it[1](eng).then_inc(it[2], it[3])

        with self.nc.Block() as block:
            @block.tensor
            def _(e):
                run(e, prog["pe"])

            @block.scalar
            def _(e):
                run(e, prog["act"])

            @block.vector
            def _(e):
                run(e, prog["dve"])

            @block.gpsimd
            def _(e):
                run(e, prog["pool"])

            @block.sync
            def _(e):
                run(e, prog["sp"])


def bcast_rows(dram_ap_1d_tensor, offset, n, parts=128):
    return bass.AP(dram_ap_1d_tensor, offset, [[0, parts], [1, n]])


class Prog:
    def __init__(self, stage="full"):
        self.stage = stage
        self.nc = bass.Bass("TRN2", target_bir_lowering=False)
        self.ins = {}
        self.outs = {}

    def inp(self, name, shape, dt=F32):
        t = self.nc.dram_tensor(name, list(shape), dt, kind="ExternalInput")
        self.ins[name] = t
        return t

    def out(self, name, shape, dt=F32):
        t = self.nc.dram_tensor(name, list(shape), dt, kind="ExternalOutput")
        self.outs[name] = t
        return t

    def scratch(self, name, shape, dt):
        return self.nc.dram_tensor(name, list(shape), dt)


def declare_inputs(P):
    i = P.inp
    i("x_all", [SL, D]); i("x_own", [TOK, D]); i("ctx", [SC, D])
    i("cvec", [128, 8, 2])
    i("ada_w", [2, D, 3 * D]); i("ada_bf", [128, 2, 24]); i("ada_bg", [2, D])
    i("ln_g", [2, D]); i("ln_b", [2, D])
    i("win0", [D, 768]); i("convw", [128, 2, 4]); i("convb", [128, 2])
    i("gatew", [2, 2, 4, 64, 64]); i("gateb", [128, 2, 2, 2]); i("lam", [128, 2, 2])
    i("fw", [128, 128]); i("fb", [128, 1])
    i("ident", [128, 128]); i("c128", [128, 128]); i("s128", [128, 128]); i("s128n", [128, 128])
    i("cs1", [128, 256]); i("cs2", [128, 256]); i("tt1", [128, 256]); i("tt2", [128, 256])
    i("c256", [128, 2, 256]); i("s256", [128, 2, 256])
    i("wout0", [1536, D])
    i("idx", [32, 128, 1], I32)
    i("win1", [D, 1440]); i("wkr", [D, 96]); i("wkrp", [D, 96])
    i("qng", [128, 2]); i("kvg", [128, 1])
    i("wuq", [256, 1536]); i("wuqp", [256, 1536]); i("wukv", [128, 2048]); i("wout1", [D, D])
    i("ropec", [96, TOK], BF16); i("ropes", [96, TOK], BF16)


def adaln(S, P, layer, const, mod, gbc, q2="pool"):
    ada_w = P.ins["ada_w"]
    abg = const["abg"]
    S.dma("sp", [(abg[:], bcast_rows(P.ins["ada_bg"], layer * D, D))], W=[abg])
    psm = const["ps"][0]
    wb = const["adawblk"]
    scf, scbc, abf = const["scf"], const["scbc"], const["ada_bf"]
    for cb in range(12):
        w = wb[cb % 2]
        src = ada_w.ap()[layer, :, cb * 256:(cb + 1) * 256].rearrange("(k p) c -> p k c", p=128)
        S.dma("sp" if cb % 2 == 0 else q2, [(w[:], src)], W=[w])
        if cb < 8:
            for fi in range(2):
                fc = cb * 2 + fi
                for k in range(8):
                    S.op("pe", lambda e, w=w, k=k, fi=fi, fc=fc: e.matmul(
                        psm[:, fc * 2:fc * 2 + 2], lhsT=w[:, k, fi * 128:(fi + 1) * 128], rhs=scf[:, k, :],
                        start=(k == 0), stop=(k == 7)), R=[w, scf], W=[psm])
        else:
            for j in range(2):
                pg = const["ps"][1 + j]
                for k in range(8):
                    S.op("pe", lambda e, w=w, k=k, j=j, pg=pg: e.matmul(
                        pg[:, 0:256], lhsT=scbc[:, k, j, :], rhs=w[:, k, :], start=(k == 0), stop=(k == 7)),
                        R=[w, scbc], W=[pg])
                c0 = (cb - 8) * 256
                S.op("dve", lambda e, j=j, pg=pg, c0=c0: e.tensor_tensor(
                    out=gbc[:, j, c0:c0 + 256], in0=pg[:, 0:256], in1=abg[:, c0:c0 + 256], op=ALU.add),
                    R=[pg, abg], W=[gbc])
        if cb == 7:
            S.op("dve", lambda e: e.tensor_tensor(
                out=mod[:, :, :], in0=psm[:, 0:32].rearrange("p (f j) -> p f j", j=2),
                in1=abf[:, layer, 0:16].unsqueeze(2).to_broadcast([128, 16, 2]), op=ALU.add),
                R=[psm, abf], W=[mod])
            S.op("dve", lambda e: e.tensor_scalar(
                out=mod[:, 8:16, :], in0=mod[:, 8:16, :], scalar1=1.0, scalar2=None, op0=ALU.add),
                R=[mod], W=[mod])


def adaln_setup(S, P, ps):
    I = P.ins
    cv = S.sbuf("cv", [128, 8, 2], F32)
    S.dma("sp", [(cv[:], I["cvec"].ap())], W=[cv])
    scf = S.sbuf("scf", [128, 8, 2], F32)
    S.op("act", lambda e: e.activation(out=scf[:], in_=cv[:], func=AF.Silu), R=[cv], W=[scf])
    scbc = S.sbuf("scbc", [128, 8, 2, 128], F32)
    S.op("dve", lambda e: e.tensor_copy(
        out=scbc[:].rearrange("p k j m -> p (k j) m"),
        in_=scf[:].rearrange("p k j -> p (k j)").unsqueeze(2).to_broadcast([128, 16, 128])), R=[scf], W=[scbc])
    abf = S.sbuf("abf", [128, 2, 24], F32)
    S.dma("sp", [(abf[:], I["ada_bf"].ap())], W=[abf])
    abg = S.sbuf("abg", [128, D], F32)
    return dict(ps=ps, scf=scf, scbc=scbc, ada_bf=abf, abg=abg,
                adawblk=[S.sbuf("adaw%d" % i, [128, 8, 256], F32) for i in range(2)])


def phase_A2(S, P, zsc, cc_in_lat, cc_in_ctx):
    I = P.ins
    TS = 1024
    with contextlib.ExitStack() as st:
        S.stack = st
        ps = [S.psum("psA2_%d" % i, [128, 512], F32) for i in range(8)]
        cw = S.sbuf("cw", [128, 2, 4], F32); S.dma("sp", [(cw[:], I["convw"].ap())], W=[cw])
        cb = S.sbuf("cb", [128, 2], F32); S.dma("sp", [(cb[:], I["convb"].ap())], W=[cb])
        gb = S.sbuf("gb", [128, 2, 2, 2], F32); S.dma("sp", [(gb[:], I["gateb"].ap())], W=[gb])
        lam = S.sbuf("lam", [128, 2, 2], F32); S.dma("sp", [(lam[:], I["lam"].ap())], W=[lam])
        identf = S.sbuf("identf", [128, 128], F32); S.dma("sp", [(identf[:], I["ident"].ap())], W=[identf])
        sp_ = S.sbuf("sp_", [128, 2, 2], F32)
        S.op("act", lambda e: e.activation(out=sp_[:], in_=lam[:], func=AF.Exp, scale=-1.0), R=[lam], W=[sp_])
        S.op("act", lambda e: e.activation(out=sp_[:], in_=sp_[:], func=AF.Ln, bias=1.0, scale=1.0), R=[sp_], W=[sp_])
        sc8 = S.sbuf("sc8", [128, 2, 2], F32)
        sc16 = S.sbuf("sc16", [128, 2, 2], F32)
        S.op("dve", lambda e: e.tensor_scalar(out=sc8[:], in0=sp_[:], scalar1=-8.0, scalar2=None, op0=ALU.mult), R=[sp_], W=[sc8])
        S.op("dve", lambda e: e.tensor_scalar(out=sc16[:], in0=sp_[:], scalar1=-16.0, scalar2=None, op0=ALU.mult), R=[sp_], W=[sc16])
        gwf = S.sbuf("gwf", [128, 8, 128], F32)
        S.op("pool", lambda e: e.memset(gwf[:], 0.0), W=[gwf])
        pairs = []
        for c in range(2):
            for d in range(2):
                for kd in range(2):
                    for hh in range(2):
                        pairs.append((gwf[hh * 64:(hh + 1) * 64, (c * 2 + d) * 2 + kd, hh * 64:(hh + 1) * 64],
                                      I["gatew"].ap()[d, kd, 2 * c + hh]))
        S.dma("sp", pairs, W=[gwf])
        gw = S.sbuf("gw", [128, 8, 128], BF16)
        S.op("dve", lambda e: e.tensor_copy(out=gw[:], in_=gwf[:]), R=[gwf], W=[gw])
        dg = S.sbuf("dg", [128, 8, 128], BF16)
        for c in range(2):
            for k in range(4):
                S.op("dve", lambda e, c=c, k=k: e.tensor_scalar(
                    out=dg[:, c * 4 + k, :], in0=identf[:], scalar1=cw[:, c, k:k + 1], scalar2=None, op0=ALU.mult),
                    R=[identf, cw], W=[dg])
        XW = TA + 8
        xv = S.sbuf("xv", [128, XW], BF16)
        S.op("pool", lambda e: e.memset(xv[:], 0.0), W=[xv])
        xl_t = S.sbuf("xl", [128, TA], BF16).t
        R_t = S.sbuf("Rr", [128, TA], F32).t
        tiles = [(0, SC)] + [(SC + i * TS, TS) for i in range(SL // TS)]
        xlB = [S.view(xl_t, "xl%d" % i) for i in range(len(tiles))]
        RB = [S.view(R_t, "R%d" % i) for i in range(len(tiles))]
        tmp = {}
        for nm in ("r", "i", "a", "a2", "h"):
            tmp[nm] = [S.sbuf("t_%s%d" % (nm, i), [128, TS], F32) for i in range(2)]
        gt = [S.sbuf("gt%d" % i, [128, TS], BF16) for i in range(2)]
        mo = [S.sbuf("mo%d" % i, [128, TS], BF16) for i in range(2)]
        carry = S.sbuf("carry", [128, 1], F32)
        psR = [S.view(ps[0].t, "psR0"), S.view(ps[2].t, "psR1")]
        for c in range(2):
            S.dma("sp", [(xv[:, 1:1 + SC], zsc.ap()[c, :, 0:SC]), (xv[:, 260:260 + SL], zsc.ap()[c, :, SC:TA])], W=[xv])
            for ti, (tok0, ntok) in enumerate(tiles):
                base = tok0 if ti == 0 else 259 + (tok0 - SC)
                for h0 in range(0, ntok, 512):
                    n = min(512, ntok - h0)
                    pb = ps[4 + (h0 // 512) % 2]
                    for k in range(4):
                        S.op("pe", lambda e, pb=pb, k=k, c=c, n=n, o=base + h0 + k: e.matmul(
                            pb[:, 0:n], lhsT=dg[:, c * 4 + k, :], rhs=xv[:, o:o + n], start=(k == 0), stop=(k == 3)),
                            R=[dg, xv], W=[pb])
                    S.op("act", lambda e, pb=pb, n=n, c=c, o=tok0 + h0: e.activation(
                        out=xl_t[:, o:o + n], in_=pb[:, 0:n], func=AF.Identity, bias=cb[:, c:c + 1], scale=1.0),
                        R=[pb, cb], W=[xlB[ti]])
            for d in range(2):
                order = list(range(len(tiles)))
                if d == 1:
                    order = [0] + order[:0:-1]
                for n_i, ti in enumerate(order):
                    tok0, ntok = tiles[ti]
                    pi = n_i % 2
                    pr, pim = ps[pi * 4:pi * 4 + 2], ps[pi * 4 + 2:pi * 4 + 4]
                    for h0 in range(0, ntok, 512):
                        n = min(512, ntok - h0)
                        for kd, pp in ((0, pr), (1, pim)):
                            pb = pp[h0 // 512]
                            S.op("pe", lambda e, pb=pb, n=n, kd=kd, c=c, d=d, o=tok0 + h0: e.matmul(
                                pb[:, 0:n], lhsT=gw[:, (c * 2 + d) * 2 + kd, :], rhs=xl_t[:, o:o + n], start=True, stop=True),
                                R=[gw, xlB[ti]], W=[pb])
                    r_, i_, a_, a2_, h_ = (tmp[k][pi] for k in ("r", "i", "a", "a2", "h"))
                    s_, bx_, b_ = a2_, i_, i_
                    for h0 in range(0, ntok, 512):
                        n = min(512, ntok - h0)
                        S.op("act", lambda e, n=n, h0=h0, pb=pr[h0 // 512], r_=r_, d=d, c=c: e.activation(
                            out=r_[:, h0:h0 + n], in_=pb[:, 0:n], func=AF.Sigmoid, bias=gb[:, d, 0, c:c + 1], scale=1.0),
                            R=[pr[h0 // 512], gb], W=[r_])
                        S.op("act", lambda e, n=n, h0=h0, pb=pim[h0 // 512], i_=i_, d=d, c=c: e.activation(
                            out=i_[:, h0:h0 + n], in_=pb[:, 0:n], func=AF.Sigmoid, bias=gb[:, d, 1, c:c + 1], scale=1.0),
                            R=[pim[h0 // 512], gb], W=[i_])
                    S.op("act", lambda e, a_=a_, r_=r_, ntok=ntok, d=d, c=c: e.activation(
                        out=a_[:, 0:ntok], in_=r_[:, 0:ntok], func=AF.Exp, scale=sc8[:, d, c:c + 1]), R=[r_, sc8], W=[a_])
                    S.op("act", lambda e, a2_=a2_, r_=r_, ntok=ntok, d=d, c=c: e.activation(
                        out=a2_[:, 0:ntok], in_=r_[:, 0:ntok], func=AF.Exp, scale=sc16[:, d, c:c + 1]), R=[r_, sc16], W=[a2_])
                    S.op("act", lambda e, s_=s_, a2_=a2_, ntok=ntok: e.activation(
                        out=s_[:, 0:ntok], in_=a2_[:, 0:ntok], func=AF.Sqrt, bias=1.0, scale=-1.0), R=[a2_], W=[s_])
                    S.op("pool", lambda e, bx_=bx_, i_=i_, ntok=ntok, tok0=tok0: e.tensor_tensor(
                        out=bx_[:, 0:ntok], in0=i_[:, 0:ntok], in1=xl_t[:, tok0:tok0 + ntok], op=ALU.mult),
                        R=[i_, xlB[ti]], W=[bx_])
                    S.op("dve", lambda e, b_=b_, bx_=bx_, s_=s_, ntok=ntok: e.tensor_tensor(
                        out=b_[:, 0:ntok], in0=bx_[:, 0:ntok], in1=s_[:, 0:ntok], op=ALU.mult), R=[bx_, s_], W=[b_])
                    init = 0.0 if n_i == 0 else carry[:, 0:1]
                    if d == 0:
                        S.op("dve", lambda e, a_=a_, b_=b_, ntok=ntok, tok0=tok0, init=init: e.tensor_tensor_scan(
                            out=R_t[:, tok0:tok0 + ntok], data0=a_[:, 0:ntok], data1=b_[:, 0:ntok], initial=init,
                            op0=ALU.mult, op1=ALU.add), R=[a_, b_, carry], W=[RB[ti]])
                        S.op("dve", lambda e, o=tok0 + ntok - 1: e.tensor_copy(out=carry[:], in_=R_t[:, o:o + 1]),
                             R=[RB[ti]], W=[carry])
                    else:
                        S.op("dve", lambda e, a_=a_, b_=b_, h_=h_, ntok=ntok, init=init: e.tensor_tensor_scan(
                            out=h_[:, 0:ntok][:, ::-1], data0=a_[:, 0:ntok][:, ::-1], data1=b_[:, 0:ntok][:, ::-1], initial=init,
                            op0=ALU.mult, op1=ALU.add), R=[a_, b_, carry], W=[h_])
                        S.op("dve", lambda e, h_=h_: e.tensor_copy(out=carry[:], in_=h_[:, 0:1]), R=[h_], W=[carry])
                        g_ = gt[pi]
                        S.dma("sp", [(g_[:, 0:ntok], zsc.ap()[3 + c, :, tok0:tok0 + ntok])], W=[g_])
                        S.op("pool", lambda e, h_=h_, ntok=ntok, tok0=tok0: e.tensor_tensor(
                            out=h_[:, 0:ntok], in0=h_[:, 0:ntok], in1=R_t[:, tok0:tok0 + ntok], op=ALU.add),
                            R=[h_, RB[ti]], W=[h_])
                        m_ = mo[pi]
                        S.op("pool", lambda e, h_=h_, m_=m_, g_=g_, ntok=ntok: e.tensor_tensor(
                            out=m_[:, 0:ntok], in0=h_[:, 0:ntok], in1=g_[:, 0:ntok], op=ALU.mult), R=[h_, g_], W=[m_])
                        if ti == 0:
                            dst = cc_in_ctx.ap()[c * 128:(c + 1) * 128, :]
                        else:
                            dst = cc_in_lat.ap()[ti - 1].rearrange("p (b c t) -> p b c t", b=2, c=3)[:, :, c, :]
                        src_ = m_[:, 0:ntok] if ti == 0 else m_[:, 0:ntok].rearrange("p (b t) -> p b t", t=512)
                        S.dma("sp", [(dst, src_)], R=[m_], owner=m_)
        S.barrier()
        S.flush()


def phase_A3(S, P, zsc, cc_in_lat, cc_in_ctx):
    I = P.ins
    NL = 1.0 / np.sqrt(float(SL) * 128.0)
    NC_ = 1.0 / np.sqrt(float(SC) * 128.0)
    with contextlib.ExitStack() as st:
        S.stack = st
        ps = [S.psum("psA3_%d" % i, [128, 512], F32) for i in range(8)]
        def ld(name, shape, src, dt=F32):
            t = S.sbuf(name, shape, dt)
            S.dma("sp", [(t[:], src)], W=[t])
            return t
        stg = S.sbuf("stg", [128, 512], F32)
        def ldbf(name, shape, src):
            n = int(np.prod(shape[1:]))
            fv = stg[:, 0:n]
            if len(shape) == 3:
                fv = fv.rearrange("p (a b) -> p a b", a=shape[1])
            S.dma("sp", [(fv, src)], W=[stg])
            b = S.sbuf(name, shape, BF16)
            S.op("dve", lambda e: e.tensor_copy(out=b[:], in_=fv), R=[stg], W=[b])
            return b
        c128f = ld("c128f", [128, 128], I["c128"].ap())
        s128nf = ld("s128nf", [128, 128], I["s128n"].ap())
        fwf = ld("fwf", [128, 128], I["fw"].ap())
        fb = ld("fbb", [128, 1], I["fb"].ap())
        c128 = ldbf("c128b", [128, 128], I["c128"].ap())
        s128 = ldbf("s128b", [128, 128], I["s128"].ap())
        cs1 = ldbf("cs1", [128, 256], I["cs1"].ap())
        cs2 = ldbf("cs2", [128, 256], I["cs2"].ap())
        tt1 = ld("tt1", [128, 256], I["tt1"].ap())
        tt2 = ld("tt2", [128, 256], I["tt2"].ap())
        c256 = ldbf("c256", [128, 2, 256], I["c256"].ap())
        s256 = ldbf("s256", [128, 2, 256], I["s256"].ap())
        mcat = S.sbuf("mcat", [128, 256], BF16)
        S.op("pe", lambda e: e.matmul(ps[0][:, 0:128], lhsT=c128f[:], rhs=fwf[:], start=True, stop=True), R=[c128f, fwf], W=[ps[0]])
        S.op("pe", lambda e: e.matmul(ps[0][:, 128:256], lhsT=s128nf[:], rhs=fwf[:], start=True, stop=True), R=[s128nf, fwf], W=[ps[0]])
        S.op("dve", lambda e: e.tensor_copy(out=mcat[:], in_=ps[0][:, 0:256]), R=[ps[0]], W=[mcat])
        zf = S.sbuf("zf", [128, TA], BF16)
        S.dma("sp", [(zf[:], zsc.ap()[2])], W=[zf])
        gf = S.sbuf("gf", [128, TA], BF16)
        S.dma("sp", [(gf[:], zsc.ap()[5])], W=[gf])
        pc = S.sbuf("pc", [128, 2, 256], BF16)
        for lc in range(2):
            S.op("pe", lambda e, lc=lc: e.matmul(ps[1][:, lc * 256:(lc + 1) * 256], lhsT=zf[:, lc * 128:(lc + 1) * 128], rhs=mcat[:],
                                                 start=True, stop=True), R=[zf, mcat], W=[ps[1]])
        S.op("dve", lambda e: e.tensor_copy(out=pc[:].rearrange("p a b -> p (a b)"), in_=ps[1][:, :]), R=[ps[1]], W=[pc])
        n = 0
        for lc in range(2):
            for (half, tab) in ((0, c256), (1, s256)):
                S.op("pe", lambda e, lc=lc, half=half, tab=tab, n=n: e.matmul(
                    ps[2][:, 0:256], lhsT=pc[:, lc, half * 128:(half + 1) * 128], rhs=tab[:, lc, :],
                    start=(n == 0), stop=(n == 3)), R=[pc, tab], W=[ps[2]])
                n += 1
        tc_ = S.sbuf("tc_", [128, 256], F32)
        S.op("act", lambda e: e.activation(out=tc_[:], in_=ps[2][:, 0:256], func=AF.Identity, bias=fb[:, 0:1], scale=NC_),
             R=[ps[2], fb], W=[tc_])
        gfc = S.view(gf.t, "gfc")
        S.op("dve", lambda e: e.tensor_tensor(out=gf[:, 0:SC], in0=tc_[:], in1=gf[:, 0:SC], op=ALU.mult), R=[tc_, gf], W=[gf])
        X = S.sbuf("X", [128, 128, 128], BF16)
        Bp = S.sbuf("Bp", [128, 2, 128, 128], BF16)
        zl = zf[:, SC:TA].rearrange("p (a b) -> p a b", b=128)
        mch = S.sbuf("mch", [128, 2, 2, 64], BF16)
        S.op("dve", lambda e: e.tensor_copy(
            out=mch[:], in_=ps[0][:, 0:256].rearrange("p (ri jh jj) -> p jh ri jj", ri=2, jh=2)), R=[ps[0]], W=[mch])
        t1 = [S.sbuf("t1_%d" % i, [128, 256], F32) for i in range(2)]
        t2 = [S.sbuf("t2_%d" % i, [128, 256], F32) for i in range(2)]
        psP = ps[0:2]
        psA = ps[2:6]
        for jh in range(2):
            for l2 in range(0, 128, 4):
                pb = psP[(l2 // 4) % 2]
                for q in range(4):
                    S.op("pe", lambda e, pb=pb, q=q, l2=l2, jh=jh: e.matmul(
                        pb[:, q * 128:(q + 1) * 128], lhsT=zl[:, :, l2 + q],
                        rhs=mch[:, jh, :, :].rearrange("p a b -> p (a b)"), start=True, stop=True),
                        R=[zf, mch], W=[pb])
                if (l2 // 4) % 2 == 0:
                    S.op("act", lambda e, pb=pb, l2=l2: e.activation(
                        out=X[:, l2:l2 + 4, :].rearrange("p a b -> p (a b)"), in_=pb[:, :], func=AF.Copy), R=[pb], W=[X])
                else:
                    S.op("dve", lambda e, pb=pb, l2=l2: e.tensor_copy(
                        out=X[:, l2:l2 + 4, :].rearrange("p a b -> p (a b)"), in_=pb[:, :]), R=[pb], W=[X])
            for jj in range(64):
                j = jh * 64 + jj
                pb = psA[j % 4]
                S.op("pe", lambda e, pb=pb, jj=jj: e.matmul(pb[:, 0:256], lhsT=X[:, :, jj], rhs=cs1[:], start=True, stop=False),
                     R=[X, cs1], W=[pb])
                S.op("pe", lambda e, pb=pb, jj=jj: e.matmul(pb[:, 0:256], lhsT=X[:, :, 64 + jj], rhs=cs2[:], start=False, stop=True),
                     R=[X, cs2], W=[pb])
                a1, a2 = t1[j % 2], t2[j % 2]
                S.op("dve", lambda e, pb=pb, a1=a1: e.tensor_tensor(out=a1[:], in0=pb[:, 0:256], in1=tt1[:], op=ALU.mult),
                     R=[pb, tt1], W=[a1])
                S.op("dve", lambda e, pb=pb, a2=a2: e.tensor_tensor(
                    out=a2[:].rearrange("p (h k) -> p h k", h=2),
                    in0=pb[:, 0:256].rearrange("p (h k) -> p h k", h=2)[:, ::-1, :],
                    in1=tt2[:].rearrange("p (h k) -> p h k", h=2), op=ALU.mult), R=[pb, tt2], W=[a2])
                S.op("pool", lambda e, a1=a1, a2=a2, j=j: e.tensor_tensor(
                    out=Bp[:, :, :, j], in0=a1[:].rearrange("p (h k) -> p h k", h=2),
                    in1=a2[:].rearrange("p (h k) -> p h k", h=2), op=ALU.add), R=[a1, a2], W=[Bp])
        tb = [S.sbuf("tb%d" % i, [128, 4, 128], F32) for i in range(2)]
        psB = ps[6:8]
        gl = gf[:, SC:TA].rearrange("p (k2 k1) -> p k1 k2", k1=128)
        for k1 in range(0, 128, 4):
            pb = psB[(k1 // 4) % 2]
            for q in range(4):
                S.op("pe", lambda e, pb=pb, q=q, k1=k1: e.matmul(
                    pb[:, q * 128:(q + 1) * 128], lhsT=Bp[:, 0, k1 + q, :], rhs=c128[:], start=True, stop=False),
                    R=[Bp, c128], W=[pb])
                S.op("pe", lambda e, pb=pb, q=q, k1=k1: e.matmul(
                    pb[:, q * 128:(q + 1) * 128], lhsT=Bp[:, 1, k1 + q, :], rhs=s128[:], start=False, stop=True),
                    R=[Bp, s128], W=[pb])
            tt = tb[(k1 // 4) % 2]
            S.op("act", lambda e, pb=pb, tt=tt: e.activation(
                out=tt[:].rearrange("p a b -> p (a b)"), in_=pb[:, :], func=AF.Identity, bias=fb[:, 0:1], scale=NL),
                R=[pb, fb], W=[tt])
            S.op("dve", lambda e, tt=tt, k1=k1: e.tensor_tensor(
                out=gl[:, k1:k1 + 4, :], in0=tt[:], in1=gl[:, k1:k1 + 4, :], op=ALU.mult), R=[tt, gf], W=[gf])
        prs = [(cc_in_ctx.ap()[256:384, :], gf[:, 0:SC])]
        for i in range(16):
            prs.append((cc_in_lat.ap()[i].rearrange("p (b c t) -> p b c t", b=2, c=3)[:, :, 2, :],
                        gf[:, SC + i * 1024:SC + (i + 1) * 1024].rearrange("p (b t) -> p b t", t=512)))
        S.dma("sp", prs, R=[gf], owner=gf)
        S.barrier()
        S.flush()


def ln_epilogue(S, name, yps, resid, gate_row, lng, lnb, out_t, tmp, small):
    t = tmp
    for hf in range(2):
        S.op("dve", lambda e, hf=hf: e.tensor_tensor(out=t[:, hf * 512:(hf + 1) * 512], in0=yps[hf][:, :],
                                                     in1=gate_row[:, hf * 512:(hf + 1) * 512], op=ALU.mult),
             R=[yps[hf], gate_row.b], W=[t.b])
    S.op("dve", lambda e: e.scalar_tensor_tensor(out=t[:, :], in0=resid[:, :], scalar=ALPHA, in1=t[:, :], op0=ALU.mult, op1=ALU.add),
         R=[resid.b, t.b], W=[t.b])
    st6, mv, rs = small
    for hf in range(2):
        S.op("dve", lambda e, hf=hf: e.bn_stats(out=st6[:, hf, :], in_=t[:, hf * 512:(hf + 1) * 512]), R=[t.b], W=[st6])
    S.op("dve", lambda e: e.bn_aggr(out=mv[:, :], in_=st6[:].rearrange("p a b -> p (a b)")), R=[st6], W=[mv])
    S.op("act", lambda e: e.activation(out=rs[:, :], in_=mv[:, 1:2], func=AF.Sqrt, bias=LN_EPS, scale=1.0), R=[mv], W=[rs])
    S.op("dve", lambda e: e.reciprocal(out=rs[:, :], in_=rs[:, :]), R=[rs], W=[rs])
    S.op("dve", lambda e: e.tensor_scalar(out=t[:, :], in0=t[:, :], scalar1=mv[:, 0:1], scalar2=rs[:, 0:1],
                                          op0=ALU.subtract, op1=ALU.mult), R=[t.b, mv, rs], W=[t.b])
    S.op("pool", lambda e: e.tensor_tensor(out=t[:, :], in0=t[:, :], in1=lng[:, :], op=ALU.mult), R=[t.b, lng], W=[t.b])
    S.op("pool", lambda e: e.tensor_tensor(out=out_t[:, :], in0=t[:, :], in1=lnb[:, :], op=ALU.add), R=[t.b, lnb], W=[out_t.b])


class V:
    def __init__(self, ap_fn, b):
        self.f = ap_fn
        self.b = b

    def __getitem__(self, idx):
        return self.f()[idx]


def phase_B(S, P, cc, h1sc, gsc1, qsc, cc2_in, gbc1):
    I = P.ins
    cc_in_lat, cc_in_ctx, cc_out_lat, cc_out_ctx = cc
    ccoB = S.view(cc_out_lat, "ccol"); ccocB = S.view(cc_out_ctx, "ccoc")
    RG = [[0, 1, 2, 3], [4, 5, 6, 7]]
    S.special("pool", lambda e: e.collective_compute("AllGather", ALU.bypass, replica_groups=RG,
                                                     ins=[cc_in_ctx.ap()], outs=[cc_out_ctx.ap()]), 1, W=[ccocB])
    for i in range(16):
        S.special("pool", lambda e, i=i: e.collective_compute("AllGather", ALU.bypass, replica_groups=RG,
                                                            ins=[cc_in_lat.ap()[i]], outs=[cc_out_lat.ap()[i]]), 1, W=[ccoB])
    with contextlib.ExitStack() as stB:
        S.stack = stB
        def ld(name, shape, src, dt=F32, q="sp"):
            t = S.sbuf(name, shape, dt)
            S.dma(q, [(t[:], src)], W=[t])
            return t
        mod0 = S.sbuf("mod0B", [128, 16, 2], F32)
        gbc0 = S.sbuf("gbc0B", [128, 2, D], F32)
        mod1 = S.sbuf("mod1B", [128, 16, 2], F32)
        gbc1t = S.sbuf("gbc1B", [128, 2, D], F32)
        wo = S.sbuf("wo0", [128, 12, D], BF16)
        w1 = S.sbuf("w1", [128, 8, 1440], BF16)
        wkr = S.sbuf("wkr", [128, 8, 96], BF16)
        wkrp = S.sbuf("wkrp", [128, 8, 96], BF16)
        wuq = S.sbuf("wuq", [128, 2, 1536], BF16)
        wuqp = S.sbuf("wuqp", [128, 2, 1536], BF16)
        ident = ld("identB", [128, 128], I["ident"].ap())
        with contextlib.ExitStack() as stp:
            S.stack = stp
            psp = [S.psum("psBp_%d" % i, [128, 512], F32) for i in range(3)]
            const = adaln_setup(S, P, psp)
            adaln(S, P, 0, const, mod0, gbc0, q2="act")
            adaln(S, P, 1, const, mod1, gbc1t, q2="act")
            S.dma("sp", [(gbc1.ap(), gbc1t[:, 0, :])], R=[gbc1t], owner=gbc1t)
            stg = const["adawblk"]
            def ldw(w, nk, ncol, src2d, eng="dve"):
                for c0 in range(0, ncol, 256):
                    n = min(256, ncol - c0)
                    sg = stg[(c0 // 256) % 2]
                    S.dma("sp" if (c0 // 256) % 2 == 0 else "act",
                          [(sg[:, 0:nk, 0:n], src2d[:, c0:c0 + n].rearrange("(k p) c -> p k c", p=128))], W=[sg])
                    if eng == "act":
                        S.op("act", lambda e, w=w, sg=sg, c0=c0, n=n: e.activation(out=w[:, :, c0:c0 + n], in_=sg[:, 0:nk, 0:n], func=AF.Copy),
                             R=[sg], W=[w])
                    else:
                        S.op(eng, lambda e, w=w, sg=sg, c0=c0, n=n: e.tensor_copy(out=w[:, :, c0:c0 + n], in_=sg[:, 0:nk, 0:n]), R=[sg], W=[w])
            ldw(wo, 12, D, I["wout0"].ap()) if False else None
            for c0 in range(0, D, 256):
                for kh in range(2):
                    sg = stg[(c0 // 256 + kh) % 2]
                    S.dma("sp" if kh == 0 else "act",
                          [(sg[:, 0:6, :], I["wout0"].ap()[kh * 768:(kh + 1) * 768, c0:c0 + 256].rearrange("(k p) c -> p k c", p=128))], W=[sg])
                    if kh == 0:
                        S.op("dve", lambda e, sg=sg, c0=c0, kh=kh: e.tensor_copy(
                            out=wo[:, kh * 6:(kh + 1) * 6, c0:c0 + 256], in_=sg[:, 0:6, :]), R=[sg], W=[wo])
                    else:
                        S.op("act", lambda e, sg=sg, c0=c0, kh=kh: e.activation(
                            out=wo[:, kh * 6:(kh + 1) * 6, c0:c0 + 256], in_=sg[:, 0:6, :], func=AF.Copy), R=[sg], W=[wo])
            ldw(w1, 8, 1440, I["win1"].ap(), eng="act")
            ldw(wkr, 8, 96, I["wkr"].ap())
            ldw(wkrp, 8, 96, I["wkrp"].ap())
            ldw(wuq, 2, 1536, I["wuq"].ap(), eng="act")
            ldw(wuqp, 2, 1536, I["wuqp"].ap())
            S.barrier()
            S.flush()
        S.stack = stB
        ps = [S.psum("psB_%d" % i, [128, 512], F32) for i in range(8)]
        lng0 = ld("lng0", [128, D], bcast_rows(I["ln_g"], 0, D))
        lnb0 = ld("lnb0", [128, D], bcast_rows(I["ln_b"], 0, D), q="pool")
        qng = ld("qng", [128, 2], I["qng"].ap())
        kvg = ld("kvg", [128, 1], I["kvg"].ap())
        ropec = ld("ropec", [96, TOK], I["ropec"].ap(), dt=BF16)
        ropes = ld("ropes", [96, TOK], I["ropes"].ap(), dt=BF16, q="pool")
        ones = S.sbuf("onesB", [128, 128], BF16)
        S.op("pool", lambda e: e.memset(ones[:], 1.0), W=[ones])
        idxt = []
        S.group_begin()
        for i in range(32):
            idxt.append(ld("idx%d" % i, [128, 1], I["idx"].ap()[i], dt=I32, q="pool"))
        S.group_end()
        mblk = [S.sbuf("mblk%d" % i, [128, 12, 512], BF16) for i in range(2)]
        xres = [S.sbuf("xres%d" % i, [128, D], F32) for i in range(2)]
        h1t = [S.sbuf("h1t%d" % i, [128, D], F32) for i in range(2)]
        st6 = [S.sbuf("st6_%d" % i, [128, 2, 6], F32) for i in range(2)]
        mv = [S.sbuf("mv%d" % i, [128, 2], F32) for i in range(2)]
        rs = [S.sbuf("rs%d" % i, [128, 1], F32) for i in range(2)]
        u1t = [S.sbuf("u1T%d" % i, [128, 8, 512], BF16) for i in range(2)]
        u1B = [[S.view(u1t[i].t, "u1_%d_%d" % (i, j)) for j in range(4)] for i in range(2)]
        qcs = S.sbuf("qcs", [128, 3, 512], F32)
        sqs = S.sbuf("sqs", [128, 3, 512], BF16)
        rsq = S.sbuf("rsq", [128, 2, 512], F32)
        qnT = S.sbuf("qnT", [128, 2, 512], BF16)
        kvo = [S.sbuf("kvo%d" % i, [128, 512], BF16) for i in range(1)]
        kro = [S.sbuf("kro%d" % i, [96, 512], BF16) for i in range(1)]
        krt = S.sbuf("krt", [96, 2, 512], F32)
        gso = [S.sbuf("gso%d" % i, [128, 8, 512], BF16) for i in range(1)]
        qo = [S.sbuf("qo%d" % i, [96, 512], BF16) for i in range(2)]
        qrts = [S.sbuf("qrt%d" % i, [96, 2, 512], F32) for i in range(2)]
        rows_lat = cc_out_lat.ap().rearrange("i q (b x) -> (i q b) x", x=1536)
        psY = [ps[0:2], ps[0:2]]
        psT = ps[2:4]
        psW = ps[4:8]

        def proj_block(bi, ntok, tok0_own, is_ctx):
            u = u1t[bi % 2]
            uB = u1B[bi % 2]
            cnt = [0]
            def mm(wt, c0, ncol, pb):
                for k in range(8):
                    S.op("pe", lambda e, k=k: e.matmul(pb[0:ncol, 0:ntok], lhsT=wt[:, k, c0:c0 + ncol], rhs=u[:, k, 0:ntok],
                                                       start=(k == 0), stop=(k == 7)), R=[wt] + uB, W=[pb])
            def nextps():
                cnt[0] += 1
                return psW[cnt[0] % 4]
            pb = nextps(); mm(w1, 256, 128, pb)
            S.op("act", lambda e, pb=pb: e.activation(out=qcs[:, 2, 0:ntok], in_=pb[:, 0:ntok], func=AF.Copy), R=[pb], W=[qcs])
            S.op("act", lambda e, pb=pb: e.activation(out=sqs[:, 2, 0:ntok], in_=pb[:, 0:ntok], func=AF.Square), R=[pb], W=[sqs])
            pb = nextps()
            S.op("pe", lambda e, pb=pb: e.matmul(pb[:, 0:ntok], lhsT=ones[:], rhs=sqs[:, 2, 0:ntok], start=True, stop=True), R=[ones, sqs], W=[pb])
            S.op("act", lambda e, pb=pb: e.activation(out=rsq[:, 1, 0:ntok], in_=pb[:, 0:ntok], func=AF.Sqrt, bias=RMS_EPS, scale=1.0 / 128.0), R=[pb], W=[rsq])
            S.op("dve", lambda e: e.reciprocal(out=rsq[:, 1, 0:ntok], in_=rsq[:, 1, 0:ntok]), R=[rsq], W=[rsq])
            ko = kvo[0]
            S.op("dve", lambda e: e.scalar_tensor_tensor(out=ko[:, 0:ntok], in0=qcs[:, 2, 0:ntok], scalar=kvg[:, 0:1], in1=rsq[:, 1, 0:ntok],
                                                         op0=ALU.mult, op1=ALU.mult), R=[qcs, kvg, rsq], W=[ko])
            yield
            pb = nextps(); mm(wkr, 0, 96, pb)
            kr_ = kro[0]
            if is_ctx:
                S.op("dve", lambda e, pb=pb: e.tensor_copy(out=kr_[64:96, 0:ntok], in_=pb[64:96, 0:ntok]), R=[pb], W=[kr_])
            else:
                pb2 = nextps(); mm(wkrp, 0, 96, pb2)
                S.op("dve", lambda e, pb=pb: e.tensor_tensor(out=krt[64:96, 0, 0:ntok], in0=pb[64:96, 0:ntok],
                                                             in1=ropec[64:96, tok0_own:tok0_own + ntok], op=ALU.mult), R=[pb, ropec], W=[krt])
                S.op("dve", lambda e, pb2=pb2: e.tensor_tensor(out=krt[64:96, 1, 0:ntok], in0=pb2[64:96, 0:ntok],
                                                               in1=ropes[64:96, tok0_own:tok0_own + ntok], op=ALU.mult), R=[pb2, ropes, krt], W=[krt])
                S.op("pool", lambda e: e.tensor_tensor(out=kr_[64:96, 0:ntok], in0=krt[64:96, 0, 0:ntok], in1=krt[64:96, 1, 0:ntok], op=ALU.add),
                     R=[krt], W=[kr_])
            if is_ctx:
                dk = cc2_in[0].ap()[:, 0:ntok]
            else:
                dk = cc2_in[1].ap()[tok0_own // 2048][:, tok0_own % 2048:tok0_own % 2048 + ntok]
            S.dma("sp", [(dk[0:128, :], ko[:, 0:ntok])], R=[ko], owner=ko)
            S.dma("sp", [(dk[128:160, :], kr_[64:96, 0:ntok])], R=[kr_], owner=kr_)
            yield
            if is_ctx:
                return
            go = gso[0]
            for c in range(8):
                pb = nextps(); mm(w1, 416 + c * 128, 128, pb)
                S.op("act", lambda e, pb=pb, c=c: e.activation(out=go[:, c, 0:ntok], in_=pb[:, 0:ntok], func=AF.Silu), R=[pb], W=[go])
                yield
            S.dma("pool", [(gsc1.ap()[:, :, tok0_own:tok0_own + ntok].rearrange("c p t -> p c t"), go[:, :, 0:ntok])], R=[go], owner=go)
            for c in range(2):
                pb = nextps(); mm(w1, c * 128, 128, pb)
                S.op("act", lambda e, pb=pb, c=c: e.activation(out=qcs[:, c, 0:ntok], in_=pb[:, 0:ntok], func=AF.Copy), R=[pb], W=[qcs])
                S.op("act", lambda e, pb=pb, c=c: e.activation(out=sqs[:, c, 0:ntok], in_=pb[:, 0:ntok], func=AF.Square), R=[pb], W=[sqs])
            pb = nextps()
            for c in range(2):
                S.op("pe", lambda e, pb=pb, c=c: e.matmul(pb[:, 0:ntok], lhsT=ones[:], rhs=sqs[:, c, 0:ntok], start=(c == 0), stop=(c == 1)),
                     R=[ones, sqs], W=[pb])
            S.op("act", lambda e, pb=pb: e.activation(out=rsq[:, 0, 0:ntok], in_=pb[:, 0:ntok], func=AF.Sqrt, bias=RMS_EPS, scale=1.0 / 256.0), R=[pb], W=[rsq])
            S.op("dve", lambda e: e.reciprocal(out=rsq[:, 0, 0:ntok], in_=rsq[:, 0, 0:ntok]), R=[rsq], W=[rsq])
            for c in range(2):
                S.op("dve", lambda e, c=c: e.scalar_tensor_tensor(out=qnT[:, c, 0:ntok], in0=qcs[:, c, 0:ntok], scalar=qng[:, c:c + 1],
                                                                  in1=rsq[:, 0, 0:ntok], op0=ALU.mult, op1=ALU.mult), R=[qcs, qng, rsq], W=[qnT])
            yield
            for h in range(NH):
                pa = nextps()
                for c in range(2):
                    S.op("pe", lambda e, pa=pa, c=c, h=h: e.matmul(pa[0:96, 0:ntok], lhsT=wuq[:, c, h * 96:(h + 1) * 96], rhs=qnT[:, c, 0:ntok],
                                                                 start=(c == 0), stop=(c == 1)), R=[wuq, qnT], W=[pa])
                pp = nextps()
                for c in range(2):
                    S.op("pe", lambda e, pp=pp, c=c, h=h: e.matmul(pp[0:96, 0:ntok], lhsT=wuqp[:, c, h * 96:(h + 1) * 96], rhs=qnT[:, c, 0:ntok],
                                                                 start=(c == 0), stop=(c == 1)), R=[wuqp, qnT], W=[pp])
                q_ = qo[h % 2]
                qrt = qrts[h % 2]
                S.op("act", lambda e, pa=pa, q_=q_: e.activation(out=q_[0:64, 0:ntok], in_=pa[0:64, 0:ntok], func=AF.Copy), R=[pa], W=[q_])
                S.op("dve", lambda e, pa=pa, qrt=qrt: e.tensor_tensor(out=qrt[64:96, 0, 0:ntok], in0=pa[64:96, 0:ntok],
                                                             in1=ropec[64:96, tok0_own:tok0_own + ntok], op=ALU.mult), R=[pa, ropec], W=[qrt])
                S.op("dve", lambda e, pp=pp, qrt=qrt: e.tensor_tensor(out=qrt[64:96, 1, 0:ntok], in0=pp[64:96, 0:ntok],
                                                             in1=ropes[64:96, tok0_own:tok0_own + ntok], op=ALU.mult), R=[pp, ropes, qrt], W=[qrt])
                S.op("pool", lambda e, q_=q_, qrt=qrt: e.tensor_tensor(out=q_[64:96, 0:ntok], in0=qrt[64:96, 0, 0:ntok], in1=qrt[64:96, 1, 0:ntok], op=ALU.add),
                     R=[qrt, q_], W=[q_])
                S.dma("sp" if h % 2 == 0 else "pool", [(qsc.ap()[h, :, tok0_own:tok0_own + ntok], q_[0:96, 0:ntok])], R=[q_], owner=q_)
                yield

        mctx = S.sbuf("mctx", [128, 12, SC], BF16)
        S.dma("sp", [(mctx[:], cc_out_ctx.ap().rearrange("(k p) t -> p k t", p=128))], R=[ccocB], W=[mctx])
        ntile = 2 + TOK // 128

        def b_info(ti):
            is_ctx = ti < 2
            if is_ctx:
                return is_ctx, 1, mctx, ti * 128, I["ctx"].ap()[ti * 128:(ti + 1) * 128, :]
            li = ti - 2
            return is_ctx, 0, mblk[(li // 4) % 2], (li % 4) * 128, I["x_own"].ap()[li * 128:(li + 1) * 128, :]

        def b_gather(blk):
            mb = mblk[blk % 2]
            for r in range(4):
                S.special("pool", lambda e, r=r: e.indirect_dma_start(
                    out=mb[:, 3 * r:3 * r + 3, :].rearrange("p a b -> p (a b)"), out_offset=None, in_=rows_lat,
                    in_offset=bass.IndirectOffsetOnAxis(ap=idxt[blk * 4 + r][:, :], axis=0),
                    bounds_check=16 * 512 * 2 - 1, oob_is_err=False), 16, R=[ccoB, idxt[blk * 4 + r]], W=[mb])

        def b_y(ti):
            is_ctx, j, msrc, mcol, xsrc = b_info(ti)
            if ti == 0:
                b_gather(0)
                b_gather(1)
            if not is_ctx and (ti - 2) % 4 == 0:
                blk = (ti - 2) // 4
                if 1 <= blk and blk + 1 < TOK // 512:
                    b_gather(blk + 1)
            xr = xres[ti % 2]
            S.dma("sp", [(xr[:], xsrc)], W=[xr])
            yps = psY[ti % 2]
            for hf in range(2):
                for kc in range(12):
                    S.op("pe", lambda e, hf=hf, kc=kc: e.matmul(
                        yps[hf][:, :], lhsT=msrc[:, kc, mcol:mcol + 128], rhs=wo[:, kc, hf * 512:(hf + 1) * 512],
                        start=(kc == 0), stop=(kc == 11)), R=[msrc, wo], W=[yps[hf]])

        def b_ep(ti):
            is_ctx, j, msrc, mcol, xsrc = b_info(ti)
            xr, yps = xres[ti % 2], psY[ti % 2]
            hh = h1t[ti % 2]
            hv = V(lambda: hh[:, :], hh)
            ln_epilogue(S, "l0", yps, V(lambda: xr[:, :], xr), V(lambda: gbc0[:, j, :], gbc0), lng0, lnb0,
                        hv, hv, (st6[ti % 2], mv[ti % 2], rs[ti % 2]))
            if not is_ctx:
                S.dma("pool", [(h1sc.ap()[(ti - 2) * 128:(ti - 1) * 128, :], hh[:])], R=[hh], owner=hh)

        def b_tr(ti):
            is_ctx, j, msrc, mcol, xsrc = b_info(ti)
            hh = h1t[ti % 2]
            bi = 0 if is_ctx else 1 + (ti - 2) // 4
            sub = ti if is_ctx else (ti - 2) % 4
            u = u1t[bi % 2]
            for k in range(8):
                pb = psT[k % 2]
                S.op("pe", lambda e, pb=pb, k=k: e.transpose(out=pb[:, 0:128], in_=hh[:, k * 128:(k + 1) * 128], identity=ident[:]),
                     R=[hh, ident], W=[pb])
                S.op("act", lambda e, pb=pb, k=k: e.activation(
                    out=u[:, k, sub * 128:(sub + 1) * 128], in_=pb[:, 0:128], func=AF.Identity,
                    scale=mod1[:, 8 + k, j:j + 1], bias=mod1[:, k, j:j + 1]), R=[pb, mod1], W=[u1B[bi % 2][sub]])
            if is_ctx and ti == 1:
                pgen.append(proj_block(0, 256, 0, True))
            elif (not is_ctx) and sub == 3:
                pgen.append(proj_block(bi, 512, ((ti - 2) // 4) * 512, False))

        pgen = []

        def pump(k):
            while k > 0 and pgen:
                try:
                    next(pgen[0])
                    k -= 1
                except StopIteration:
                    pgen.pop(0)

        b_y(0)
        for ti in range(ntile):
            b_ep(ti)
            pump(3)
            if ti + 1 < ntile:
                b_y(ti + 1)
            pump(3)
            b_tr(ti)
            pump(3)
        pump(1000)
        S.barrier()
        S.flush()


def phase_C(S, P, cc2_in, cc2_out, qsc, gsc1, osc):
    I = P.ins
    NKT = TA // 128
    RG = [[0, 1, 2, 3], [4, 5, 6, 7]]
    c2o = S.view(cc2_out[0], "cc2o")
    S.special("pool", lambda e: e.collective_compute("AllGather", ALU.bypass, replica_groups=RG,
                                                     ins=[cc2_in[0].ap()], outs=[cc2_out[0].ap()]), 1, W=[c2o])
    for i in range(2):
        S.special("pool", lambda e, i=i: e.collective_compute("AllGather", ALU.bypass, replica_groups=RG,
                                                            ins=[cc2_in[1].ap()[i]], outs=[cc2_out[1].ap()[i]]), 1, W=[c2o])
    with contextlib.ExitStack() as st:
        S.stack = st
        psS = [S.psum("psS%d" % i, [128, 1024], F32) for i in range(3)]
        psO = [S.psum("psO%d" % i, [128, 512], F32) for i in range(2)]
        kvn = S.sbuf("kvn", [128, TA], BF16)
        KTt = [S.sbuf("KT%d" % i, [128, TA], BF16).t for i in range(2)]
        KTn = [S.view(KTt[i], "KTn%d" % i) for i in range(2)]
        KTr = [S.view(KTt[i], "KTr%d" % i) for i in range(2)]
        Va = [S.sbuf("Va%d" % i, [128, NKT, 128], BF16) for i in range(2)]
        qT = [S.sbuf("qT%d" % i, [128, TOK], BF16) for i in range(2)]
        pk = [(kvn[:, 0:SC], cc2_out[0].ap()[0:128, :])]
        pr = [[(KTt[i][64:96, 0:SC], cc2_out[0].ap()[128:160, :])] for i in range(2)]
        for r in range(4):
            for hf in range(2):
                c0 = SC + r * TOK + hf * 2048
                pk.append((kvn[:, c0:c0 + 2048], cc2_out[1].ap()[hf, 160 * r:160 * r + 128, :]))
                for i in range(2):
                    pr[i].append((KTt[i][64:96, c0:c0 + 2048], cc2_out[1].ap()[hf, 160 * r + 128:160 * r + 160, :]))
        S.dma("sp", pk, R=[c2o], W=[kvn])
        for i in range(2):
            S.dma("pool", pr[i], R=[c2o], W=[KTr[i]])
            S.op("pool", lambda e, i=i: e.memset(KTt[i][96:128, :], 0.0), W=[KTr[i]])
            S.op("pool", lambda e, i=i: e.memset(KTt[i][96:97, :], 1.0), W=[KTr[i]])
            S.op("pool", lambda e, i=i: e.memset(qT[i][96:128, :], 0.0), W=[qT[i]])
        S.op("pool", lambda e: e.memset(Va[0][:, :, 64:128], 1.0), W=[Va[0]])
        S.op("pool", lambda e: e.memset(Va[1][:, :, 0:64], 1.0), W=[Va[1]])
        wst = S.sbuf("wukvf", [128, 512], F32)
        wukv = S.sbuf("wukv", [128, 2048], BF16)
        for c0 in range(0, 2048, 512):
            S.dma("sp", [(wst[:], I["wukv"].ap()[:, c0:c0 + 512])], W=[wst])
            S.op("dve", lambda e, c0=c0: e.tensor_copy(out=wukv[:, c0:c0 + 512], in_=wst[:]), R=[wst], W=[wukv])
        ones = S.sbuf("onesC", [96, 128], BF16)
        S.op("pool", lambda e: e.memset(ones[:], 1.0), W=[ones])
        PT = [S.sbuf("PT%d" % i, [128, 1024], BF16) for i in range(3)]
        sq = [S.sbuf("sqC%d" % i, [96, 512], BF16) for i in range(2)]
        mx = S.sbuf("mxC", [128, 1], F32)
        qm = [S.sbuf("qmC%d" % i, [128, 1], F32) for i in range(2)]
        km = [S.sbuf("kmC%d" % i, [128, 1], F32) for i in range(2)]
        negc = [S.sbuf("negc%d" % i, [128, 1], F32) for i in range(2)]
        ot = [S.sbuf("otC%d" % i, [128, 512], F32) for i in range(2)]
        dn = [S.sbuf("dnC%d" % i, [128, 512], F32) for i in range(2)]
        gt = [S.sbuf("gtC%d" % i, [128, 512], BF16) for i in range(2)]
        oo = [S.sbuf("ooC%d" % i, [128, 512], BF16) for i in range(2)]
        cnt = [0]

        def k_units(h, evac="act"):
            par = h % 2
            S.dma("sp", [(qT[par][0:96, :], qsc.ap()[h])], W=[qT[par]])
            S.op("pool", lambda e: e.memset(qm[par][:], 0.0), W=[qm[par]])
            S.op("pool", lambda e: e.memset(km[par][:], 0.0), W=[km[par]])
            yield
            for kb in range(0, TA, 512):
                n = min(512, TA - kb)
                pb = bank[0]
                S.op("pe", lambda e, pb=pb, kb=kb, n=n: e.matmul(pb[0:64, 0:n], lhsT=wukv[:, h * 128:h * 128 + 64], rhs=kvn[:, kb:kb + n],
                                                                 start=True, stop=True), R=[wukv, kvn], W=[pb])
                if evac == "act":
                    S.op("act", lambda e, pb=pb, kb=kb, n=n: e.activation(out=KTt[par][0:64, kb:kb + n], in_=pb[0:64, 0:n], func=AF.Copy),
                         R=[pb], W=[KTn[par]])
                else:
                    S.op("dve", lambda e, pb=pb, kb=kb, n=n: e.tensor_copy(out=KTt[par][0:64, kb:kb + n], in_=pb[0:64, 0:n]),
                         R=[pb], W=[KTn[par]])
                yield

        def v_units(h):
            par = h % 2
            v0 = 0 if par == 0 else 64
            for k0 in range(0, NKT, 8):
                nk = min(8, NKT - k0)
                pb = bank[0]
                for i in range(nk):
                    S.op("pe", lambda e, pb=pb, i=i, k0=k0: e.matmul(pb[:, i * 64:(i + 1) * 64], lhsT=kvn[:, (k0 + i) * 128:(k0 + i + 1) * 128],
                                                                    rhs=wukv[:, h * 128 + 64:h * 128 + 128], start=True, stop=True),
                         R=[wukv, kvn], W=[pb])
                S.op("dve", lambda e, pb=pb, k0=k0, nk=nk: e.tensor_copy(
                    out=Va[par][:, k0:k0 + nk, v0:v0 + 64], in_=pb[:, 0:nk * 64].rearrange("p (a b) -> p a b", b=64)), R=[pb], W=[Va[par]])
                yield

        def build_units(h):
            for _ in k_units(h):
                yield
            for _ in v_units(h):
                yield
            for _ in norm_units(h):
                yield

        def norm_units(h):
            par = h % 2
            work = []
            for (src, srcB, tot, acc) in ((qT[par], [qT[par]], TOK, qm[par]), (KTt[par], [KTn[par], KTr[par]], TA, km[par])):
                for b0 in range(0, tot, 512):
                    work.append((src, srcB, min(512, tot - b0), b0, acc))

            def stage_a(u):
                src, srcB, n, b0, acc = work[u]
                s_ = sq[u % 2]
                S.op("pool", lambda e: e.tensor_tensor(out=s_[:, 0:n], in0=src[0:96, b0:b0 + n], in1=src[0:96, b0:b0 + n], op=ALU.mult),
                     R=srcB, W=[s_])

            def stage_b(u):
                src, srcB, n, b0, acc = work[u]
                s_ = sq[u % 2]
                pb = bank[0]
                S.op("pe", lambda e: e.matmul(pb[:, 0:n], lhsT=ones[:], rhs=s_[:, 0:n], start=True, stop=True), R=[ones, s_], W=[pb])
                S.op("dve", lambda e: e.tensor_reduce(out=mx[:], in_=pb[:, 0:n], op=ALU.max, axis=mybir.AxisListType.X), R=[pb], W=[mx])
                S.op("dve", lambda e: e.tensor_tensor(out=acc[:], in0=acc[:], in1=mx[:], op=ALU.max), R=[acc, mx], W=[acc])

            stage_a(0)
            yield
            for u in range(len(work)):
                if u + 1 < len(work):
                    stage_a(u + 1)
                stage_b(u)
                yield
            nb = negc[par]
            S.op("dve", lambda e: e.tensor_tensor(out=nb[:], in0=qm[par][:], in1=km[par][:], op=ALU.mult), R=[qm[par], km[par]], W=[nb])
            S.op("act", lambda e: e.activation(out=nb[:], in_=nb[:], func=AF.Sqrt), R=[nb], W=[nb])
            S.op("dve", lambda e: e.tensor_scalar(out=qT[par][96:97, :], in0=KTt[par][96:97, 0:TOK], scalar1=nb[96:97, 0:1], scalar2=-1.0,
                                                  op0=ALU.mult, op1=ALU.mult), R=[nb, KTr[par], qT[par]], W=[qT[par]])

        bank = [psO[0]]
        for i_, _ in enumerate(build_units(0)):
            bank[0] = psO[i_ % 2]
        gen = [None]
        GK = 2
        groups = [list(range(k0, min(k0 + GK, NKT))) for k0 in range(0, NKT, GK)]
        NP_ = len(groups)
        NQB = TOK // 512
        pairs = [(h, qb, kp) for h in range(NH) for qb in range(NQB) for kp in range(NP_)]
        pobuf = {}

        def emit_qk(n):
            h, qb, kp = pairs[n]
            par = h % 2
            q0 = qb * 512
            if kp == 0:
                pobuf[(h, qb)] = psO[(h * NQB + qb) % 2]
                if qb == 0 and gen[0] is not None:
                    raise RuntimeError("norm units of this head were not finished in time")
                if qb == 0 and h + 1 < NH:
                    gen[0] = k_units(h + 1, evac="dve")
            if gen[0] is not None and 4 <= kp <= NP_ - 5 and kp % 4 == 0:
                bank[0] = psO[(h * NQB + qb + 1) % 2]
                try:
                    next(gen[0])
                except StopIteration:
                    gen[0] = None
            sp_ = psS[n % 3]
            kts = groups[kp]
            for i, kt in enumerate(kts):
                S.op("pe", lambda e, i=i, kt=kt: e.matmul(
                    sp_[:, i * 512:(i + 1) * 512], lhsT=KTt[par][:, kt * 128:(kt + 1) * 128], rhs=qT[par][:, q0:q0 + 512],
                    start=True, stop=True), R=[KTn[par], KTr[par], qT[par]], W=[sp_])
            pt = PT[n % 3]
            w = 512 * len(kts)
            S.op("act", lambda e: e.activation(out=pt[:, 0:w], in_=sp_[:, 0:w], func=AF.Exp, scale=ATTN_SCALE), R=[sp_], W=[pt])

        def emit_pv(n):
            h, qb, kp = pairs[n]
            par = h % 2
            q0 = qb * 512
            po = pobuf[(h, qb)]
            pt = PT[n % 3]
            for i, kt in enumerate(groups[kp]):
                S.op("pe", lambda e, i=i, kt=kt: e.matmul(
                    po[:, :], lhsT=Va[par][:, kt, :], rhs=pt[:, i * 512:(i + 1) * 512],
                    start=(kt == 0), stop=(kt == NKT - 1)), R=[Va[par], pt], W=[po])
            if kp != NP_ - 1:
                return
            num = slice(0, 64) if par == 0 else slice(64, 128)
            den = slice(64, 128) if par == 0 else slice(0, 64)
            o_, d_, g_, oo_ = ot[qb % 2], dn[qb % 2], gt[qb % 2], oo[qb % 2]
            S.op("dve", lambda e: e.tensor_copy(out=o_[:, :], in_=po[:, :]), R=[po], W=[o_])
            S.dma("sp", [(d_[num, :], o_[den, :])], R=[o_], W=[d_])
            S.dma("pool", [(g_[num, :], gsc1.ap()[h // 2, num, q0:q0 + 512])], W=[g_])
            S.op("dve", lambda e: e.reciprocal(out=d_[num, :], in_=d_[num, :]), R=[d_], W=[d_])
            S.op("dve", lambda e: e.tensor_tensor(out=o_[num, :], in0=o_[num, :], in1=d_[num, :], op=ALU.mult), R=[o_, d_], W=[o_])
            S.op("pool", lambda e: e.tensor_tensor(out=oo_[num, :], in0=o_[num, :], in1=g_[num, :], op=ALU.mult), R=[o_, g_], W=[oo_])
            S.dma("sp", [(osc.ap()[h // 2, num, q0:q0 + 512], oo_[num, :])], R=[oo_], owner=oo_)
            if qb == 2 and h + 1 < NH:
                par_ = psS[n % 3]
                halves = [Buf(par_.t[:, 0:512], "psSh0"), Buf(par_.t[:, 512:1024], "psSh1")]
                for hb in halves:
                    hb.lw = dict(par_.lw)
                    hb.rd = dict(par_.rd)
                banks4 = [psO[0], psO[1]] + halves
                bank[0] = psO[(h * NQB + qb + 1) % 2]
                if gen[0] is not None:
                    for _ in gen[0]:
                        pass
                bank[0] = banks4[0]
                for i_, _ in enumerate(v_units(h + 1)):
                    bank[0] = banks4[(i_ + 1) % 4]
                for hb in halves:
                    for dd in (hb.lw, hb.rd):
                        for k_, v_ in dd.items():
                            if par_.rd.get(k_, 0) < v_:
                                par_.rd[k_] = v_
                gen[0] = norm_units(h + 1)

        LOOK = 2
        for n in range(len(pairs) + LOOK):
            if n < len(pairs):
                emit_qk(n)
            if n - LOOK >= 0:
                emit_pv(n - LOOK)
        S.barrier()
        S.flush()


def phase_D(S, P, osc, h1sc, gbc1, out):
    I = P.ins
    with contextlib.ExitStack() as st:
        S.stack = st
        ps = [S.psum("psD_%d" % i, [128, 512], F32) for i in range(4)]
        stg = S.sbuf("stgD", [128, 8, 256], F32)
        wo = S.sbuf("wo1", [128, 8, D], BF16)
        for c0 in range(0, D, 256):
            S.dma("sp", [(stg[:], I["wout1"].ap()[:, c0:c0 + 256].rearrange("(k p) c -> p k c", p=128))], W=[stg])
            S.op("dve", lambda e, c0=c0: e.tensor_copy(out=wo[:, :, c0:c0 + 256], in_=stg[:]), R=[stg], W=[wo])
        g1 = S.sbuf("gbc1D", [128, D], F32); S.dma("sp", [(g1[:], gbc1.ap())], W=[g1])
        lng = S.sbuf("lng1", [128, D], F32); S.dma("sp", [(lng[:], bcast_rows(I["ln_g"], D, D))], W=[lng])
        lnb = S.sbuf("lnb1", [128, D], F32); S.dma("pool", [(lnb[:], bcast_rows(I["ln_b"], D, D))], W=[lnb])
        oT = S.sbuf("oT", [128, 8, TOK], BF16)
        S.dma("sp", [(oT[:], osc.ap().rearrange("c p t -> p c t"))], W=[oT])
        hres = [S.sbuf("hres%d" % i, [128, D], F32) for i in range(2)]
        tmpt = [S.sbuf("tmpD%d" % i, [128, D], F32) for i in range(2)]
        outt = [S.sbuf("outD%d" % i, [128, D], F32) for i in range(2)]
        st6 = [S.sbuf("st6D%d" % i, [128, 2, 6], F32) for i in range(2)]
        mv = [S.sbuf("mvD%d" % i, [128, 2], F32) for i in range(2)]
        rs = [S.sbuf("rsD%d" % i, [128, 1], F32) for i in range(2)]
        psY = [ps[0:2], ps[2:4]]
        for ti in range(TOK // 128):
            hr = hres[ti % 2]
            S.dma("pool", [(hr[:], h1sc.ap()[ti * 128:(ti + 1) * 128, :])], W=[hr])
            yps = psY[ti % 2]
            for hf in range(2):
                for kc in range(8):
                    S.op("pe", lambda e, hf=hf, kc=kc, ti=ti, yps=yps: e.matmul(
                        yps[hf][:, :], lhsT=oT[:, kc, ti * 128:(ti + 1) * 128], rhs=wo[:, kc, hf * 512:(hf + 1) * 512],
                        start=(kc == 0), stop=(kc == 7)), R=[oT, wo], W=[yps[hf]])
            tt, ou = tmpt[ti % 2], outt[ti % 2]
            ln_epilogue(S, "l1", yps, V(lambda hr=hr: hr[:, :], hr), V(lambda: g1[:, :], g1), lng, lnb,
                        V(lambda ou=ou: ou[:, :], ou), V(lambda tt=tt: tt[:, :], tt), (st6[ti % 2], mv[ti % 2], rs[ti % 2]))
            S.dma("sp", [(out.ap()[ti * 128:(ti + 1) * 128, :], ou[:])], R=[ou], owner=ou)
        S.barrier()
        S.flush()


def build(stage="full"):
    P = Prog(stage)
    nc = P.nc
    declare_inputs(P)
    I = P.ins
    zsc = P.scratch("zsc", [6, 128, TA], BF16)
    cc_in_lat = P.scratch("cc_in_lat", [16, 128, 3072], BF16)
    cc_in_ctx = P.scratch("cc_in_ctx", [384, SC], BF16)
    cc_out_lat = P.scratch("cc_out_lat", [16, 512, 3072], BF16)
    cc_out_ctx = P.scratch("cc_out_ctx", [1536, SC], BF16)
    if stage == "A":
        dbg_lat = P.out("dbg_m_lat", [384, SL], BF16)
        dbg_ctx = P.out("dbg_m_ctx", [384, SC], BF16)
        dbg_z = P.out("dbg_z", [6, 128, TA], BF16)
        dbg_mod = P.out("dbg_mod", [128, 16, 2], F32)
        dbg_gbc = P.out("dbg_gbc", [128, 2, D], F32)

    with contextlib.ExitStack() as top:
        S = Sync(nc, top)
        gbc1 = P.scratch("gbc1sc", [128, D], F32)
        zscB = S.view(zsc, "zsc")
        ccinB = S.view(cc_in_lat, "ccin")
        ccincB = S.view(cc_in_ctx, "ccinc")

        with contextlib.ExitStack() as st:
            S.stack = st
            ps = [S.psum("psA1_%d" % i, [128, 512], F32) for i in range(6)]
            psTb = [S.psum("psA1T_%d" % i, [128, 512], BF16) for i in range(2)]
            ident = S.sbuf("ident", [128, 128], BF16)
            S.dma("pool", [(ident[:], I["ident"].ap())], W=[ident])
            const = adaln_setup(S, P, ps)
            mod0 = S.sbuf("mod0", [128, 16, 2], F32)
            gbc0 = S.sbuf("gbc0", [128, 2, D], F32)
            adaln(S, P, 0, const, mod0, gbc0)
            if stage == "A":
                S.dma("sp", [(dbg_mod.ap(), mod0[:])], R=[mod0])
                S.dma("sp", [(dbg_gbc.ap(), gbc0[:])], R=[gbc0])

            win = S.sbuf("win", [128, 8, 768], BF16)
            for hf in range(2):
                ws = S.sbuf("wst%d" % hf, [128, 8, 384], F32)
                S.dma("sp" if hf == 0 else "pool",
                      [(ws[:], I["win0"].ap()[:, hf * 384:(hf + 1) * 384].rearrange("(k p) c -> p k c", p=128))], W=[ws])
                S.op("dve" if hf == 0 else "pool", lambda e, ws=ws, hf=hf: e.tensor_copy(
                    out=win[:, :, hf * 384:(hf + 1) * 384], in_=ws[:]), R=[ws], W=[win])

            xin = [S.sbuf("xin%d" % i, [128, 4, D], BF16) for i in range(3)]
            uTt = [S.sbuf("uT%d" % i, [128, 8, 512], BF16).t for i in range(2)]
            uT = [[S.view(uTt[i], "uT%d_%d" % (i, k)) for k in range(8)] for i in range(2)]
            zstt = [S.sbuf("zst%d" % i, [128, 6, 512], BF16).t for i in range(2)]
            zst = [[S.view(zstt[i], "zst%d_%d" % (i, c)) for c in range(6)] for i in range(2)]
            psT = psTb
            psZ = ps[3:6]
            ntiles = 1 + SL // 512

            def tinfo(t):
                if t == 0:
                    return SC, 0, 1, I["ctx"].ap().rearrange("(j p) d -> p j d", p=128)
                return 512, SC + (t - 1) * 512, 0, I["x_all"].ap()[(t - 1) * 512:t * 512, :].rearrange("(j p) d -> p j d", p=128)

            def a1_load(t):
                ntok, tok0, j, src = tinfo(t)
                xi = xin[t % 3]
                S.dma("pool", [(xi[:, 0:ntok // 128, :], src)], W=[xi])

            def a1_T(t, k):
                ntok, tok0, j, src = tinfo(t)
                xi = xin[t % 3]
                bank = psT[k % 2]
                for jj in range(ntok // 128):
                    S.op("pe", lambda e, jj=jj: e.transpose(
                        out=bank[:, jj * 128:(jj + 1) * 128], in_=xi[:, jj, k * 128:(k + 1) * 128], identity=ident[:]),
                        R=[xi, ident], W=[bank])
                u = uT[t % 2][k]
                if k % 2 == 0:
                    S.op("act", lambda e: e.activation(
                        out=u[:, k, 0:ntok], in_=bank[:, 0:ntok], func=AF.Identity,
                        scale=mod0[:, 8 + k, j:j + 1], bias=mod0[:, k, j:j + 1]), R=[bank, mod0], W=[u])
                else:
                    S.op("dve", lambda e: e.tensor_scalar(
                        out=u[:, k, 0:ntok], in0=bank[:, 0:ntok], scalar1=mod0[:, 8 + k, j:j + 1],
                        scalar2=mod0[:, k, j:j + 1], op0=ALU.mult, op1=ALU.add), R=[bank, mod0], W=[u])

            def a1_Z(t, c):
                ntok, tok0, j, src = tinfo(t)
                zb = psZ[c % 3]
                for k in range(8):
                    u = uT[t % 2][k]
                    S.op("pe", lambda e, u=u, k=k: e.matmul(
                        zb[:, 0:ntok], lhsT=win[:, k, c * 128:(c + 1) * 128], rhs=u[:, k, 0:ntok],
                        start=(k == 0), stop=(k == 7)), R=[win, u], W=[zb])
                zs = zst[t % 2][c]
                if c < 3:
                    S.op("dve", lambda e: e.tensor_copy(out=zs[:, c, 0:ntok], in_=zb[:, 0:ntok]), R=[zb], W=[zs])
                else:
                    S.op("act", lambda e: e.activation(out=zs[:, c, 0:ntok], in_=zb[:, 0:ntok], func=AF.Silu), R=[zb], W=[zs])
                if c == 5:
                    zt = zst[t % 2][0].t
                    S.dma("sp" if t % 2 == 1 else "act",
                          [(zsc.ap()[:, :, tok0:tok0 + ntok].rearrange("c p t -> p c t"), zt[:, :, 0:ntok])],
                          R=zst[t % 2], W=[], owner=zst[t % 2][0])

            a1_load(0)
            a1_load(1)
            a1_load(2)
            for k in range(8):
                a1_T(0, k)
            for t in range(ntiles):
                for k in range(8):
                    if t + 1 < ntiles:
                        a1_T(t + 1, k)
                    if k < 6:
                        a1_Z(t, k)
                if t + 3 < ntiles:
                    a1_load(t + 3)
            S.barrier()
            S.flush()
        phase_A2(S, P, zsc, cc_in_lat, cc_in_ctx)
        phase_A3(S, P, zsc, cc_in_lat, cc_in_ctx)
        if stage == "A":
            with contextlib.ExitStack() as st:
                S.stack = st
                for c in range(3):
                    bt = S.sbuf("dbgm%d" % c, [128, SC], BF16)
                    S.dma("sp", [(bt[:], cc_in_ctx.ap()[c * 128:(c + 1) * 128, :])], W=[bt])
                    S.dma("sp", [(dbg_ctx.ap()[c * 128:(c + 1) * 128, :], bt[:])], R=[bt])
                    bl = S.sbuf("dbgl%d" % c, [128, SL], BF16)
                    S.dma("sp", [(bl[:, i * 1024:(i + 1) * 1024].rearrange("p (b t) -> p b t", t=512),
                                  cc_in_lat.ap()[i].rearrange("p (b c t) -> p b c t", b=2, c=3)[:, :, c, :]) for i in range(16)], W=[bl])
                    S.dma("sp", [(dbg_lat.ap()[c * 128:(c + 1) * 128, :], bl[:])], R=[bl])
                S.barrier()
                S.flush()
        if stage == "A":
            with contextlib.ExitStack() as st:
                S.stack = st
                for c in range(6):
                    bt = S.sbuf("dbgb%d" % c, [128, TA], BF16)
                    S.dma("sp", [(bt[:], zsc.ap()[c])], W=[bt])
                    S.dma("sp", [(dbg_z.ap()[c], bt[:])], R=[bt])
                S.barrier()
                S.flush()
        if stage == "A":
            return P
        S.stack = top
        h1sc = P.scratch("h1sc", [TOK, D], F32)
        gsc1 = P.scratch("gsc1", [8, 128, TOK], BF16)
        qsc = P.scratch("qsc", [NH, 96, TOK], BF16)
        cc2_in = (P.scratch("cc2c_in", [160, SC], BF16), P.scratch("cc2l_in", [2, 160, 2048], BF16))
        cc2_out = (P.scratch("cc2c_out", [640, SC], BF16), P.scratch("cc2l_out", [2, 640, 2048], BF16))
        osc = P.scratch("osc", [8, 128, TOK], BF16)
        outT = P.out("out", [TOK, D], F32)
        phase_B(S, P, (cc_in_lat, cc_in_ctx, cc_out_lat, cc_out_ctx), h1sc, gsc1, qsc, cc2_in, gbc1)
        if stage == "B":
            dbg_h1 = P.out("dbg_h1", [TOK, D], F32)
            dbg_q = P.out("dbg_q", [NH, 96, TOK], BF16)
            dbg_kv = P.out("dbg_kv", [160, SC + TOK], BF16)
            with contextlib.ExitStack() as st:
                S.stack = st
                bts = [S.sbuf("dbh%d" % i, [128, 4, D], F32) for i in range(2)]
                for i in range(TOK // 512):
                    bt = bts[i % 2]
                    S.dma("sp", [(bt[:], h1sc.ap()[i * 512:(i + 1) * 512, :].rearrange("(j p) d -> p j d", p=128))], W=[bt])
                    S.dma("sp", [(dbg_h1.ap()[i * 512:(i + 1) * 512, :].rearrange("(j p) d -> p j d", p=128), bt[:])], R=[bt])
                bqs = [S.sbuf("dbq%d" % i, [96, TOK], BF16) for i in range(2)]
                for h in range(NH):
                    bt = bqs[h % 2]
                    S.dma("sp", [(bt[:], qsc.ap()[h])], W=[bt])
                    S.dma("sp", [(dbg_q.ap()[h], bt[:])], R=[bt])
                for (r0, n) in ((0, 128), (128, 32)):
                    bt = S.sbuf("dbk%d" % r0, [n, SC + TOK], BF16)
                    S.dma("sp", [(bt[:, 0:SC], cc2_in[0].ap()[r0:r0 + n, :]),
                                 (bt[:, SC:SC + 2048], cc2_in[1].ap()[0, r0:r0 + n, :]),
                                 (bt[:, SC + 2048:SC + 4096], cc2_in[1].ap()[1, r0:r0 + n, :])], W=[bt])
                    S.dma("sp", [(dbg_kv.ap()[r0:r0 + n, :], bt[:])], R=[bt])
                S.barrier()
                S.flush()
            return P
        phase_C(S, P, cc2_in, cc2_out, qsc, gsc1, osc)
        phase_D(S, P, osc, h1sc, gbc1, outT)
        return P


def _fm(v, nchunk):
    return np.ascontiguousarray(np.asarray(v, np.float32).reshape(nchunk, 128).T)


def _consts():
    n = np.arange(128)
    ang = 2.0 * np.pi * np.outer(n, n) / 128.0
    c128 = np.cos(ang).astype(np.float32)
    s128 = np.sin(ang).astype(np.float32)
    angt = 2.0 * np.pi * np.outer(n, n) / 16384.0
    tr = np.cos(angt).astype(np.float32)
    sn = np.sin(angt).astype(np.float32)
    m = np.arange(256)
    ang256 = 2.0 * np.pi * np.outer(m, m) / 256.0
    c256 = np.cos(ang256).astype(np.float32).reshape(2, 128, 256).transpose(1, 0, 2)
    s256 = np.sin(ang256).astype(np.float32).reshape(2, 128, 256).transpose(1, 0, 2)
    return dict(
        ident=np.eye(128, dtype=np.float32), c128=c128, s128=s128, s128n=-s128,
        cs1=np.concatenate([c128, -s128], 1), cs2=np.concatenate([s128, c128], 1),
        tt1=np.concatenate([tr, tr], 1), tt2=np.concatenate([sn, -sn], 1),
        c256=np.ascontiguousarray(c256), s256=np.ascontiguousarray(s256))


def _rope_tables(tc):
    t = np.arange(TOK) + TOK * tc
    inv = (10000.0 ** (-np.arange(8, dtype=np.float32) / 8.0)).astype(np.float32)
    row = (t // 64).astype(np.float32)
    col = (t % 64).astype(np.float32)
    ar = row[None, :] * inv[:, None]
    ac = col[None, :] * inv[:, None]
    cosf = np.concatenate([np.cos(ar), np.cos(ar), np.cos(ac), np.cos(ac)], 0)
    sinf = np.concatenate([-np.sin(ar), np.sin(ar), -np.sin(ac), np.sin(ac)], 0)
    c = np.zeros((96, TOK), np.float32)
    s = np.zeros((96, TOK), np.float32)
    c[64:96] = cosf
    s[64:96] = sinf
    return c, s


_ROPE_PERM = np.concatenate([np.arange(8, 16), np.arange(0, 8), np.arange(24, 32), np.arange(16, 24)])


def prep_inputs(inp):
    f = lambda a: np.ascontiguousarray(np.asarray(a, np.float32))
    x, c, ctx, c_ctx = f(inp["x"]), f(inp["c"]), f(inp["ctx"]), f(inp["c_ctx"])
    ada_w, ada_b = f(inp["ada_w"]), f(inp["ada_b"])
    w_in_rf = f(inp["w_in_rf"])[0]
    conv_w, conv_b = f(inp["conv_w"])[0], f(inp["conv_b"])[0]
    gw, gb, lam = f(inp["lru_gate_w"])[0], f(inp["lru_gate_b"])[0], f(inp["lru_lambda"])[0]
    fnw, fnb = f(inp["fnet_w"])[0], f(inp["fnet_b"])[0]
    w_out_rf = f(inp["w_out_rf"])[0]
    w_in_mla = f(inp["w_in_mla"])[0]
    qng, kvg = f(inp["q_norm_g"])[0], f(inp["kv_norm_g"])[0]
    w_uq, w_ukv, w_out_mla = f(inp["w_uq"])[0], f(inp["w_ukv"])[0], f(inp["w_out_mla"])[0]
    cs = _consts()
    ada_bf = np.ascontiguousarray(ada_b.reshape(2, 24, 128).transpose(2, 0, 1))
    ada_bg = np.ascontiguousarray(ada_b[:, 2048:3072])
    perm_rows = np.concatenate([np.concatenate([np.arange(256 * r, 256 * r + 256),
                                                1024 + np.arange(128 * r, 128 * r + 128)]) for r in range(4)])
    wout0 = np.ascontiguousarray(w_out_rf[perm_rows])
    kr_cols = 384 + _ROPE_PERM
    wkr = np.ascontiguousarray(np.concatenate([w_in_mla[:, 256:320], w_in_mla[:, 384:416]], 1))
    wkrp = np.ascontiguousarray(np.concatenate([w_in_mla[:, 256:320], w_in_mla[:, kr_cols]], 1))
    qperm = np.arange(1536).reshape(16, 96)
    qperm[:, 64:96] = qperm[:, 64:96][:, _ROPE_PERM]
    wuqp = np.ascontiguousarray(w_uq[:, qperm.reshape(-1)])
    maps = []
    for core in range(8):
        b, g = core // 4, core % 4
        cols = np.concatenate([np.arange(256 * g, 256 * g + 256), 1024 + np.arange(128 * g, 128 * g + 128),
                               1536 + np.arange(256 * g, 256 * g + 256), 2560 + np.arange(128 * g, 128 * g + 128)])
        ch = np.arange(256 * g, 256 * g + 256)
        rc, rs = _rope_tables(g)
        idx = np.zeros((32, 128, 1), np.int32)
        for blk in range(8):
            for r in range(4):
                gblk = g * 8 + blk
                idx[blk * 4 + r, :, 0] = ((gblk // 2) * 512 + r * 128 + np.arange(128)) * 2 + gblk % 2
        m = dict(
            x_all=x[b], x_own=np.ascontiguousarray(x[b, TOK * g:TOK * (g + 1)]), ctx=ctx[b],
            cvec=np.ascontiguousarray(np.stack([c[b], c_ctx], -1).reshape(8, 128, 2).transpose(1, 0, 2)),
            ada_w=ada_w, ada_bf=ada_bf, ada_bg=ada_bg, ln_g=f(inp["ln_g"]), ln_b=f(inp["ln_b"]),
            win0=np.ascontiguousarray(w_in_rf[:, cols]),
            convw=np.ascontiguousarray(conv_w[:, ch].reshape(4, 2, 128).transpose(2, 1, 0)),
            convb=_fm(conv_b[ch], 2),
            gatew=np.ascontiguousarray(gw[:, :, 4 * g:4 * g + 4]),
            gateb=np.ascontiguousarray(gb[:, :, ch].reshape(2, 2, 2, 128).transpose(3, 0, 1, 2)),
            lam=np.ascontiguousarray(lam[:, ch].reshape(2, 2, 128).transpose(2, 0, 1)),
            fw=fnw[g], fb=np.ascontiguousarray(fnb[128 * g:128 * g + 128, None]),
            wout0=wout0, idx=np.ascontiguousarray(idx),
            win1=w_in_mla, qng=_fm(qng, 2), kvg=_fm(kvg, 1),
            wuq=w_uq, wuqp=wuqp, wukv=w_ukv, wout1=w_out_mla,
            ropec=rc.astype(ml_dtypes.bfloat16), ropes=rs.astype(ml_dtypes.bfloat16), wkr=wkr, wkrp=wkrp, **cs)
        maps.append(m)
    return maps


_CACHE = {}


def kernel(**inputs):
    maps = prep_inputs(inputs)
    if "full" not in _CACHE:
        _CACHE["full"] = build("full")
    P = _CACHE["full"]
    res = run_bass_kernel_spmd(P.nc, maps, core_ids=list(range(8)))
    out = np.zeros((2, SL, D), np.float32)
    for core in range(8):
        b, g = core // 4, core % 4
        out[b, TOK * g:TOK * (g + 1)] = res.results[core]["out"]
    return out
```

```python
import contextlib
import numpy as np
import ml_dtypes
import concourse.bass as bass
import concourse.mybir as mybir
from concourse.bass_utils import run_bass_kernel_spmd

F32 = mybir.dt.float32
BF16 = mybir.dt.bfloat16
I32 = mybir.dt.int32
AF = mybir.ActivationFunctionType
ALU = mybir.AluOpType

D = 1024
SL = 16384
SC = 256
TA = SL + SC
TOK = 4096
NH = 16
ALPHA = 4.0 ** 0.25
LN_EPS = 1e-6
RMS_EPS = 1e-6
ATTN_SCALE = 96.0 ** -0.5


class Buf:
    __slots__ = ("t", "lw", "rd", "dsem", "dcnt", "name")

    def __init__(self, t, name=""):
        self.t = t
        self.lw = {}
        self.rd = {}
        self.dsem = None
        self.dcnt = 0
        self.name = name

    def __getitem__(self, idx):
        return self.t[idx]


class Sync:
    ENG = ("pe", "act", "dve", "pool", "sp")
    NPOOL = 64

    def __init__(self, nc, stack, same_engine_wait=True):
        self.nc = nc
        self.stack = stack
        self.sems = {}
        self.cnt = {}
        for e in ("pe", "act", "dve", "pool"):
            self.sems[e] = stack.enter_context(nc.semaphore("c_" + e))
            self.cnt[e] = 0
        self.free = {"hw": [], "sw": []}
        for i in range(self.NPOOL):
            k = "d%d" % i
            self.sems[k] = stack.enter_context(nc.semaphore(k))
            self.cnt[k] = 0
            self.free["hw" if i < 38 else "sw"].append(k)
        self.known = {e: {} for e in self.ENG}
        self.same = same_engine_wait
        self.nwaits = 0
        self.nins = 0
        self.nalloc = 0
        self.owners = []
        self.group = None
        self.prog = {e: [] for e in self.ENG}

    def sbuf(self, name, shape, dt):
        self.nalloc += 1
        t = self.stack.enter_context(self.nc.sbuf_tensor("s%d_%s" % (self.nalloc, name), list(shape), dt))
        return Buf(t, name)

    def psum(self, name, shape, dt):
        self.nalloc += 1
        t = self.stack.enter_context(self.nc.psum_tensor("p%d_%s" % (self.nalloc, name), list(shape), dt))
        return Buf(t, name)

    def view(self, t, name=""):
        return Buf(t, name)

    def _dsem(self, b, q):
        kind = "sw" if q == "pool" else "hw"
        if b.dsem is None:
            b.dsem = self.free[kind].pop()
            self.owners.append(b)
        elif (int(b.dsem[1:]) >= 38) != (kind == "sw"):
            raise RuntimeError("buffer %s mixes software- and hardware-DGE DMAs on one semaphore" % b.name)
        return b.dsem

    def release(self):
        for b in self.owners:
            self.free["sw" if int(b.dsem[1:]) >= 38 else "hw"].append(b.dsem)
            b.dsem = None
        self.owners = []

    def group_begin(self):
        self.group = (Buf(None, "grp"), [])

    def group_end(self):
        g, bufs = self.group
        self.group = None
        if g.dsem is None:
            return
        k = g.dsem
        for b in bufs:
            b.lw = {k: self.cnt[k]}

    def _waits(self, q, R, W):
        need = {}
        for b in R:
            for k, v in b.lw.items():
                if need.get(k, 0) < v:
                    need[k] = v
        for b in W:
            for d in (b.lw, b.rd):
                for k, v in d.items():
                    if need.get(k, 0) < v:
                        need[k] = v
        kn = self.known[q]
        for k, v in need.items():
            if k == q and (q == "pe" or not self.same):
                continue
            if kn.get(k, 0) >= v:
                continue
            self.prog[q].append(("w", self.sems[k], v))
            kn[k] = v
            self.nwaits += 1

    def _post(self, ev, R, W):
        k, v = ev
        for b in W:
            b.lw = {k: v}
            b.rd = {}
        for b in R:
            if b not in W:
                b.rd[k] = v

    def op(self, q, fn, R=(), W=()):
        self._waits(q, R, W)
        self.cnt[q] += 1
        self.prog[q].append(("i", fn, self.sems[q], 1))
        self._post((q, self.cnt[q]), R, W)
        self.nins += 1

    def dma(self, q, pairs, R=(), W=(), owner=None):
        if self.group is not None and owner is None:
            owner = self.group[0]
            self.group[1].extend(W)
        if owner is None:
            owner = W[0] if W else R[0]
        k = self._dsem(owner, q)
        self._waits(q, R, W)
        for (o, i) in pairs:
            self.prog[q].append(("i", (lambda e, o=o, i=i: e.dma_start(out=o, in_=i)), self.sems[k], 16))
            self.cnt[k] += 16
        self._post((k, self.cnt[k]), R, W)
        self.nins += len(pairs)

    def special(self, q, fn, inc, R=(), W=(), owner=None):
        if owner is None:
            owner = W[0]
        k = self._dsem(owner, q)
        self._waits(q, R, W)
        self.prog[q].append(("i", fn, self.sems[k], inc))
        self.cnt[k] += inc
        self._post((k, self.cnt[k]), R, W)
        self.nins += 1

    def barrier(self):
        for q in self.ENG:
            kn = self.known[q]
            for k, v in self.cnt.items():
                if v and kn.get(k, 0) < v:
                    self.prog[q].append(("w", self.sems[k], v))
                    kn[k] = v
                    self.nwaits += 1
        self.release()

    def flush(self):
        prog = self.prog
        self.prog = {e: [] for e in self.ENG}

        def run(eng, items):
            for it in items:
                if it[0] == "w":
                    eng.wait_ge(it[1], it[2])
                else:
                    it[1](eng).then_inc(it[2], it[3])

        with self.nc.Block() as block:
            @block.tensor
            def _(e):
                run(e, prog["pe"])

            @block.scalar
            def _(e):
                run(e, prog["act"])

            @block.vector
            def _(e):
                run(e, prog["dve"])

            @block.gpsimd
            def _(e):
                run(e, prog["pool"])

            @block.sync
            def _(e):
                run(e, prog["sp"])


def bcast_rows(dram_ap_1d_tensor, offset, n, parts=128):
    return bass.AP(dram_ap_1d_tensor, offset, [[0, parts], [1, n]])


class Prog:
    def __init__(self, stage="full"):
        self.stage = stage
        self.nc = bass.Bass("TRN2", target_bir_lowering=False)
        self.ins = {}
        self.outs = {}

    def inp(self, name, shape, dt=F32):
        t = self.nc.dram_tensor(name, list(shape), dt, kind="ExternalInput")
        self.ins[name] = t
        return t

    def out(self, name, shape, dt=F32):
        t = self.nc.dram_tensor(name, list(shape), dt, kind="ExternalOutput")
        self.outs[name] = t
        return t

    def scratch(self, name, shape, dt):
        return self.nc.dram_tensor(name, list(shape), dt)


def declare_inputs(P):
    i = P.inp
    i("x_all", [SL, D]); i("x_own", [TOK, D]); i("ctx", [SC, D])
    i("cvec", [128, 8, 2])
    i("ada_w", [2, D, 3 * D]); i("ada_bf", [128, 2, 24]); i("ada_bg", [2, D])
    i("ln_g", [2, D]); i("ln_b", [2, D])
    i("win0", [D, 768]); i("convw", [128, 2, 4]); i("convb", [128, 2])
    i("gatew", [2, 2, 4, 64, 64]); i("gateb", [128, 2, 2, 2]); i("lam", [128, 2, 2])
    i("fw", [128, 128]); i("fb", [128, 1])
    i("ident", [128, 128]); i("c128", [128, 128]); i("s128", [128, 128]); i("s128n", [128, 128])
    i("cs1", [128, 256]); i("cs2", [128, 256]); i("tt1", [128, 256]); i("tt2", [128, 256])
    i("c256", [128, 2, 256]); i("s256", [128, 2, 256])
    i("wout0", [1536, D])
    i("idx", [32, 128, 1], I32)
    i("win1", [D, 1440]); i("wkr", [D, 96]); i("wkrp", [D, 96])
    i("qng", [128, 2]); i("kvg", [128, 1])
    i("wuq", [256, 1536]); i("wuqp", [256, 1536]); i("wukv", [128, 2048]); i("wout1", [D, D])
    i("ropec", [96, TOK], BF16); i("ropes", [96, TOK], BF16)


def adaln(S, P, layer, const, mod, gbc, q2="pool"):
    ada_w = P.ins["ada_w"]
    abg = const["abg"]
    S.dma("sp", [(abg[:], bcast_rows(P.ins["ada_bg"], layer * D, D))], W=[abg])
    psm = const["ps"][0]
    wb = const["adawblk"]
    scf, scbc, abf = const["scf"], const["scbc"], const["ada_bf"]
    for cb in range(12):
        w = wb[cb % 2]
        src = ada_w.ap()[layer, :, cb * 256:(cb + 1) * 256].rearrange("(k p) c -> p k c", p=128)
        S.dma("sp" if cb % 2 == 0 else q2, [(w[:], src)], W=[w])
        if cb < 8:
            for fi in range(2):
                fc = cb * 2 + fi
                for k in range(8):
                    S.op("pe", lambda e, w=w, k=k, fi=fi, fc=fc: e.matmul(
                        psm[:, fc * 2:fc * 2 + 2], lhsT=w[:, k, fi * 128:(fi + 1) * 128], rhs=scf[:, k, :],
                        start=(k == 0), stop=(k == 7)), R=[w, scf], W=[psm])
        else:
            for j in range(2):
                pg = const["ps"][1 + j]
                for k in range(8):
                    S.op("pe", lambda e, w=w, k=k, j=j, pg=pg: e.matmul(
                        pg[:, 0:256], lhsT=scbc[:, k, j, :], rhs=w[:, k, :], start=(k == 0), stop=(k == 7)),
                        R=[w, scbc], W=[pg])
                c0 = (cb - 8) * 256
                S.op("dve", lambda e, j=j, pg=pg, c0=c0: e.tensor_tensor(
                    out=gbc[:, j, c0:c0 + 256], in0=pg[:, 0:256], in1=abg[:, c0:c0 + 256], op=ALU.add),
                    R=[pg, abg], W=[gbc])
        if cb == 7:
            S.op("dve", lambda e: e.tensor_tensor(
                out=mod[:, :, :], in0=psm[:, 0:32].rearrange("p (f j) -> p f j", j=2),
                in1=abf[:, layer, 0:16].unsqueeze(2).to_broadcast([128, 16, 2]), op=ALU.add),
                R=[psm, abf], W=[mod])
            S.op("dve", lambda e: e.tensor_scalar(
                out=mod[:, 8:16, :], in0=mod[:, 8:16, :], scalar1=1.0, scalar2=None, op0=ALU.add),
                R=[mod], W=[mod])


def adaln_setup(S, P, ps):
    I = P.ins
    cv = S.sbuf("cv", [128, 8, 2], F32)
    S.dma("sp", [(cv[:], I["cvec"].ap())], W=[cv])
    scf = S.sbuf("scf", [128, 8, 2], F32)
    S.op("act", lambda e: e.activation(out=scf[:], in_=cv[:], func=AF.Silu), R=[cv], W=[scf])
    scbc = S.sbuf("scbc", [128, 8, 2, 128], F32)
    S.op("dve", lambda e: e.tensor_copy(
        out=scbc[:].rearrange("p k j m -> p (k j) m"),
        in_=scf[:].rearrange("p k j -> p (k j)").unsqueeze(2).to_broadcast([128, 16, 128])), R=[scf], W=[scbc])
    abf = S.sbuf("abf", [128, 2, 24], F32)
    S.dma("sp", [(abf[:], I["ada_bf"].ap())], W=[abf])
    abg = S.sbuf("abg", [128, D], F32)
    return dict(ps=ps, scf=scf, scbc=scbc, ada_bf=abf, abg=abg,
                adawblk=[S.sbuf("adaw%d" % i, [128, 8, 256], F32) for i in range(2)])


def phase_A2(S, P, zsc, cc_in_lat, cc_in_ctx):
    I = P.ins
    TS = 1024
    with contextlib.ExitStack() as st:
        S.stack = st
        ps = [S.psum("psA2_%d" % i, [128, 512], F32) for i in range(8)]
        cw = S.sbuf("cw", [128, 2, 4], F32); S.dma("sp", [(cw[:], I["convw"].ap())], W=[cw])
        cb = S.sbuf("cb", [128, 2], F32); S.dma("sp", [(cb[:], I["convb"].ap())], W=[cb])
        gb = S.sbuf("gb", [128, 2, 2, 2], F32); S.dma("sp", [(gb[:], I["gateb"].ap())], W=[gb])
        lam = S.sbuf("lam", [128, 2, 2], F32); S.dma("sp", [(lam[:], I["lam"].ap())], W=[lam])
        identf = S.sbuf("identf", [128, 128], F32); S.dma("sp", [(identf[:], I["ident"].ap())], W=[identf])
        sp_ = S.sbuf("sp_", [128, 2, 2], F32)
        S.op("act", lambda e: e.activation(out=sp_[:], in_=lam[:], func=AF.Exp, scale=-1.0), R=[lam], W=[sp_])
        S.op("act", lambda e: e.activation(out=sp_[:], in_=sp_[:], func=AF.Ln, bias=1.0, scale=1.0), R=[sp_], W=[sp_])
        sc8 = S.sbuf("sc8", [128, 2, 2], F32)
        sc16 = S.sbuf("sc16", [128, 2, 2], F32)
        S.op("dve", lambda e: e.tensor_scalar(out=sc8[:], in0=sp_[:], scalar1=-8.0, scalar2=None, op0=ALU.mult), R=[sp_], W=[sc8])
        S.op("dve", lambda e: e.tensor_scalar(out=sc16[:], in0=sp_[:], scalar1=-16.0, scalar2=None, op0=ALU.mult), R=[sp_], W=[sc16])
        gwf = S.sbuf("gwf", [128, 8, 128], F32)
        S.op("pool", lambda e: e.memset(gwf[:], 0.0), W=[gwf])
        pairs = []
        for c in range(2):
            for d in range(2):
                for kd in range(2):
                    for hh in range(2):
                        pairs.append((gwf[hh * 64:(hh + 1) * 64, (c * 2 + d) * 2 + kd, hh * 64:(hh + 1) * 64],
                                      I["gatew"].ap()[d, kd, 2 * c + hh]))
        S.dma("sp", pairs, W=[gwf])
        gw = S.sbuf("gw", [128, 8, 128], BF16)
        S.op("dve", lambda e: e.tensor_copy(out=gw[:], in_=gwf[:]), R=[gwf], W=[gw])
        dg = S.sbuf("dg", [128, 8, 128], BF16)
        for c in range(2):
            for k in range(4):
                S.op("dve", lambda e, c=c, k=k: e.tensor_scalar(
                    out=dg[:, c * 4 + k, :], in0=identf[:], scalar1=cw[:, c, k:k + 1], scalar2=None, op0=ALU.mult),
                    R=[identf, cw], W=[dg])
        XW = TA + 8
        xv = S.sbuf("xv", [128, XW], BF16)
        S.op("pool", lambda e: e.memset(xv[:], 0.0), W=[xv])
        xl_t = S.sbuf("xl", [128, TA], BF16).t
        R_t = S.sbuf("Rr", [128, TA], F32).t
        tiles = [(0, SC)] + [(SC + i * TS, TS) for i in range(SL // TS)]
        xlB = [S.view(xl_t, "xl%d" % i) for i in range(len(tiles))]
        RB = [S.view(R_t, "R%d" % i) for i in range(len(tiles))]
        tmp = {}
        for nm in ("r", "i", "a", "a2", "h"):
            tmp[nm] = [S.sbuf("t_%s%d" % (nm, i), [128, TS], F32) for i in range(2)]
        gt = [S.sbuf("gt%d" % i, [128, TS], BF16) for i in range(2)]
        mo = [S.sbuf("mo%d" % i, [128, TS], BF16) for i in range(2)]
        carry = S.sbuf("carry", [128, 1], F32)
        psR = [S.view(ps[0].t, "psR0"), S.view(ps[2].t, "psR1")]
        for c in range(2):
            S.dma("sp", [(xv[:, 1:1 + SC], zsc.ap()[c, :, 0:SC]), (xv[:, 260:260 + SL], zsc.ap()[c, :, SC:TA])], W=[xv])
            for ti, (tok0, ntok) in enumerate(tiles):
                base = tok0 if ti == 0 else 259 + (tok0 - SC)
                for h0 in range(0, ntok, 512):
                    n = min(512, ntok - h0)
                    pb = ps[4 + (h0 // 512) % 2]
                    for k in range(4):
                        S.op("pe", lambda e, pb=pb, k=k, c=c, n=n, o=base + h0 + k: e.matmul(
                            pb[:, 0:n], lhsT=dg[:, c * 4 + k, :], rhs=xv[:, o:o + n], start=(k == 0), stop=(k == 3)),
                            R=[dg, xv], W=[pb])
                    S.op("act", lambda e, pb=pb, n=n, c=c, o=tok0 + h0: e.activation(
                        out=xl_t[:, o:o + n], in_=pb[:, 0:n], func=AF.Identity, bias=cb[:, c:c + 1], scale=1.0),
                        R=[pb, cb], W=[xlB[ti]])
            def tile_gen(d, n_i, ti):
                tok0, ntok = tiles[ti]
                pi = n_i % 2
                pr, pim = ps[pi * 4:pi * 4 + 2], ps[pi * 4 + 2:pi * 4 + 4]
                for h0 in range(0, ntok, 512):
                    n = min(512, ntok - h0)
                    for kd, pp in ((0, pr), (1, pim)):
                        pb = pp[h0 // 512]
                        S.op("pe", lambda e, pb=pb, n=n, kd=kd, c=c, d=d, o=tok0 + h0: e.matmul(
                            pb[:, 0:n], lhsT=gw[:, (c * 2 + d) * 2 + kd, :], rhs=xl_t[:, o:o + n], start=True, stop=True),
                            R=[gw, xlB[ti]], W=[pb])
                r_, i_, a_, a2_, h_ = (tmp[k][pi] for k in ("r", "i", "a", "a2", "h"))
                s_, bx_, b_ = a2_, i_, i_
                for h0 in range(0, ntok, 512):
                    n = min(512, ntok - h0)
                    S.op("act", lambda e, n=n, h0=h0, pb=pr[h0 // 512], r_=r_, d=d, c=c: e.activation(
                        out=r_[:, h0:h0 + n], in_=pb[:, 0:n], func=AF.Sigmoid, bias=gb[:, d, 0, c:c + 1], scale=1.0),
                        R=[pr[h0 // 512], gb], W=[r_])
                    S.op("act", lambda e, n=n, h0=h0, pb=pim[h0 // 512], i_=i_, d=d, c=c: e.activation(
                        out=i_[:, h0:h0 + n], in_=pb[:, 0:n], func=AF.Sigmoid, bias=gb[:, d, 1, c:c + 1], scale=1.0),
                        R=[pim[h0 // 512], gb], W=[i_])
                yield
                S.op("act", lambda e, a_=a_, r_=r_, ntok=ntok, d=d, c=c: e.activation(
                    out=a_[:, 0:ntok], in_=r_[:, 0:ntok], func=AF.Exp, scale=sc8[:, d, c:c + 1]), R=[r_, sc8], W=[a_])
                S.op("act", lambda e, a2_=a2_, r_=r_, ntok=ntok, d=d, c=c: e.activation(
                    out=a2_[:, 0:ntok], in_=r_[:, 0:ntok], func=AF.Exp, scale=sc16[:, d, c:c + 1]), R=[r_, sc16], W=[a2_])
                yield
                S.op("act", lambda e, s_=s_, a2_=a2_, ntok=ntok: e.activation(
                    out=s_[:, 0:ntok], in_=a2_[:, 0:ntok], func=AF.Sqrt, bias=1.0, scale=-1.0), R=[a2_], W=[s_])
                yield
                S.op("pool", lambda e, bx_=bx_, i_=i_, ntok=ntok, tok0=tok0: e.tensor_tensor(
                    out=bx_[:, 0:ntok], in0=i_[:, 0:ntok], in1=xl_t[:, tok0:tok0 + ntok], op=ALU.mult),
                    R=[i_, xlB[ti]], W=[bx_])
                S.op("dve", lambda e, b_=b_, bx_=bx_, s_=s_, ntok=ntok: e.tensor_tensor(
                    out=b_[:, 0:ntok], in0=bx_[:, 0:ntok], in1=s_[:, 0:ntok], op=ALU.mult), R=[bx_, s_], W=[b_])
                init = 0.0 if n_i == 0 else carry[:, 0:1]
                if d == 0:
                    S.op("dve", lambda e, a_=a_, b_=b_, ntok=ntok, tok0=tok0, init=init: e.tensor_tensor_scan(
                        out=R_t[:, tok0:tok0 + ntok], data0=a_[:, 0:ntok], data1=b_[:, 0:ntok], initial=init,
                        op0=ALU.mult, op1=ALU.add), R=[a_, b_, carry], W=[RB[ti]])
                    S.op("dve", lambda e, o=tok0 + ntok - 1: e.tensor_copy(out=carry[:], in_=R_t[:, o:o + 1]),
                         R=[RB[ti]], W=[carry])
                else:
                    S.op("dve", lambda e, a_=a_, b_=b_, h_=h_, ntok=ntok, init=init: e.tensor_tensor_scan(
                        out=h_[:, 0:ntok][:, ::-1], data0=a_[:, 0:ntok][:, ::-1], data1=b_[:, 0:ntok][:, ::-1], initial=init,
                        op0=ALU.mult, op1=ALU.add), R=[a_, b_, carry], W=[h_])
                    S.op("dve", lambda e, h_=h_: e.tensor_copy(out=carry[:], in_=h_[:, 0:1]), R=[h_], W=[carry])
                    g_ = gt[pi]
                    S.dma("sp", [(g_[:, 0:ntok], zsc.ap()[3 + c, :, tok0:tok0 + ntok])], W=[g_])
                    S.op("pool", lambda e, h_=h_, ntok=ntok, tok0=tok0: e.tensor_tensor(
                        out=h_[:, 0:ntok], in0=h_[:, 0:ntok], in1=R_t[:, tok0:tok0 + ntok], op=ALU.add),
                        R=[h_, RB[ti]], W=[h_])
                    m_ = mo[pi]
                    S.op("pool", lambda e, h_=h_, m_=m_, g_=g_, ntok=ntok: e.tensor_tensor(
                        out=m_[:, 0:ntok], in0=h_[:, 0:ntok], in1=g_[:, 0:ntok], op=ALU.mult), R=[h_, g_], W=[m_])
                    if ti == 0:
                        dst = cc_in_ctx.ap()[c * 128:(c + 1) * 128, :]
                    else:
                        dst = cc_in_lat.ap()[ti - 1].rearrange("p (b c t) -> p b c t", b=2, c=3)[:, :, c, :]
                    src_ = m_[:, 0:ntok] if ti == 0 else m_[:, 0:ntok].rearrange("p (b t) -> p b t", t=512)
                    S.dma("sp", [(dst, src_)], R=[m_], owner=m_)

            for d in range(2):
                order = list(range(len(tiles)))
                if d == 1:
                    order = [0] + order[:0:-1]
                k_ = 0
                while k_ < len(order):
                    gens = [tile_gen(d, k_ + j_, order[k_ + j_]) for j_ in range(min(2, len(order) - k_))]
                    for stage_ in range(4):
                        for g_ in gens:
                            next(g_, None)
                    k_ += len(gens)
        S.barrier()
        S.flush()


def phase_A3(S, P, zsc, cc_in_lat, cc_in_ctx):
    I = P.ins
    NL = 1.0 / np.sqrt(float(SL) * 128.0)
    NC_ = 1.0 / np.sqrt(float(SC) * 128.0)
    with contextlib.ExitStack() as st:
        S.stack = st
        ps = [S.psum("psA3_%d" % i, [128, 512], F32) for i in range(8)]
        def ld(name, shape, src, dt=F32):
            t = S.sbuf(name, shape, dt)
            S.dma("sp", [(t[:], src)], W=[t])
            return t
        stg = S.sbuf("stg", [128, 512], F32)
        def ldbf(name, shape, src):
            n = int(np.prod(shape[1:]))
            fv = stg[:, 0:n]
            if len(shape) == 3:
                fv = fv.rearrange("p (a b) -> p a b", a=shape[1])
            S.dma("sp", [(fv, src)], W=[stg])
            b = S.sbuf(name, shape, BF16)
            S.op("dve", lambda e: e.tensor_copy(out=b[:], in_=fv), R=[stg], W=[b])
            return b
        c128f = ld("c128f", [128, 128], I["c128"].ap())
        s128nf = ld("s128nf", [128, 128], I["s128n"].ap())
        fwf = ld("fwf", [128, 128], I["fw"].ap())
        fb = ld("fbb", [128, 1], I["fb"].ap())
        c128 = ldbf("c128b", [128, 128], I["c128"].ap())
        s128 = ldbf("s128b", [128, 128], I["s128"].ap())
        cs1 = ldbf("cs1", [128, 256], I["cs1"].ap())
        cs2 = ldbf("cs2", [128, 256], I["cs2"].ap())
        tt1 = ld("tt1", [128, 256], I["tt1"].ap())
        tt2 = ld("tt2", [128, 256], I["tt2"].ap())
        c256 = ldbf("c256", [128, 2, 256], I["c256"].ap())
        s256 = ldbf("s256", [128, 2, 256], I["s256"].ap())
        mcat = S.sbuf("mcat", [128, 256], BF16)
        S.op("pe", lambda e: e.matmul(ps[0][:, 0:128], lhsT=c128f[:], rhs=fwf[:], start=True, stop=True), R=[c128f, fwf], W=[ps[0]])
        S.op("pe", lambda e: e.matmul(ps[0][:, 128:256], lhsT=s128nf[:], rhs=fwf[:], start=True, stop=True), R=[s128nf, fwf], W=[ps[0]])
        S.op("dve", lambda e: e.tensor_copy(out=mcat[:], in_=ps[0][:, 0:256]), R=[ps[0]], W=[mcat])
        zf = S.sbuf("zf", [128, TA], BF16)
        S.dma("sp", [(zf[:], zsc.ap()[2])], W=[zf])
        gf = S.sbuf("gf", [128, TA], BF16)
        S.dma("sp", [(gf[:], zsc.ap()[5])], W=[gf])
        pc = S.sbuf("pc", [128, 2, 256], BF16)
        for lc in range(2):
            S.op("pe", lambda e, lc=lc: e.matmul(ps[1][:, lc * 256:(lc + 1) * 256], lhsT=zf[:, lc * 128:(lc + 1) * 128], rhs=mcat[:],
                                                 start=True, stop=True), R=[zf, mcat], W=[ps[1]])
        S.op("dve", lambda e: e.tensor_copy(out=pc[:].rearrange("p a b -> p (a b)"), in_=ps[1][:, :]), R=[ps[1]], W=[pc])
        n = 0
        for lc in range(2):
            for (half, tab) in ((0, c256), (1, s256)):
                S.op("pe", lambda e, lc=lc, half=half, tab=tab, n=n: e.matmul(
                    ps[2][:, 0:256], lhsT=pc[:, lc, half * 128:(half + 1) * 128], rhs=tab[:, lc, :],
                    start=(n == 0), stop=(n == 3)), R=[pc, tab], W=[ps[2]])
                n += 1
        tc_ = S.sbuf("tc_", [128, 256], F32)
        S.op("act", lambda e: e.activation(out=tc_[:], in_=ps[2][:, 0:256], func=AF.Identity, bias=fb[:, 0:1], scale=NC_),
             R=[ps[2], fb], W=[tc_])
        gfc = S.view(gf.t, "gfc")
        S.op("dve", lambda e: e.tensor_tensor(out=gf[:, 0:SC], in0=tc_[:], in1=gf[:, 0:SC], op=ALU.mult), R=[tc_, gf], W=[gf])
        X = S.sbuf("X", [128, 128, 128], BF16)
        Bp = S.sbuf("Bp", [128, 2, 128, 128], BF16)
        zl = zf[:, SC:TA].rearrange("p (a b) -> p a b", b=128)
        mch = S.sbuf("mch", [128, 2, 2, 64], BF16)
        S.op("dve", lambda e: e.tensor_copy(
            out=mch[:], in_=ps[0][:, 0:256].rearrange("p (ri jh jj) -> p jh ri jj", ri=2, jh=2)), R=[ps[0]], W=[mch])
        t1 = [S.sbuf("t1_%d" % i, [128, 256], F32) for i in range(2)]
        t2 = [S.sbuf("t2_%d" % i, [128, 256], F32) for i in range(2)]
        psP = ps[0:2]
        psA = ps[2:6]
        for jh in range(2):
            for l2 in range(0, 128, 4):
                pb = psP[(l2 // 4) % 2]
                for q in range(4):
                    S.op("pe", lambda e, pb=pb, q=q, l2=l2, jh=jh: e.matmul(
                        pb[:, q * 128:(q + 1) * 128], lhsT=zl[:, :, l2 + q],
                        rhs=mch[:, jh, :, :].rearrange("p a b -> p (a b)"), start=True, stop=True),
                        R=[zf, mch], W=[pb])
                if (l2 // 4) % 2 == 0:
                    S.op("act", lambda e, pb=pb, l2=l2: e.activation(
                        out=X[:, l2:l2 + 4, :].rearrange("p a b -> p (a b)"), in_=pb[:, :], func=AF.Copy), R=[pb], W=[X])
                else:
                    S.op("dve", lambda e, pb=pb, l2=l2: e.tensor_copy(
                        out=X[:, l2:l2 + 4, :].rearrange("p a b -> p (a b)"), in_=pb[:, :]), R=[pb], W=[X])
            for jj in range(64):
                j = jh * 64 + jj
                pb = psA[j % 4]
                S.op("pe", lambda e, pb=pb, jj=jj: e.matmul(pb[:, 0:256], lhsT=X[:, :, jj], rhs=cs1[:], start=True, stop=False),
                     R=[X, cs1], W=[pb])
                S.op("pe", lambda e, pb=pb, jj=jj: e.matmul(pb[:, 0:256], lhsT=X[:, :, 64 + jj], rhs=cs2[:], start=False, stop=True),
                     R=[X, cs2], W=[pb])
                a1, a2 = t1[j % 2], t2[j % 2]
                S.op("dve", lambda e, pb=pb, a1=a1: e.tensor_tensor(out=a1[:], in0=pb[:, 0:256], in1=tt1[:], op=ALU.mult),
                     R=[pb, tt1], W=[a1])
                S.op("dve", lambda e, pb=pb, a2=a2: e.tensor_tensor(
                    out=a2[:].rearrange("p (h k) -> p h k", h=2),
                    in0=pb[:, 0:256].rearrange("p (h k) -> p h k", h=2)[:, ::-1, :],
                    in1=tt2[:].rearrange("p (h k) -> p h k", h=2), op=ALU.mult), R=[pb, tt2], W=[a2])
                S.op("pool", lambda e, a1=a1, a2=a2, j=j: e.tensor_tensor(
                    out=Bp[:, :, :, j], in0=a1[:].rearrange("p (h k) -> p h k", h=2),
                    in1=a2[:].rearrange("p (h k) -> p h k", h=2), op=ALU.add), R=[a1, a2], W=[Bp])
        tb = [S.sbuf("tb%d" % i, [128, 4, 128], F32) for i in range(2)]
        psB = ps[6:8]
        gl = gf[:, SC:TA].rearrange("p (k2 k1) -> p k1 k2", k1=128)
        for k1 in range(0, 128, 4):
            pb = psB[(k1 // 4) % 2]
            for q in range(4):
                S.op("pe", lambda e, pb=pb, q=q, k1=k1: e.matmul(
                    pb[:, q * 128:(q + 1) * 128], lhsT=Bp[:, 0, k1 + q, :], rhs=c128[:], start=True, stop=False),
                    R=[Bp, c128], W=[pb])
                S.op("pe", lambda e, pb=pb, q=q, k1=k1: e.matmul(
                    pb[:, q * 128:(q + 1) * 128], lhsT=Bp[:, 1, k1 + q, :], rhs=s128[:], start=False, stop=True),
                    R=[Bp, s128], W=[pb])
            tt = tb[(k1 // 4) % 2]
            S.op("act", lambda e, pb=pb, tt=tt: e.activation(
                out=tt[:].rearrange("p a b -> p (a b)"), in_=pb[:, :], func=AF.Identity, bias=fb[:, 0:1], scale=NL),
                R=[pb, fb], W=[tt])
            S.op("dve", lambda e, tt=tt, k1=k1: e.tensor_tensor(
                out=gl[:, k1:k1 + 4, :], in0=tt[:], in1=gl[:, k1:k1 + 4, :], op=ALU.mult), R=[tt, gf], W=[gf])
        prs = [(cc_in_ctx.ap()[256:384, :], gf[:, 0:SC])]
        for i in range(16):
            prs.append((cc_in_lat.ap()[i].rearrange("p (b c t) -> p b c t", b=2, c=3)[:, :, 2, :],
                        gf[:, SC + i * 1024:SC + (i + 1) * 1024].rearrange("p (b t) -> p b t", t=512)))
        S.dma("sp", prs, R=[gf], owner=gf)
        S.barrier()
        S.flush()


def ln_epilogue(S, name, yps, resid, gate_row, lng, lnb, out_t, tmp, small):
    t = tmp
    for hf in range(2):
        S.op("dve", lambda e, hf=hf: e.tensor_tensor(out=t[:, hf * 512:(hf + 1) * 512], in0=yps[hf][:, :],
                                                     in1=gate_row[:, hf * 512:(hf + 1) * 512], op=ALU.mult),
             R=[yps[hf], gate_row.b], W=[t.b])
    S.op("dve", lambda e: e.scalar_tensor_tensor(out=t[:, :], in0=resid[:, :], scalar=ALPHA, in1=t[:, :], op0=ALU.mult, op1=ALU.add),
         R=[resid.b, t.b], W=[t.b])
    st6, mv, rs = small
    for hf in range(2):
        S.op("dve", lambda e, hf=hf: e.bn_stats(out=st6[:, hf, :], in_=t[:, hf * 512:(hf + 1) * 512]), R=[t.b], W=[st6])
    S.op("dve", lambda e: e.bn_aggr(out=mv[:, :], in_=st6[:].rearrange("p a b -> p (a b)")), R=[st6], W=[mv])
    S.op("act", lambda e: e.activation(out=rs[:, :], in_=mv[:, 1:2], func=AF.Sqrt, bias=LN_EPS, scale=1.0), R=[mv], W=[rs])
    S.op("dve", lambda e: e.reciprocal(out=rs[:, :], in_=rs[:, :]), R=[rs], W=[rs])
    S.op("dve", lambda e: e.tensor_scalar(out=t[:, :], in0=t[:, :], scalar1=mv[:, 0:1], scalar2=rs[:, 0:1],
                                          op0=ALU.subtract, op1=ALU.mult), R=[t.b, mv, rs], W=[t.b])
    S.op("pool", lambda e: e.tensor_tensor(out=t[:, :], in0=t[:, :], in1=lng[:, :], op=ALU.mult), R=[t.b, lng], W=[t.b])
    S.op("pool", lambda e: e.tensor_tensor(out=out_t[:, :], in0=t[:, :], in1=lnb[:, :], op=ALU.add), R=[t.b, lnb], W=[out_t.b])


class V:
    def __init__(self, ap_fn, b):
        self.f = ap_fn
        self.b = b

    def __getitem__(self, idx):
        return self.f()[idx]


def phase_B(S, P, cc, h1sc, gsc1, qsc, cc2_in, gbc1):
    I = P.ins
    cc_in_lat, cc_in_ctx, cc_out_lat, cc_out_ctx = cc
    ccoB = S.view(cc_out_lat, "ccol"); ccocB = S.view(cc_out_ctx, "ccoc")
    RG = [[0, 1, 2, 3], [4, 5, 6, 7]]
    S.special("pool", lambda e: e.collective_compute("AllGather", ALU.bypass, replica_groups=RG,
                                                     ins=[cc_in_ctx.ap()], outs=[cc_out_ctx.ap()]), 1, W=[ccocB])
    for i in range(16):
        S.special("pool", lambda e, i=i: e.collective_compute("AllGather", ALU.bypass, replica_groups=RG,
                                                            ins=[cc_in_lat.ap()[i]], outs=[cc_out_lat.ap()[i]]), 1, W=[ccoB])
    with contextlib.ExitStack() as stB:
        S.stack = stB
        def ld(name, shape, src, dt=F32, q="sp"):
            t = S.sbuf(name, shape, dt)
            S.dma(q, [(t[:], src)], W=[t])
            return t
        mod0 = S.sbuf("mod0B", [128, 16, 2], F32)
        gbc0 = S.sbuf("gbc0B", [128, 2, D], F32)
        mod1 = S.sbuf("mod1B", [128, 16, 2], F32)
        gbc1t = S.sbuf("gbc1B", [128, 2, D], F32)
        wo = S.sbuf("wo0", [128, 12, D], BF16)
        w1 = S.sbuf("w1", [128, 8, 1440], BF16)
        wkr = S.sbuf("wkr", [128, 8, 96], BF16)
        wkrp = S.sbuf("wkrp", [128, 8, 96], BF16)
        wuq = S.sbuf("wuq", [128, 2, 1536], BF16)
        wuqp = S.sbuf("wuqp", [128, 2, 1536], BF16)
        ident = ld("identB", [128, 128], I["ident"].ap())
        with contextlib.ExitStack() as stp:
            S.stack = stp
            psp = [S.psum("psBp_%d" % i, [128, 512], F32) for i in range(3)]
            const = adaln_setup(S, P, psp)
            adaln(S, P, 0, const, mod0, gbc0, q2="act")
            adaln(S, P, 1, const, mod1, gbc1t, q2="act")
            S.dma("sp", [(gbc1.ap(), gbc1t[:, 0, :])], R=[gbc1t], owner=gbc1t)
            stg = const["adawblk"]
            def ldw(w, nk, ncol, src2d, eng="dve"):
                for c0 in range(0, ncol, 256):
                    n = min(256, ncol - c0)
                    sg = stg[(c0 // 256) % 2]
                    S.dma("sp" if (c0 // 256) % 2 == 0 else "act",
                          [(sg[:, 0:nk, 0:n], src2d[:, c0:c0 + n].rearrange("(k p) c -> p k c", p=128))], W=[sg])
                    if eng == "act":
                        S.op("act", lambda e, w=w, sg=sg, c0=c0, n=n: e.activation(out=w[:, :, c0:c0 + n], in_=sg[:, 0:nk, 0:n], func=AF.Copy),
                             R=[sg], W=[w])
                    else:
                        S.op(eng, lambda e, w=w, sg=sg, c0=c0, n=n: e.tensor_copy(out=w[:, :, c0:c0 + n], in_=sg[:, 0:nk, 0:n]), R=[sg], W=[w])
            ldw(wo, 12, D, I["wout0"].ap()) if False else None
            for c0 in range(0, D, 256):
                for kh in range(2):
                    sg = stg[(c0 // 256 + kh) % 2]
                    S.dma("sp" if kh == 0 else "act",
                          [(sg[:, 0:6, :], I["wout0"].ap()[kh * 768:(kh + 1) * 768, c0:c0 + 256].rearrange("(k p) c -> p k c", p=128))], W=[sg])
                    if kh == 0:
                        S.op("dve", lambda e, sg=sg, c0=c0, kh=kh: e.tensor_copy(
                            out=wo[:, kh * 6:(kh + 1) * 6, c0:c0 + 256], in_=sg[:, 0:6, :]), R=[sg], W=[wo])
                    else:
                        S.op("act", lambda e, sg=sg, c0=c0, kh=kh: e.activation(
                            out=wo[:, kh * 6:(kh + 1) * 6, c0:c0 + 256], in_=sg[:, 0:6, :], func=AF.Copy), R=[sg], W=[wo])
            ldw(w1, 8, 1440, I["win1"].ap(), eng="act")
            ldw(wkr, 8, 96, I["wkr"].ap())
            ldw(wkrp, 8, 96, I["wkrp"].ap())
            ldw(wuq, 2, 1536, I["wuq"].ap(), eng="act")
            ldw(wuqp, 2, 1536, I["wuqp"].ap())
            S.barrier()
            S.flush()
        S.stack = stB
        ps = [S.psum("psB_%d" % i, [128, 512], F32) for i in range(8)]
        lng0 = ld("lng0", [128, D], bcast_rows(I["ln_g"], 0, D))
        lnb0 = ld("lnb0", [128, D], bcast_rows(I["ln_b"], 0, D), q="pool")
        qng = ld("qng", [128, 2], I["qng"].ap())
        kvg = ld("kvg", [128, 1], I["kvg"].ap())
        ropec = ld("ropec", [96, TOK], I["ropec"].ap(), dt=BF16)
        ropes = ld("ropes", [96, TOK], I["ropes"].ap(), dt=BF16, q="pool")
        ones = S.sbuf("onesB", [128, 128], BF16)
        S.op("pool", lambda e: e.memset(ones[:], 1.0), W=[ones])
        idxt = []
        S.group_begin()
        for i in range(32):
            idxt.append(ld("idx%d" % i, [128, 1], I["idx"].ap()[i], dt=I32, q="pool"))
        S.group_end()
        mblk = [S.sbuf("mblk%d" % i, [128, 12, 512], BF16) for i in range(2)]
        xres = [S.sbuf("xres%d" % i, [128, D], F32) for i in range(2)]
        h1t = [S.sbuf("h1t%d" % i, [128, D], F32) for i in range(2)]
        st6 = [S.sbuf("st6_%d" % i, [128, 2, 6], F32) for i in range(2)]
        mv = [S.sbuf("mv%d" % i, [128, 2], F32) for i in range(2)]
        rs = [S.sbuf("rs%d" % i, [128, 1], F32) for i in range(2)]
        u1t = [S.sbuf("u1T%d" % i, [128, 8, 512], BF16) for i in range(2)]
        u1B = [[S.view(u1t[i].t, "u1_%d_%d" % (i, j)) for j in range(4)] for i in range(2)]
        qcs = S.sbuf("qcs", [128, 3, 512], F32)
        sqs = S.sbuf("sqs", [128, 3, 512], BF16)
        rsq = S.sbuf("rsq", [128, 2, 512], F32)
        qnT = S.sbuf("qnT", [128, 2, 512], BF16)
        kvo = [S.sbuf("kvo%d" % i, [128, 512], BF16) for i in range(1)]
        kro = [S.sbuf("kro%d" % i, [96, 512], BF16) for i in range(1)]
        krt = S.sbuf("krt", [96, 2, 512], F32)
        gso = [S.sbuf("gso%d" % i, [128, 8, 512], BF16) for i in range(1)]
        qo = [S.sbuf("qo%d" % i, [96, 512], BF16) for i in range(2)]
        qrts = [S.sbuf("qrt%d" % i, [96, 2, 512], F32) for i in range(2)]
        rows_lat = cc_out_lat.ap().rearrange("i q (b x) -> (i q b) x", x=1536)
        psY = [ps[0:2], ps[0:2]]
        psT = ps[2:4]
        psW = ps[4:8]

        def proj_block(bi, ntok, tok0_own, is_ctx):
            u = u1t[bi % 2]
            uB = u1B[bi % 2]
            cnt = [0]
            def mm(wt, c0, ncol, pb):
                for k in range(8):
                    S.op("pe", lambda e, k=k: e.matmul(pb[0:ncol, 0:ntok], lhsT=wt[:, k, c0:c0 + ncol], rhs=u[:, k, 0:ntok],
                                                       start=(k == 0), stop=(k == 7)), R=[wt] + uB, W=[pb])
            def nextps():
                cnt[0] += 1
                return psW[cnt[0] % 4]
            pb = nextps(); mm(w1, 256, 128, pb)
            S.op("act", lambda e, pb=pb: e.activation(out=qcs[:, 2, 0:ntok], in_=pb[:, 0:ntok], func=AF.Copy), R=[pb], W=[qcs])
            S.op("act", lambda e, pb=pb: e.activation(out=sqs[:, 2, 0:ntok], in_=pb[:, 0:ntok], func=AF.Square), R=[pb], W=[sqs])
            pb = nextps()
            S.op("pe", lambda e, pb=pb: e.matmul(pb[:, 0:ntok], lhsT=ones[:], rhs=sqs[:, 2, 0:ntok], start=True, stop=True), R=[ones, sqs], W=[pb])
            S.op("act", lambda e, pb=pb: e.activation(out=rsq[:, 1, 0:ntok], in_=pb[:, 0:ntok], func=AF.Sqrt, bias=RMS_EPS, scale=1.0 / 128.0), R=[pb], W=[rsq])
            S.op("dve", lambda e: e.reciprocal(out=rsq[:, 1, 0:ntok], in_=rsq[:, 1, 0:ntok]), R=[rsq], W=[rsq])
            ko = kvo[0]
            S.op("dve", lambda e: e.scalar_tensor_tensor(out=ko[:, 0:ntok], in0=qcs[:, 2, 0:ntok], scalar=kvg[:, 0:1], in1=rsq[:, 1, 0:ntok],
                                                         op0=ALU.mult, op1=ALU.mult), R=[qcs, kvg, rsq], W=[ko])
            yield
            pb = nextps(); mm(wkr, 0, 96, pb)
            kr_ = kro[0]
            if is_ctx:
                S.op("dve", lambda e, pb=pb: e.tensor_copy(out=kr_[64:96, 0:ntok], in_=pb[64:96, 0:ntok]), R=[pb], W=[kr_])
            else:
                pb2 = nextps(); mm(wkrp, 0, 96, pb2)
                S.op("dve", lambda e, pb=pb: e.tensor_tensor(out=krt[64:96, 0, 0:ntok], in0=pb[64:96, 0:ntok],
                                                             in1=ropec[64:96, tok0_own:tok0_own + ntok], op=ALU.mult), R=[pb, ropec], W=[krt])
                S.op("dve", lambda e, pb2=pb2: e.tensor_tensor(out=krt[64:96, 1, 0:ntok], in0=pb2[64:96, 0:ntok],
                                                               in1=ropes[64:96, tok0_own:tok0_own + ntok], op=ALU.mult), R=[pb2, ropes, krt], W=[krt])
                S.op("pool", lambda e: e.tensor_tensor(out=kr_[64:96, 0:ntok], in0=krt[64:96, 0, 0:ntok], in1=krt[64:96, 1, 0:ntok], op=ALU.add),
                     R=[krt], W=[kr_])
            if is_ctx:
                dk = cc2_in[0].ap()[:, 0:ntok]
            else:
                dk = cc2_in[1].ap()[tok0_own // 2048][:, tok0_own % 2048:tok0_own % 2048 + ntok]
            S.dma("sp", [(dk[0:128, :], ko[:, 0:ntok])], R=[ko], owner=ko)
            S.dma("sp", [(dk[128:160, :], kr_[64:96, 0:ntok])], R=[kr_], owner=kr_)
            yield
            if is_ctx:
                return
            go = gso[0]
            for c in range(8):
                pb = nextps(); mm(w1, 416 + c * 128, 128, pb)
                S.op("act", lambda e, pb=pb, c=c: e.activation(out=go[:, c, 0:ntok], in_=pb[:, 0:ntok], func=AF.Silu), R=[pb], W=[go])
                yield
            S.dma("pool", [(gsc1.ap()[:, :, tok0_own:tok0_own + ntok].rearrange("c p t -> p c t"), go[:, :, 0:ntok])], R=[go], owner=go)
            for c in range(2):
                pb = nextps(); mm(w1, c * 128, 128, pb)
                S.op("act", lambda e, pb=pb, c=c: e.activation(out=qcs[:, c, 0:ntok], in_=pb[:, 0:ntok], func=AF.Copy), R=[pb], W=[qcs])
                S.op("act", lambda e, pb=pb, c=c: e.activation(out=sqs[:, c, 0:ntok], in_=pb[:, 0:ntok], func=AF.Square), R=[pb], W=[sqs])
            pb = nextps()
            for c in range(2):
                S.op("pe", lambda e, pb=pb, c=c: e.matmul(pb[:, 0:ntok], lhsT=ones[:], rhs=sqs[:, c, 0:ntok], start=(c == 0), stop=(c == 1)),
                     R=[ones, sqs], W=[pb])
            S.op("act", lambda e, pb=pb: e.activation(out=rsq[:, 0, 0:ntok], in_=pb[:, 0:ntok], func=AF.Sqrt, bias=RMS_EPS, scale=1.0 / 256.0), R=[pb], W=[rsq])
            S.op("dve", lambda e: e.reciprocal(out=rsq[:, 0, 0:ntok], in_=rsq[:, 0, 0:ntok]), R=[rsq], W=[rsq])
            for c in range(2):
                S.op("dve", lambda e, c=c: e.scalar_tensor_tensor(out=qnT[:, c, 0:ntok], in0=qcs[:, c, 0:ntok], scalar=qng[:, c:c + 1],
                                                                  in1=rsq[:, 0, 0:ntok], op0=ALU.mult, op1=ALU.mult), R=[qcs, qng, rsq], W=[qnT])
            yield
            for h in range(NH):
                pa = nextps()
                for c in range(2):
                    S.op("pe", lambda e, pa=pa, c=c, h=h: e.matmul(pa[0:96, 0:ntok], lhsT=wuq[:, c, h * 96:(h + 1) * 96], rhs=qnT[:, c, 0:ntok],
                                                                 start=(c == 0), stop=(c == 1)), R=[wuq, qnT], W=[pa])
                pp = nextps()
                for c in range(2):
                    S.op("pe", lambda e, pp=pp, c=c, h=h: e.matmul(pp[0:96, 0:ntok], lhsT=wuqp[:, c, h * 96:(h + 1) * 96], rhs=qnT[:, c, 0:ntok],
                                                                 start=(c == 0), stop=(c == 1)), R=[wuqp, qnT], W=[pp])
                q_ = qo[h % 2]
                qrt = qrts[h % 2]
                S.op("act", lambda e, pa=pa, q_=q_: e.activation(out=q_[0:64, 0:ntok], in_=pa[0:64, 0:ntok], func=AF.Copy), R=[pa], W=[q_])
                S.op("dve", lambda e, pa=pa, qrt=qrt: e.tensor_tensor(out=qrt[64:96, 0, 0:ntok], in0=pa[64:96, 0:ntok],
                                                             in1=ropec[64:96, tok0_own:tok0_own + ntok], op=ALU.mult), R=[pa, ropec], W=[qrt])
                S.op("dve", lambda e, pp=pp, qrt=qrt: e.tensor_tensor(out=qrt[64:96, 1, 0:ntok], in0=pp[64:96, 0:ntok],
                                                             in1=ropes[64:96, tok0_own:tok0_own + ntok], op=ALU.mult), R=[pp, ropes, qrt], W=[qrt])
                S.op("pool", lambda e, q_=q_, qrt=qrt: e.tensor_tensor(out=q_[64:96, 0:ntok], in0=qrt[64:96, 0, 0:ntok], in1=qrt[64:96, 1, 0:ntok], op=ALU.add),
                     R=[qrt, q_], W=[q_])
                S.dma("sp" if h % 2 == 0 else "pool", [(qsc.ap()[h, :, tok0_own:tok0_own + ntok], q_[0:96, 0:ntok])], R=[q_], owner=q_)
                yield

        mctx = S.sbuf("mctx", [128, 12, SC], BF16)
        S.dma("sp", [(mctx[:], cc_out_ctx.ap().rearrange("(k p) t -> p k t", p=128))], R=[ccocB], W=[mctx])
        ntile = 2 + TOK // 128

        def b_info(ti):
            is_ctx = ti < 2
            if is_ctx:
                return is_ctx, 1, mctx, ti * 128, I["ctx"].ap()[ti * 128:(ti + 1) * 128, :]
            li = ti - 2
            return is_ctx, 0, mblk[(li // 4) % 2], (li % 4) * 128, I["x_own"].ap()[li * 128:(li + 1) * 128, :]

        def b_gather(blk):
            mb = mblk[blk % 2]
            for r in range(4):
                S.special("pool", lambda e, r=r: e.indirect_dma_start(
                    out=mb[:, 3 * r:3 * r + 3, :].rearrange("p a b -> p (a b)"), out_offset=None, in_=rows_lat,
                    in_offset=bass.IndirectOffsetOnAxis(ap=idxt[blk * 4 + r][:, :], axis=0),
                    bounds_check=16 * 512 * 2 - 1, oob_is_err=False), 16, R=[ccoB, idxt[blk * 4 + r]], W=[mb])

        def b_y(ti):
            is_ctx, j, msrc, mcol, xsrc = b_info(ti)
            if ti == 0:
                b_gather(0)
                b_gather(1)
            if not is_ctx and (ti - 2) % 4 == 0:
                blk = (ti - 2) // 4
                if 1 <= blk and blk + 1 < TOK // 512:
                    b_gather(blk + 1)
            xr = xres[ti % 2]
            S.dma("sp", [(xr[:], xsrc)], W=[xr])
            yps = psY[ti % 2]
            for hf in range(2):
                for kc in range(12):
                    S.op("pe", lambda e, hf=hf, kc=kc: e.matmul(
                        yps[hf][:, :], lhsT=msrc[:, kc, mcol:mcol + 128], rhs=wo[:, kc, hf * 512:(hf + 1) * 512],
                        start=(kc == 0), stop=(kc == 11)), R=[msrc, wo], W=[yps[hf]])

        def b_ep(ti):
            is_ctx, j, msrc, mcol, xsrc = b_info(ti)
            xr, yps = xres[ti % 2], psY[ti % 2]
            hh = h1t[ti % 2]
            hv = V(lambda: hh[:, :], hh)
            ln_epilogue(S, "l0", yps, V(lambda: xr[:, :], xr), V(lambda: gbc0[:, j, :], gbc0), lng0, lnb0,
                        hv, hv, (st6[ti % 2], mv[ti % 2], rs[ti % 2]))
            if not is_ctx:
                S.dma("pool", [(h1sc.ap()[(ti - 2) * 128:(ti - 1) * 128, :], hh[:])], R=[hh], owner=hh)

        def b_tr(ti):
            is_ctx, j, msrc, mcol, xsrc = b_info(ti)
            hh = h1t[ti % 2]
            bi = 0 if is_ctx else 1 + (ti - 2) // 4
            sub = ti if is_ctx else (ti - 2) % 4
            u = u1t[bi % 2]
            for k in range(8):
                pb = psT[k % 2]
                S.op("pe", lambda e, pb=pb, k=k: e.transpose(out=pb[:, 0:128], in_=hh[:, k * 128:(k + 1) * 128], identity=ident[:]),
                     R=[hh, ident], W=[pb])
                S.op("act", lambda e, pb=pb, k=k: e.activation(
                    out=u[:, k, sub * 128:(sub + 1) * 128], in_=pb[:, 0:128], func=AF.Identity,
                    scale=mod1[:, 8 + k, j:j + 1], bias=mod1[:, k, j:j + 1]), R=[pb, mod1], W=[u1B[bi % 2][sub]])
            if is_ctx and ti == 1:
                pgen.append(proj_block(0, 256, 0, True))
            elif (not is_ctx) and sub == 3:
                pgen.append(proj_block(bi, 512, ((ti - 2) // 4) * 512, False))

        pgen = []

        def pump(k):
            while k > 0 and pgen:
                try:
                    next(pgen[0])
                    k -= 1
                except StopIteration:
                    pgen.pop(0)

        b_y(0)
        for ti in range(ntile):
            b_ep(ti)
            pump(3)
            if ti + 1 < ntile:
                b_y(ti + 1)
            pump(3)
            b_tr(ti)
            pump(3)
        pump(1000)
        S.barrier()
        S.flush()


def phase_C(S, P, cc2_in, cc2_out, qsc, gsc1, osc):
    I = P.ins
    NKT = TA // 128
    RG = [[0, 1, 2, 3], [4, 5, 6, 7]]
    c2o = S.view(cc2_out[0], "cc2o")
    S.special("pool", lambda e: e.collective_compute("AllGather", ALU.bypass, replica_groups=RG,
                                                     ins=[cc2_in[0].ap()], outs=[cc2_out[0].ap()]), 1, W=[c2o])
    for i in range(2):
        S.special("pool", lambda e, i=i: e.collective_compute("AllGather", ALU.bypass, replica_groups=RG,
                                                            ins=[cc2_in[1].ap()[i]], outs=[cc2_out[1].ap()[i]]), 1, W=[c2o])
    with contextlib.ExitStack() as st:
        S.stack = st
        psS = [S.psum("psS%d" % i, [128, 1024], F32) for i in range(3)]
        psO = [S.psum("psO%d" % i, [128, 512], F32) for i in range(2)]
        kvn = S.sbuf("kvn", [128, TA], BF16)
        KTt = [S.sbuf("KT%d" % i, [128, TA], BF16).t for i in range(2)]
        KTn = [S.view(KTt[i], "KTn%d" % i) for i in range(2)]
        KTr = [S.view(KTt[i], "KTr%d" % i) for i in range(2)]
        Va = [S.sbuf("Va%d" % i, [128, NKT, 128], BF16) for i in range(2)]
        qT = [S.sbuf("qT%d" % i, [128, TOK], BF16) for i in range(2)]
        pk = [(kvn[:, 0:SC], cc2_out[0].ap()[0:128, :])]
        pr = [[(KTt[i][64:96, 0:SC], cc2_out[0].ap()[128:160, :])] for i in range(2)]
        for r in range(4):
            for hf in range(2):
                c0 = SC + r * TOK + hf * 2048
                pk.append((kvn[:, c0:c0 + 2048], cc2_out[1].ap()[hf, 160 * r:160 * r + 128, :]))
                for i in range(2):
                    pr[i].append((KTt[i][64:96, c0:c0 + 2048], cc2_out[1].ap()[hf, 160 * r + 128:160 * r + 160, :]))
        S.dma("sp", pk, R=[c2o], W=[kvn])
        for i in range(2):
            S.dma("pool", pr[i], R=[c2o], W=[KTr[i]])
            S.op("pool", lambda e, i=i: e.memset(KTt[i][96:128, :], 0.0), W=[KTr[i]])
            S.op("pool", lambda e, i=i: e.memset(KTt[i][96:97, :], 1.0), W=[KTr[i]])
            S.op("pool", lambda e, i=i: e.memset(qT[i][96:128, :], 0.0), W=[qT[i]])
        S.op("pool", lambda e: e.memset(Va[0][:, :, 64:128], 1.0), W=[Va[0]])
        S.op("pool", lambda e: e.memset(Va[1][:, :, 0:64], 1.0), W=[Va[1]])
        wst = S.sbuf("wukvf", [128, 512], F32)
        wukv = S.sbuf("wukv", [128, 2048], BF16)
        for c0 in range(0, 2048, 512):
            S.dma("sp", [(wst[:], I["wukv"].ap()[:, c0:c0 + 512])], W=[wst])
            S.op("dve", lambda e, c0=c0: e.tensor_copy(out=wukv[:, c0:c0 + 512], in_=wst[:]), R=[wst], W=[wukv])
        ones = S.sbuf("onesC", [96, 128], BF16)
        S.op("pool", lambda e: e.memset(ones[:], 1.0), W=[ones])
        PT = [S.sbuf("PT%d" % i, [128, 1024], BF16) for i in range(3)]
        sq = [S.sbuf("sqC%d" % i, [96, 512], BF16) for i in range(2)]
        mx = S.sbuf("mxC", [128, 1], F32)
        qm = [S.sbuf("qmC%d" % i, [128, 1], F32) for i in range(2)]
        km = [S.sbuf("kmC%d" % i, [128, 1], F32) for i in range(2)]
        negc = [S.sbuf("negc%d" % i, [128, 1], F32) for i in range(2)]
        ot = [S.sbuf("otC%d" % i, [128, 512], F32) for i in range(2)]
        dn = [S.sbuf("dnC%d" % i, [128, 512], F32) for i in range(2)]
        gt = [S.sbuf("gtC%d" % i, [128, 512], BF16) for i in range(2)]
        oo = [S.sbuf("ooC%d" % i, [128, 512], BF16) for i in range(2)]
        cnt = [0]

        def build_units(h, split=False):
            par = h % 2
            v0 = 0 if par == 0 else 64
            S.dma("sp", [(qT[par][0:96, :], qsc.ap()[h])], W=[qT[par]])
            S.op("pool", lambda e: e.memset(qm[par][:], 0.0), W=[qm[par]])
            S.op("pool", lambda e: e.memset(km[par][:], 0.0), W=[km[par]])
            yield
            for kb in range(0, TA, 512):
                n = min(512, TA - kb)
                pb = bank[0]
                S.op("pe", lambda e, pb=pb, kb=kb, n=n: e.matmul(pb[0:64, 0:n], lhsT=wukv[:, h * 128:h * 128 + 64], rhs=kvn[:, kb:kb + n],
                                                                 start=True, stop=True), R=[wukv, kvn], W=[pb])
                S.op("act", lambda e, pb=pb, kb=kb, n=n: e.activation(out=KTt[par][0:64, kb:kb + n], in_=pb[0:64, 0:n], func=AF.Copy),
                     R=[pb], W=[KTn[par]])
                yield
            for k0 in range(0, NKT, 8):
                nk = min(8, NKT - k0)
                pb = bank[0]
                for i in range(nk):
                    S.op("pe", lambda e, pb=pb, i=i, k0=k0: e.matmul(pb[:, i * 64:(i + 1) * 64], lhsT=kvn[:, (k0 + i) * 128:(k0 + i + 1) * 128],
                                                                    rhs=wukv[:, h * 128 + 64:h * 128 + 128], start=True, stop=True),
                         R=[wukv, kvn], W=[pb])
                if (k0 // 8) % 2 == 0:
                    S.op("dve", lambda e, pb=pb, k0=k0, nk=nk: e.tensor_copy(
                        out=Va[par][:, k0:k0 + nk, v0:v0 + 64], in_=pb[:, 0:nk * 64].rearrange("p (a b) -> p a b", b=64)), R=[pb], W=[Va[par]])
                else:
                    S.op("act", lambda e, pb=pb, k0=k0, nk=nk: e.activation(
                        out=Va[par][:, k0:k0 + nk, v0:v0 + 64], in_=pb[:, 0:nk * 64].rearrange("p (a b) -> p a b", b=64), func=AF.Copy),
                        R=[pb], W=[Va[par]])
                yield
            if not split:
                for _ in norm_units(h):
                    yield

        def norm_units(h):
            par = h % 2
            work = []
            for (src, srcB, tot, acc) in ((qT[par], [qT[par]], TOK, qm[par]), (KTt[par], [KTn[par], KTr[par]], TA, km[par])):
                for b0 in range(0, tot, 512):
                    work.append((src, srcB, min(512, tot - b0), b0, acc))

            def stage_a(u):
                src, srcB, n, b0, acc = work[u]
                s_ = sq[u % 2]
                S.op("pool", lambda e: e.tensor_tensor(out=s_[:, 0:n], in0=src[0:96, b0:b0 + n], in1=src[0:96, b0:b0 + n], op=ALU.mult),
                     R=srcB, W=[s_])

            def stage_b(u):
                src, srcB, n, b0, acc = work[u]
                s_ = sq[u % 2]
                pb = bank[0]
                S.op("pe", lambda e: e.matmul(pb[:, 0:n], lhsT=ones[:], rhs=s_[:, 0:n], start=True, stop=True), R=[ones, s_], W=[pb])
                S.op("dve", lambda e: e.tensor_reduce(out=mx[:], in_=pb[:, 0:n], op=ALU.max, axis=mybir.AxisListType.X), R=[pb], W=[mx])
                S.op("dve", lambda e: e.tensor_tensor(out=acc[:], in0=acc[:], in1=mx[:], op=ALU.max), R=[acc, mx], W=[acc])

            stage_a(0)
            yield
            for u in range(len(work)):
                if u + 1 < len(work):
                    stage_a(u + 1)
                stage_b(u)
                yield
            nb = negc[par]
            S.op("dve", lambda e: e.tensor_tensor(out=nb[:], in0=qm[par][:], in1=km[par][:], op=ALU.mult), R=[qm[par], km[par]], W=[nb])
            S.op("act", lambda e: e.activation(out=nb[:], in_=nb[:], func=AF.Sqrt), R=[nb], W=[nb])
            S.op("dve", lambda e: e.tensor_scalar(out=qT[par][96:97, :], in0=KTt[par][96:97, 0:TOK], scalar1=nb[96:97, 0:1], scalar2=-1.0,
                                                  op0=ALU.mult, op1=ALU.mult), R=[nb, KTr[par], qT[par]], W=[qT[par]])

        bank = [psO[0]]
        for i_, _ in enumerate(build_units(0)):
            bank[0] = psO[i_ % 2]
        gen = [None]
        GK = 2
        groups = [list(range(k0, min(k0 + GK, NKT))) for k0 in range(0, NKT, GK)]
        NP_ = len(groups)
        NQB = TOK // 512
        pairs = [(h, qb, kp) for h in range(NH) for qb in range(NQB) for kp in range(NP_)]
        pobuf = {}

        def emit_qk(n):
            h, qb, kp = pairs[n]
            par = h % 2
            q0 = qb * 512
            if kp == 0:
                pobuf[(h, qb)] = psO[(h * NQB + qb) % 2]
                if qb == 0 and gen[0] is not None:
                    raise RuntimeError("norm units of this head were not finished in time")
            if gen[0] is not None and qb >= 3 and 4 <= kp <= NP_ - 5 and kp % 4 == 0:
                bank[0] = psO[(h * NQB + qb + 1) % 2]
                try:
                    next(gen[0])
                except StopIteration:
                    gen[0] = None
            sp_ = psS[n % 3]
            kts = groups[kp]
            for i, kt in enumerate(kts):
                S.op("pe", lambda e, i=i, kt=kt: e.matmul(
                    sp_[:, i * 512:(i + 1) * 512], lhsT=KTt[par][:, kt * 128:(kt + 1) * 128], rhs=qT[par][:, q0:q0 + 512],
                    start=True, stop=True), R=[KTn[par], KTr[par], qT[par]], W=[sp_])
            pt = PT[n % 3]
            w = 512 * len(kts)
            S.op("act", lambda e: e.activation(out=pt[:, 0:w], in_=sp_[:, 0:w], func=AF.Exp, scale=ATTN_SCALE), R=[sp_], W=[pt])

        def emit_pv(n):
            h, qb, kp = pairs[n]
            par = h % 2
            q0 = qb * 512
            po = pobuf[(h, qb)]
            pt = PT[n % 3]
            for i, kt in enumerate(groups[kp]):
                S.op("pe", lambda e, i=i, kt=kt: e.matmul(
                    po[:, :], lhsT=Va[par][:, kt, :], rhs=pt[:, i * 512:(i + 1) * 512],
                    start=(kt == 0), stop=(kt == NKT - 1)), R=[Va[par], pt], W=[po])
            if kp != NP_ - 1:
                return
            num = slice(0, 64) if par == 0 else slice(64, 128)
            den = slice(64, 128) if par == 0 else slice(0, 64)
            o_, d_, g_, oo_ = ot[qb % 2], dn[qb % 2], gt[qb % 2], oo[qb % 2]
            S.op("dve", lambda e: e.tensor_copy(out=o_[:, :], in_=po[:, :]), R=[po], W=[o_])
            S.dma("sp", [(d_[num, :], o_[den, :])], R=[o_], W=[d_])
            S.dma("pool", [(g_[num, :], gsc1.ap()[h // 2, num, q0:q0 + 512])], W=[g_])
            S.op("dve", lambda e: e.reciprocal(out=d_[num, :], in_=d_[num, :]), R=[d_], W=[d_])
            S.op("dve", lambda e: e.tensor_tensor(out=o_[num, :], in0=o_[num, :], in1=d_[num, :], op=ALU.mult), R=[o_, d_], W=[o_])
            S.op("pool", lambda e: e.tensor_tensor(out=oo_[num, :], in0=o_[num, :], in1=g_[num, :], op=ALU.mult), R=[o_, g_], W=[oo_])
            S.dma("sp", [(osc.ap()[h // 2, num, q0:q0 + 512], oo_[num, :])], R=[oo_], owner=oo_)
            if qb == 2 and h + 1 < NH:
                par_ = psS[n % 3]
                halves = [Buf(par_.t[:, 0:512], "psSh0"), Buf(par_.t[:, 512:1024], "psSh1")]
                for hb in halves:
                    hb.lw = dict(par_.lw)
                    hb.rd = dict(par_.rd)
                banks4 = [psO[0], psO[1]] + halves
                bank[0] = banks4[0]
                for i_, _ in enumerate(build_units(h + 1, split=True)):
                    bank[0] = banks4[(i_ + 1) % 4]
                for hb in halves:
                    for dd in (hb.lw, hb.rd):
                        for k_, v_ in dd.items():
                            if par_.rd.get(k_, 0) < v_:
                                par_.rd[k_] = v_
                gen[0] = norm_units(h + 1)

        LOOK = 2
        for n in range(len(pairs) + LOOK):
            if n < len(pairs):
                emit_qk(n)
            if n - LOOK >= 0:
                emit_pv(n - LOOK)
        S.barrier()
        S.flush()


def phase_D(S, P, osc, h1sc, gbc1, out):
    I = P.ins
    with contextlib.ExitStack() as st:
        S.stack = st
        ps = [S.psum("psD_%d" % i, [128, 512], F32) for i in range(4)]
        stg = S.sbuf("stgD", [128, 8, 256], F32)
        wo = S.sbuf("wo1", [128, 8, D], BF16)
        for c0 in range(0, D, 256):
            S.dma("sp", [(stg[:], I["wout1"].ap()[:, c0:c0 + 256].rearrange("(k p) c -> p k c", p=128))], W=[stg])
            S.op("dve", lambda e, c0=c0: e.tensor_copy(out=wo[:, :, c0:c0 + 256], in_=stg[:]), R=[stg], W=[wo])
        g1 = S.sbuf("gbc1D", [128, D], F32); S.dma("sp", [(g1[:], gbc1.ap())], W=[g1])
        lng = S.sbuf("lng1", [128, D], F32); S.dma("sp", [(lng[:], bcast_rows(I["ln_g"], D, D))], W=[lng])
        lnb = S.sbuf("lnb1", [128, D], F32); S.dma("pool", [(lnb[:], bcast_rows(I["ln_b"], D, D))], W=[lnb])
        oT = S.sbuf("oT", [128, 8, TOK], BF16)
        S.dma("sp", [(oT[:], osc.ap().rearrange("c p t -> p c t"))], W=[oT])
        hres = [S.sbuf("hres%d" % i, [128, D], F32) for i in range(2)]
        tmpt = [S.sbuf("tmpD%d" % i, [128, D], F32) for i in range(2)]
        outt = [S.sbuf("outD%d" % i, [128, D], F32) for i in range(2)]
        st6 = [S.sbuf("st6D%d" % i, [128, 2, 6], F32) for i in range(2)]
        mv = [S.sbuf("mvD%d" % i, [128, 2], F32) for i in range(2)]
        rs = [S.sbuf("rsD%d" % i, [128, 1], F32) for i in range(2)]
        psY = [ps[0:2], ps[2:4]]
        for ti in range(TOK // 128):
            hr = hres[ti % 2]
            S.dma("pool", [(hr[:], h1sc.ap()[ti * 128:(ti + 1) * 128, :])], W=[hr])
            yps = psY[ti % 2]
            for hf in range(2):
                for kc in range(8):
                    S.op("pe", lambda e, hf=hf, kc=kc, ti=ti, yps=yps: e.matmul(
                        yps[hf][:, :], lhsT=oT[:, kc, ti * 128:(ti + 1) * 128], rhs=wo[:, kc, hf * 512:(hf + 1) * 512],
                        start=(kc == 0), stop=(kc == 7)), R=[oT, wo], W=[yps[hf]])
            tt, ou = tmpt[ti % 2], outt[ti % 2]
            ln_epilogue(S, "l1", yps, V(lambda hr=hr: hr[:, :], hr), V(lambda: g1[:, :], g1), lng, lnb,
                        V(lambda ou=ou: ou[:, :], ou), V(lambda tt=tt: tt[:, :], tt), (st6[ti % 2], mv[ti % 2], rs[ti % 2]))
            S.dma("sp", [(out.ap()[ti * 128:(ti + 1) * 128, :], ou[:])], R=[ou], owner=ou)
        S.barrier()
        S.flush()


def build(stage="full"):
    P = Prog(stage)
    nc = P.nc
    declare_inputs(P)
    I = P.ins
    zsc = P.scratch("zsc", [6, 128, TA], BF16)
    cc_in_lat = P.scratch("cc_in_lat", [16, 128, 3072], BF16)
    cc_in_ctx = P.scratch("cc_in_ctx", [384, SC], BF16)
    cc_out_lat = P.scratch("cc_out_lat", [16, 512, 3072], BF16)
    cc_out_ctx = P.scratch("cc_out_ctx", [1536, SC], BF16)
    if stage == "A":
        dbg_lat = P.out("dbg_m_lat", [384, SL], BF16)
        dbg_ctx = P.out("dbg_m_ctx", [384, SC], BF16)
        dbg_z = P.out("dbg_z", [6, 128, TA], BF16)
        dbg_mod = P.out("dbg_mod", [128, 16, 2], F32)
        dbg_gbc = P.out("dbg_gbc", [128, 2, D], F32)

    with contextlib.ExitStack() as top:
        S = Sync(nc, top)
        gbc1 = P.scratch("gbc1sc", [128, D], F32)
        zscB = S.view(zsc, "zsc")
        ccinB = S.view(cc_in_lat, "ccin")
        ccincB = S.view(cc_in_ctx, "ccinc")

        with contextlib.ExitStack() as st:
            S.stack = st
            ps = [S.psum("psA1_%d" % i, [128, 512], F32) for i in range(6)]
            psTb = [S.psum("psA1T_%d" % i, [128, 512], BF16) for i in range(2)]
            ident = S.sbuf("ident", [128, 128], BF16)
            S.dma("pool", [(ident[:], I["ident"].ap())], W=[ident])
            const = adaln_setup(S, P, ps)
            mod0 = S.sbuf("mod0", [128, 16, 2], F32)
            gbc0 = S.sbuf("gbc0", [128, 2, D], F32)
            adaln(S, P, 0, const, mod0, gbc0)
            if stage == "A":
                S.dma("sp", [(dbg_mod.ap(), mod0[:])], R=[mod0])
                S.dma("sp", [(dbg_gbc.ap(), gbc0[:])], R=[gbc0])

            win = S.sbuf("win", [128, 8, 768], BF16)
            for hf in range(2):
                ws = S.sbuf("wst%d" % hf, [128, 8, 384], F32)
                S.dma("sp" if hf == 0 else "pool",
                      [(ws[:], I["win0"].ap()[:, hf * 384:(hf + 1) * 384].rearrange("(k p) c -> p k c", p=128))], W=[ws])
                S.op("dve" if hf == 0 else "pool", lambda e, ws=ws, hf=hf: e.tensor_copy(
                    out=win[:, :, hf * 384:(hf + 1) * 384], in_=ws[:]), R=[ws], W=[win])

            xin = [S.sbuf("xin%d" % i, [128, 4, D], BF16) for i in range(3)]
            uTt = [S.sbuf("uT%d" % i, [128, 8, 512], BF16).t for i in range(2)]
            uT = [[S.view(uTt[i], "uT%d_%d" % (i, k)) for k in range(8)] for i in range(2)]
            zstt = [S.sbuf("zst%d" % i, [128, 6, 512], BF16).t for i in range(2)]
            zst = [[S.view(zstt[i], "zst%d_%d" % (i, c)) for c in range(6)] for i in range(2)]
            psT = psTb
            psZ = ps[3:6]
            ntiles = 1 + SL // 512

            def tinfo(t):
                if t == 0:
                    return SC, 0, 1, I["ctx"].ap().rearrange("(j p) d -> p j d", p=128)
                return 512, SC + (t - 1) * 512, 0, I["x_all"].ap()[(t - 1) * 512:t * 512, :].rearrange("(j p) d -> p j d", p=128)

            def a1_load(t):
                ntok, tok0, j, src = tinfo(t)
                xi = xin[t % 3]
                S.dma("pool", [(xi[:, 0:ntok // 128, :], src)], W=[xi])

            def a1_T(t, k):
                ntok, tok0, j, src = tinfo(t)
                xi = xin[t % 3]
                bank = psT[k % 2]
                for jj in range(ntok // 128):
                    S.op("pe", lambda e, jj=jj: e.transpose(
                        out=bank[:, jj * 128:(jj + 1) * 128], in_=xi[:, jj, k * 128:(k + 1) * 128], identity=ident[:]),
                        R=[xi, ident], W=[bank])
                u = uT[t % 2][k]
                if k % 2 == 0:
                    S.op("act", lambda e: e.activation(
                        out=u[:, k, 0:ntok], in_=bank[:, 0:ntok], func=AF.Identity,
                        scale=mod0[:, 8 + k, j:j + 1], bias=mod0[:, k, j:j + 1]), R=[bank, mod0], W=[u])
                else:
                    S.op("dve", lambda e: e.tensor_scalar(
                        out=u[:, k, 0:ntok], in0=bank[:, 0:ntok], scalar1=mod0[:, 8 + k, j:j + 1],
                        scalar2=mod0[:, k, j:j + 1], op0=ALU.mult, op1=ALU.add), R=[bank, mod0], W=[u])

            def a1_Z(t, c):
                ntok, tok0, j, src = tinfo(t)
                zb = psZ[c % 3]
                for k in range(8):
                    u = uT[t % 2][k]
                    S.op("pe", lambda e, u=u, k=k: e.matmul(
                        zb[:, 0:ntok], lhsT=win[:, k, c * 128:(c + 1) * 128], rhs=u[:, k, 0:ntok],
                        start=(k == 0), stop=(k == 7)), R=[win, u], W=[zb])
                zs = zst[t % 2][c]
                if c < 3:
                    S.op("dve", lambda e: e.tensor_copy(out=zs[:, c, 0:ntok], in_=zb[:, 0:ntok]), R=[zb], W=[zs])
                else:
                    S.op("act", lambda e: e.activation(out=zs[:, c, 0:ntok], in_=zb[:, 0:ntok], func=AF.Silu), R=[zb], W=[zs])
                if c == 5:
                    zt = zst[t % 2][0].t
                    S.dma("sp" if t % 2 == 1 else "act",
                          [(zsc.ap()[:, :, tok0:tok0 + ntok].rearrange("c p t -> p c t"), zt[:, :, 0:ntok])],
                          R=zst[t % 2], W=[], owner=zst[t % 2][0])

            a1_load(0)
            a1_load(1)
            a1_load(2)
            for k in range(8):
                a1_T(0, k)
            for t in range(ntiles):
                for k in range(8):
                    if t + 1 < ntiles:
                        a1_T(t + 1, k)
                    if k < 6:
                        a1_Z(t, k)
                if t + 3 < ntiles:
                    a1_load(t + 3)
            S.barrier()
            S.flush()
        phase_A2(S, P, zsc, cc_in_lat, cc_in_ctx)
        phase_A3(S, P, zsc, cc_in_lat, cc_in_ctx)
        if stage == "A":
            with contextlib.ExitStack() as st:
                S.stack = st
                for c in range(3):
                    bt = S.sbuf("dbgm%d" % c, [128, SC], BF16)
                    S.dma("sp", [(bt[:], cc_in_ctx.ap()[c * 128:(c + 1) * 128, :])], W=[bt])
                    S.dma("sp", [(dbg_ctx.ap()[c * 128:(c + 1) * 128, :], bt[:])], R=[bt])
                    bl = S.sbuf("dbgl%d" % c, [128, SL], BF16)
                    S.dma("sp", [(bl[:, i * 1024:(i + 1) * 1024].rearrange("p (b t) -> p b t", t=512),
                                  cc_in_lat.ap()[i].rearrange("p (b c t) -> p b c t", b=2, c=3)[:, :, c, :]) for i in range(16)], W=[bl])
                    S.dma("sp", [(dbg_lat.ap()[c * 128:(c + 1) * 128, :], bl[:])], R=[bl])
                S.barrier()
                S.flush()
        if stage == "A":
            with contextlib.ExitStack() as st:
                S.stack = st
                for c in range(6):
                    bt = S.sbuf("dbgb%d" % c, [128, TA], BF16)
                    S.dma("sp", [(bt[:], zsc.ap()[c])], W=[bt])
                    S.dma("sp", [(dbg_z.ap()[c], bt[:])], R=[bt])
                S.barrier()
                S.flush()
        if stage == "A":
            return P
        S.stack = top
        h1sc = P.scratch("h1sc", [TOK, D], F32)
        gsc1 = P.scratch("gsc1", [8, 128, TOK], BF16)
        qsc = P.scratch("qsc", [NH, 96, TOK], BF16)
        cc2_in = (P.scratch("cc2c_in", [160, SC], BF16), P.scratch("cc2l_in", [2, 160, 2048], BF16))
        cc2_out = (P.scratch("cc2c_out", [640, SC], BF16), P.scratch("cc2l_out", [2, 640, 2048], BF16))
        osc = P.scratch("osc", [8, 128, TOK], BF16)
        outT = P.out("out", [TOK, D], F32)
        phase_B(S, P, (cc_in_lat, cc_in_ctx, cc_out_lat, cc_out_ctx), h1sc, gsc1, qsc, cc2_in, gbc1)
        if stage == "B":
            dbg_h1 = P.out("dbg_h1", [TOK, D], F32)
            dbg_q = P.out("dbg_q", [NH, 96, TOK], BF16)
            dbg_kv = P.out("dbg_kv", [160, SC + TOK], BF16)
            with contextlib.ExitStack() as st:
                S.stack = st
                bts = [S.sbuf("dbh%d" % i, [128, 4, D], F32) for i in range(2)]
                for i in range(TOK // 512):
                    bt = bts[i % 2]
                    S.dma("sp", [(bt[:], h1sc.ap()[i * 512:(i + 1) * 512, :].rearrange("(j p) d -> p j d", p=128))], W=[bt])
                    S.dma("sp", [(dbg_h1.ap()[i * 512:(i + 1) * 512, :].rearrange("(j p) d -> p j d", p=128), bt[:])], R=[bt])
                bqs = [S.sbuf("dbq%d" % i, [96, TOK], BF16) for i in range(2)]
                for h in range(NH):
                    bt = bqs[h % 2]
                    S.dma("sp", [(bt[:], qsc.ap()[h])], W=[bt])
                    S.dma("sp", [(dbg_q.ap()[h], bt[:])], R=[bt])
                for (r0, n) in ((0, 128), (128, 32)):
                    bt = S.sbuf("dbk%d" % r0, [n, SC + TOK], BF16)
                    S.dma("sp", [(bt[:, 0:SC], cc2_in[0].ap()[r0:r0 + n, :]),
                                 (bt[:, SC:SC + 2048], cc2_in[1].ap()[0, r0:r0 + n, :]),
                                 (bt[:, SC + 2048:SC + 4096], cc2_in[1].ap()[1, r0:r0 + n, :])], W=[bt])
                    S.dma("sp", [(dbg_kv.ap()[r0:r0 + n, :], bt[:])], R=[bt])
                S.barrier()
                S.flush()
            return P
        phase_C(S, P, cc2_in, cc2_out, qsc, gsc1, osc)
        phase_D(S, P, osc, h1sc, gbc1, outT)
        return P


def _fm(v, nchunk):
    return np.ascontiguousarray(np.asarray(v, np.float32).reshape(nchunk, 128).T)


def _consts():
    n = np.arange(128)
    ang = 2.0 * np.pi * np.outer(n, n) / 128.0
    c128 = np.cos(ang).astype(np.float32)
    s128 = np.sin(ang).astype(np.float32)
    angt = 2.0 * np.pi * np.outer(n, n) / 16384.0
    tr = np.cos(angt).astype(np.float32)
    sn = np.sin(angt).astype(np.float32)
    m = np.arange(256)
    ang256 = 2.0 * np.pi * np.outer(m, m) / 256.0
    c256 = np.cos(ang256).astype(np.float32).reshape(2, 128, 256).transpose(1, 0, 2)
    s256 = np.sin(ang256).astype(np.float32).reshape(2, 128, 256).transpose(1, 0, 2)
    return dict(
        ident=np.eye(128, dtype=np.float32), c128=c128, s128=s128, s128n=-s128,
        cs1=np.concatenate([c128, -s128], 1), cs2=np.concatenate([s128, c128], 1),
        tt1=np.concatenate([tr, tr], 1), tt2=np.concatenate([sn, -sn], 1),
        c256=np.ascontiguousarray(c256), s256=np.ascontiguousarray(s256))


def _rope_tables(tc):
    t = np.arange(TOK) + TOK * tc
    inv = (10000.0 ** (-np.arange(8, dtype=np.float32) / 8.0)).astype(np.float32)
    row = (t // 64).astype(np.float32)
    col = (t % 64).astype(np.float32)
    ar = row[None, :] * inv[:, None]
    ac = col[None, :] * inv[:, None]
    cosf = np.concatenate([np.cos(ar), np.cos(ar), np.cos(ac), np.cos(ac)], 0)
    sinf = np.concatenate([-np.sin(ar), np.sin(ar), -np.sin(ac), np.sin(ac)], 0)
    c = np.zeros((96, TOK), np.float32)
    s = np.zeros((96, TOK), np.float32)
    c[64:96] = cosf
    s[64:96] = sinf
    return c, s


_ROPE_PERM = np.concatenate([np.arange(8, 16), np.arange(0, 8), np.arange(24, 32), np.arange(16, 24)])


def prep_inputs(inp):
    f = lambda a: np.ascontiguousarray(np.asarray(a, np.float32))
    x, c, ctx, c_ctx = f(inp["x"]), f(inp["c"]), f(inp["ctx"]), f(inp["c_ctx"])
    ada_w, ada_b = f(inp["ada_w"]), f(inp["ada_b"])
    w_in_rf = f(inp["w_in_rf"])[0]
    conv_w, conv_b = f(inp["conv_w"])[0], f(inp["conv_b"])[0]
    gw, gb, lam = f(inp["lru_gate_w"])[0], f(inp["lru_gate_b"])[0], f(inp["lru_lambda"])[0]
    fnw, fnb = f(inp["fnet_w"])[0], f(inp["fnet_b"])[0]
    w_out_rf = f(inp["w_out_rf"])[0]
    w_in_mla = f(inp["w_in_mla"])[0]
    qng, kvg = f(inp["q_norm_g"])[0], f(inp["kv_norm_g"])[0]
    w_uq, w_ukv, w_out_mla = f(inp["w_uq"])[0], f(inp["w_ukv"])[0], f(inp["w_out_mla"])[0]
    cs = _consts()
    ada_bf = np.ascontiguousarray(ada_b.reshape(2, 24, 128).transpose(2, 0, 1))
    ada_bg = np.ascontiguousarray(ada_b[:, 2048:3072])
    perm_rows = np.concatenate([np.concatenate([np.arange(256 * r, 256 * r + 256),
                                                1024 + np.arange(128 * r, 128 * r + 128)]) for r in range(4)])
    wout0 = np.ascontiguousarray(w_out_rf[perm_rows])
    kr_cols = 384 + _ROPE_PERM
    wkr = np.ascontiguousarray(np.concatenate([w_in_mla[:, 256:320], w_in_mla[:, 384:416]], 1))
    wkrp = np.ascontiguousarray(np.concatenate([w_in_mla[:, 256:320], w_in_mla[:, kr_cols]], 1))
    qperm = np.arange(1536).reshape(16, 96)
    qperm[:, 64:96] = qperm[:, 64:96][:, _ROPE_PERM]
    wuqp = np.ascontiguousarray(w_uq[:, qperm.reshape(-1)])
    maps = []
    for core in range(8):
        b, g = core // 4, core % 4
        cols = np.concatenate([np.arange(256 * g, 256 * g + 256), 1024 + np.arange(128 * g, 128 * g + 128),
                               1536 + np.arange(256 * g, 256 * g + 256), 2560 + np.arange(128 * g, 128 * g + 128)])
        ch = np.arange(256 * g, 256 * g + 256)
        rc, rs = _rope_tables(g)
        idx = np.zeros((32, 128, 1), np.int32)
        for blk in range(8):
            for r in range(4):
                gblk = g * 8 + blk
                idx[blk * 4 + r, :, 0] = ((gblk // 2) * 512 + r * 128 + np.arange(128)) * 2 + gblk % 2
        m = dict(
            x_all=x[b], x_own=np.ascontiguousarray(x[b, TOK * g:TOK * (g + 1)]), ctx=ctx[b],
            cvec=np.ascontiguousarray(np.stack([c[b], c_ctx], -1).reshape(8, 128, 2).transpose(1, 0, 2)),
            ada_w=ada_w, ada_bf=ada_bf, ada_bg=ada_bg, ln_g=f(inp["ln_g"]), ln_b=f(inp["ln_b"]),
            win0=np.ascontiguousarray(w_in_rf[:, cols]),
            convw=np.ascontiguousarray(conv_w[:, ch].reshape(4, 2, 128).transpose(2, 1, 0)),
            convb=_fm(conv_b[ch], 2),
            gatew=np.ascontiguousarray(gw[:, :, 4 * g:4 * g + 4]),
            gateb=np.ascontiguousarray(gb[:, :, ch].reshape(2, 2, 2, 128).transpose(3, 0, 1, 2)),
            lam=np.ascontiguousarray(lam[:, ch].reshape(2, 2, 128).transpose(2, 0, 1)),
            fw=fnw[g], fb=np.ascontiguousarray(fnb[128 * g:128 * g + 128, None]),
            wout0=wout0, idx=np.ascontiguousarray(idx),
            win1=w_in_mla, qng=_fm(qng, 2), kvg=_fm(kvg, 1),
            wuq=w_uq, wuqp=wuqp, wukv=w_ukv, wout1=w_out_mla,
            ropec=rc.astype(ml_dtypes.bfloat16), ropes=rs.astype(ml_dtypes.bfloat16), wkr=wkr, wkrp=wkrp, **cs)
        maps.append(m)
    return maps


_CACHE = {}


def kernel(**inputs):
    maps = prep_inputs(inputs)
    if "full" not in _CACHE:
        _CACHE["full"] = build("full")
    P = _CACHE["full"]
    res = run_bass_kernel_spmd(P.nc, maps, core_ids=list(range(8)))
    out = np.zeros((2, SL, D), np.float32)
    for core in range(8):
        b, g = core // 4, core % 4
        out[b, TOK * g:TOK * (g + 1)] = res.results[core]["out"]
    return out
```

```python
import contextlib
import numpy as np
import ml_dtypes
import concourse.bass as bass
import concourse.mybir as mybir
from concourse.bass_utils import run_bass_kernel_spmd

F32 = mybir.dt.float32
BF16 = mybir.dt.bfloat16
I32 = mybir.dt.int32
AF = mybir.ActivationFunctionType
ALU = mybir.AluOpType

D = 1024
SL = 16384
SC = 256
TA = SL + SC
TOK = 4096
NH = 16
ALPHA = 4.0 ** 0.25
LN_EPS = 1e-6
RMS_EPS = 1e-6
ATTN_SCALE = 96.0 ** -0.5


class Buf:
    __slots__ = ("t", "lw", "rd", "dsem", "dcnt", "name")

    def __init__(self, t, name=""):
        self.t = t
        self.lw = {}
        self.rd = {}
        self.dsem = None
        self.dcnt = 0
        self.name = name

    def __getitem__(self, idx):
        return self.t[idx]


class Sync:
    ENG = ("pe", "act", "dve", "pool", "sp")
    NPOOL = 64

    def __init__(self, nc, stack, same_engine_wait=True):
        self.nc = nc
        self.stack = stack
        self.sems = {}
        self.cnt = {}
        for e in ("pe", "act", "dve", "pool"):
            self.sems[e] = stack.enter_context(nc.semaphore("c_" + e))
            self.cnt[e] = 0
        self.free = {"hw": [], "sw": []}
        for i in range(self.NPOOL):
            k = "d%d" % i
            self.sems[k] = stack.enter_context(nc.semaphore(k))
            self.cnt[k] = 0
            self.free["hw" if i < 38 else "sw"].append(k)
        self.known = {e: {} for e in self.ENG}
        self.same = same_engine_wait
        self.nwaits = 0
        self.nins = 0
        self.nalloc = 0
        self.owners = []
        self.group = None
        self.prog = {e: [] for e in self.ENG}

    def sbuf(self, name, shape, dt):
        self.nalloc += 1
        t = self.stack.enter_context(self.nc.sbuf_tensor("s%d_%s" % (self.nalloc, name), list(shape), dt))
        return Buf(t, name)

    def psum(self, name, shape, dt):
        self.nalloc += 1
        t = self.stack.enter_context(self.nc.psum_tensor("p%d_%s" % (self.nalloc, name), list(shape), dt))
        return Buf(t, name)

    def view(self, t, name=""):
        return Buf(t, name)

    def _dsem(self, b, q):
        kind = "sw" if q == "pool" else "hw"
        if b.dsem is None:
            b.dsem = self.free[kind].pop()
            self.owners.append(b)
        elif (int(b.dsem[1:]) >= 38) != (kind == "sw"):
            raise RuntimeError("buffer %s mixes software- and hardware-DGE DMAs on one semaphore" % b.name)
        return b.dsem

    def release(self):
        for b in self.owners:
            self.free["sw" if int(b.dsem[1:]) >= 38 else "hw"].append(b.dsem)
            b.dsem = None
        self.owners = []

    def group_begin(self):
        self.group = (Buf(None, "grp"), [])

    def group_end(self):
        g, bufs = self.group
        self.group = None
        if g.dsem is None:
            return
        k = g.dsem
        for b in bufs:
            b.lw = {k: self.cnt[k]}

    def _waits(self, q, R, W):
        need = {}
        for b in R:
            for k, v in b.lw.items():
                if need.get(k, 0) < v:
                    need[k] = v
        for b in W:
            for d in (b.lw, b.rd):
                for k, v in d.items():
                    if need.get(k, 0) < v:
                        need[k] = v
        kn = self.known[q]
        for k, v in need.items():
            if k == q and (q == "pe" or not self.same):
                continue
            if kn.get(k, 0) >= v:
                continue
            self.prog[q].append(("w", self.sems[k], v))
            kn[k] = v
            self.nwaits += 1

    def _post(self, ev, R, W):
        k, v = ev
        for b in W:
            b.lw = {k: v}
            b.rd = {}
        for b in R:
            if b not in W:
                b.rd[k] = v

    def op(self, q, fn, R=(), W=()):
        self._waits(q, R, W)
        self.cnt[q] += 1
        self.prog[q].append(("i", fn, self.sems[q], 1))
        self._post((q, self.cnt[q]), R, W)
        self.nins += 1

    def dma(self, q, pairs, R=(), W=(), owner=None):
        if self.group is not None and owner is None:
            owner = self.group[0]
            self.group[1].extend(W)
        if owner is None:
            owner = W[0] if W else R[0]
        k = self._dsem(owner, q)
        self._waits(q, R, W)
        for (o, i) in pairs:
            self.prog[q].append(("i", (lambda e, o=o, i=i: e.dma_start(out=o, in_=i)), self.sems[k], 16))
            self.cnt[k] += 16
        self._post((k, self.cnt[k]), R, W)
        self.nins += len(pairs)

    def special(self, q, fn, inc, R=(), W=(), owner=None):
        if owner is None:
            owner = W[0]
        k = self._dsem(owner, q)
        self._waits(q, R, W)
        self.prog[q].append(("i", fn, self.sems[k], inc))
        self.cnt[k] += inc
        self._post((k, self.cnt[k]), R, W)
        self.nins += 1

    def barrier(self):
        for q in self.ENG:
            kn = self.known[q]
            for k, v in self.cnt.items():
                if v and kn.get(k, 0) < v:
                    self.prog[q].append(("w", self.sems[k], v))
                    kn[k] = v
                    self.nwaits += 1
        self.release()

    def flush(self):
        prog = self.prog
        self.prog = {e: [] for e in self.ENG}

        def run(eng, items):
            for it in items:
                if it[0] == "w":
                    eng.wait_ge(it[1], it[2])
                else:
                    it[1](eng).then_inc(it[2], it[3])

        with self.nc.Block() as block:
            @block.tensor
            def _(e):
                run(e, prog["pe"])

            @block.scalar
            def _(e):
                run(e, prog["act"])

            @block.vector
            def _(e):
                run(e, prog["dve"])

            @block.gpsimd
            def _(e):
                run(e, prog["pool"])

            @block.sync
            def _(e):
                run(e, prog["sp"])


def bcast_rows(dram_ap_1d_tensor, offset, n, parts=128):
    return bass.AP(dram_ap_1d_tensor, offset, [[0, parts], [1, n]])


class Prog:
    def __init__(self, stage="full"):
        self.stage = stage
        self.nc = bass.Bass("TRN2", target_bir_lowering=False)
        self.ins = {}
        self.outs = {}

    def inp(self, name, shape, dt=F32):
        t = self.nc.dram_tensor(name, list(shape), dt, kind="ExternalInput")
        self.ins[name] = t
        return t

    def out(self, name, shape, dt=F32):
        t = self.nc.dram_tensor(name, list(shape), dt, kind="ExternalOutput")
        self.outs[name] = t
        return t

    def scratch(self, name, shape, dt):
        return self.nc.dram_tensor(name, list(shape), dt)


def declare_inputs(P):
    i = P.inp
    i("x_all", [SL, D]); i("x_own", [TOK, D]); i("ctx", [SC, D])
    i("cvec", [128, 8, 2])
    i("ada_w", [2, D, 3 * D]); i("ada_bf", [128, 2, 24]); i("ada_bg", [2, D])
    i("ln_g", [2, D]); i("ln_b", [2, D])
    i("win0", [D, 768]); i("convw", [128, 2, 4]); i("convb", [128, 2])
    i("gatew", [2, 2, 4, 64, 64]); i("gateb", [128, 2, 2, 2]); i("lam", [128, 2, 2])
    i("fw", [128, 128]); i("fb", [128, 1])
    i("ident", [128, 128]); i("c128", [128, 128]); i("s128", [128, 128]); i("s128n", [128, 128])
    i("cs1", [128, 256]); i("cs2", [128, 256]); i("tt1", [128, 256]); i("tt2", [128, 256])
    i("c256", [128, 2, 256]); i("s256", [128, 2, 256])
    i("wout0", [1536, D])
    i("idx", [32, 128, 1], I32)
    i("win1", [D, 1440]); i("wkr", [D, 96]); i("wkrp", [D, 96])
    i("qng", [128, 2]); i("kvg", [128, 1])
    i("wuq", [256, 1536]); i("wuqp", [256, 1536]); i("wukv", [128, 2048]); i("wout1", [D, D])
    i("ropec", [96, TOK], BF16); i("ropes", [96, TOK], BF16)


def adaln(S, P, layer, const, mod, gbc, q2="pool"):
    ada_w = P.ins["ada_w"]
    abg = const["abg"]
    S.dma("sp", [(abg[:], bcast_rows(P.ins["ada_bg"], layer * D, D))], W=[abg])
    psm = const["ps"][0]
    wb = const["adawblk"]
    scf, scbc, abf = const["scf"], const["scbc"], const["ada_bf"]
    for cb in range(12):
        w = wb[cb % 2]
        src = ada_w.ap()[layer, :, cb * 256:(cb + 1) * 256].rearrange("(k p) c -> p k c", p=128)
        S.dma("sp" if cb % 2 == 0 else q2, [(w[:], src)], W=[w])
        if cb < 8:
            for fi in range(2):
                fc = cb * 2 + fi
                for k in range(8):
                    S.op("pe", lambda e, w=w, k=k, fi=fi, fc=fc: e.matmul(
                        psm[:, fc * 2:fc * 2 + 2], lhsT=w[:, k, fi * 128:(fi + 1) * 128], rhs=scf[:, k, :],
                        start=(k == 0), stop=(k == 7)), R=[w, scf], W=[psm])
        else:
            for j in range(2):
                pg = const["ps"][1 + j]
                for k in range(8):
                    S.op("pe", lambda e, w=w, k=k, j=j, pg=pg: e.matmul(
                        pg[:, 0:256], lhsT=scbc[:, k, j, :], rhs=w[:, k, :], start=(k == 0), stop=(k == 7)),
                        R=[w, scbc], W=[pg])
                c0 = (cb - 8) * 256
                S.op("dve", lambda e, j=j, pg=pg, c0=c0: e.tensor_tensor(
                    out=gbc[:, j, c0:c0 + 256], in0=pg[:, 0:256], in1=abg[:, c0:c0 + 256], op=ALU.add),
                    R=[pg, abg], W=[gbc])
        if cb == 7:
            S.op("dve", lambda e: e.tensor_tensor(
                out=mod[:, :, :], in0=psm[:, 0:32].rearrange("p (f j) -> p f j", j=2),
                in1=abf[:, layer, 0:16].unsqueeze(2).to_broadcast([128, 16, 2]), op=ALU.add),
                R=[psm, abf], W=[mod])
            S.op("dve", lambda e: e.tensor_scalar(
                out=mod[:, 8:16, :], in0=mod[:, 8:16, :], scalar1=1.0, scalar2=None, op0=ALU.add),
                R=[mod], W=[mod])


def adaln_setup(S, P, ps):
    I = P.ins
    cv = S.sbuf("cv", [128, 8, 2], F32)
    S.dma("sp", [(cv[:], I["cvec"].ap())], W=[cv])
    scf = S.sbuf("scf", [128, 8, 2], F32)
    S.op("act", lambda e: e.activation(out=scf[:], in_=cv[:], func=AF.Silu), R=[cv], W=[scf])
    scbc = S.sbuf("scbc", [128, 8, 2, 128], F32)
    S.op("dve", lambda e: e.tensor_copy(
        out=scbc[:].rearrange("p k j m -> p (k j) m"),
        in_=scf[:].rearrange("p k j -> p (k j)").unsqueeze(2).to_broadcast([128, 16, 128])), R=[scf], W=[scbc])
    abf = S.sbuf("abf", [128, 2, 24], F32)
    S.dma("sp", [(abf[:], I["ada_bf"].ap())], W=[abf])
    abg = S.sbuf("abg", [128, D], F32)
    return dict(ps=ps, scf=scf, scbc=scbc, ada_bf=abf, abg=abg,
                adawblk=[S.sbuf("adaw%d" % i, [128, 8, 256], F32) for i in range(2)])


def phase_A2(S, P, zsc, cc_in_lat, cc_in_ctx):
    I = P.ins
    TS = 1024
    with contextlib.ExitStack() as st:
        S.stack = st
        ps = [S.psum("psA2_%d" % i, [128, 512], F32) for i in range(8)]
        cw = S.sbuf("cw", [128, 2, 4], F32); S.dma("sp", [(cw[:], I["convw"].ap())], W=[cw])
        cb = S.sbuf("cb", [128, 2], F32); S.dma("sp", [(cb[:], I["convb"].ap())], W=[cb])
        gb = S.sbuf("gb", [128, 2, 2, 2], F32); S.dma("sp", [(gb[:], I["gateb"].ap())], W=[gb])
        lam = S.sbuf("lam", [128, 2, 2], F32); S.dma("sp", [(lam[:], I["lam"].ap())], W=[lam])
        identf = S.sbuf("identf", [128, 128], F32); S.dma("sp", [(identf[:], I["ident"].ap())], W=[identf])
        sp_ = S.sbuf("sp_", [128, 2, 2], F32)
        S.op("act", lambda e: e.activation(out=sp_[:], in_=lam[:], func=AF.Exp, scale=-1.0), R=[lam], W=[sp_])
        S.op("act", lambda e: e.activation(out=sp_[:], in_=sp_[:], func=AF.Ln, bias=1.0, scale=1.0), R=[sp_], W=[sp_])
        sc8 = S.sbuf("sc8", [128, 2, 2], F32)
        sc16 = S.sbuf("sc16", [128, 2, 2], F32)
        S.op("dve", lambda e: e.tensor_scalar(out=sc8[:], in0=sp_[:], scalar1=-8.0, scalar2=None, op0=ALU.mult), R=[sp_], W=[sc8])
        S.op("dve", lambda e: e.tensor_scalar(out=sc16[:], in0=sp_[:], scalar1=-16.0, scalar2=None, op0=ALU.mult), R=[sp_], W=[sc16])
        gwf = S.sbuf("gwf", [128, 8, 128], F32)
        S.op("pool", lambda e: e.memset(gwf[:], 0.0), W=[gwf])
        pairs = []
        for c in range(2):
            for d in range(2):
                for kd in range(2):
                    for hh in range(2):
                        pairs.append((gwf[hh * 64:(hh + 1) * 64, (c * 2 + d) * 2 + kd, hh * 64:(hh + 1) * 64],
                                      I["gatew"].ap()[d, kd, 2 * c + hh]))
        S.dma("sp", pairs, W=[gwf])
        gw = S.sbuf("gw", [128, 8, 128], BF16)
        S.op("dve", lambda e: e.tensor_copy(out=gw[:], in_=gwf[:]), R=[gwf], W=[gw])
        dg = S.sbuf("dg", [128, 8, 128], BF16)
        for c in range(2):
            for k in range(4):
                S.op("dve", lambda e, c=c, k=k: e.tensor_scalar(
                    out=dg[:, c * 4 + k, :], in0=identf[:], scalar1=cw[:, c, k:k + 1], scalar2=None, op0=ALU.mult),
                    R=[identf, cw], W=[dg])
        XW = TA + 8
        xv = S.sbuf("xv", [128, XW], BF16)
        S.op("pool", lambda e: e.memset(xv[:], 0.0), W=[xv])
        xl_t = S.sbuf("xl", [128, TA], BF16).t
        R_t = S.sbuf("Rr", [128, TA], F32).t
        tiles = [(0, SC)] + [(SC + i * TS, TS) for i in range(SL // TS)]
        xlB = [S.view(xl_t, "xl%d" % i) for i in range(len(tiles))]
        RB = [S.view(R_t, "R%d" % i) for i in range(len(tiles))]
        tmp = {}
        for nm in ("r", "i", "a", "a2", "h"):
            tmp[nm] = [S.sbuf("t_%s%d" % (nm, i), [128, TS], F32) for i in range(3)]
        gt = [S.sbuf("gt%d" % i, [128, TS], BF16) for i in range(2)]
        mo = [S.sbuf("mo%d" % i, [128, TS], BF16) for i in range(2)]
        carry = S.sbuf("carry", [128, 1], F32)
        psR = [S.view(ps[0].t, "psR0"), S.view(ps[2].t, "psR1")]
        for c in range(2):
            S.dma("sp", [(xv[:, 1:1 + SC], zsc.ap()[c, :, 0:SC]), (xv[:, 260:260 + SL], zsc.ap()[c, :, SC:TA])], W=[xv])
            for ti, (tok0, ntok) in enumerate(tiles):
                base = tok0 if ti == 0 else 259 + (tok0 - SC)
                for h0 in range(0, ntok, 512):
                    n = min(512, ntok - h0)
                    pb = ps[4 + (h0 // 512) % 2]
                    for k in range(4):
                        S.op("pe", lambda e, pb=pb, k=k, c=c, n=n, o=base + h0 + k: e.matmul(
                            pb[:, 0:n], lhsT=dg[:, c * 4 + k, :], rhs=xv[:, o:o + n], start=(k == 0), stop=(k == 3)),
                            R=[dg, xv], W=[pb])
                    S.op("act", lambda e, pb=pb, n=n, c=c, o=tok0 + h0: e.activation(
                        out=xl_t[:, o:o + n], in_=pb[:, 0:n], func=AF.Identity, bias=cb[:, c:c + 1], scale=1.0),
                        R=[pb, cb], W=[xlB[ti]])
            def tile_gen(d, n_i, ti):
                tok0, ntok = tiles[ti]
                pi = n_i % 2
                pr, pim = ps[pi * 4:pi * 4 + 2], ps[pi * 4 + 2:pi * 4 + 4]
                for h0 in range(0, ntok, 512):
                    n = min(512, ntok - h0)
                    for kd, pp in ((0, pr), (1, pim)):
                        pb = pp[h0 // 512]
                        S.op("pe", lambda e, pb=pb, n=n, kd=kd, c=c, d=d, o=tok0 + h0: e.matmul(
                            pb[:, 0:n], lhsT=gw[:, (c * 2 + d) * 2 + kd, :], rhs=xl_t[:, o:o + n], start=True, stop=True),
                            R=[gw, xlB[ti]], W=[pb])
                r_, i_, a_, a2_, h_ = (tmp[k][n_i % 3] for k in ("r", "i", "a", "a2", "h"))
                s_, bx_, b_ = a2_, i_, i_
                for h0 in range(0, ntok, 512):
                    n = min(512, ntok - h0)
                    S.op("act", lambda e, n=n, h0=h0, pb=pr[h0 // 512], r_=r_, d=d, c=c: e.activation(
                        out=r_[:, h0:h0 + n], in_=pb[:, 0:n], func=AF.Sigmoid, bias=gb[:, d, 0, c:c + 1], scale=1.0),
                        R=[pr[h0 // 512], gb], W=[r_])
                    S.op("act", lambda e, n=n, h0=h0, pb=pim[h0 // 512], i_=i_, d=d, c=c: e.activation(
                        out=i_[:, h0:h0 + n], in_=pb[:, 0:n], func=AF.Sigmoid, bias=gb[:, d, 1, c:c + 1], scale=1.0),
                        R=[pim[h0 // 512], gb], W=[i_])
                yield
                S.op("act", lambda e, a_=a_, r_=r_, ntok=ntok, d=d, c=c: e.activation(
                    out=a_[:, 0:ntok], in_=r_[:, 0:ntok], func=AF.Exp, scale=sc8[:, d, c:c + 1]), R=[r_, sc8], W=[a_])
                S.op("act", lambda e, a2_=a2_, r_=r_, ntok=ntok, d=d, c=c: e.activation(
                    out=a2_[:, 0:ntok], in_=r_[:, 0:ntok], func=AF.Exp, scale=sc16[:, d, c:c + 1]), R=[r_, sc16], W=[a2_])
                yield
                S.op("act", lambda e, s_=s_, a2_=a2_, ntok=ntok: e.activation(
                    out=s_[:, 0:ntok], in_=a2_[:, 0:ntok], func=AF.Sqrt, bias=1.0, scale=-1.0), R=[a2_], W=[s_])
                yield
                S.op("pool", lambda e, bx_=bx_, i_=i_, ntok=ntok, tok0=tok0: e.tensor_tensor(
                    out=bx_[:, 0:ntok], in0=i_[:, 0:ntok], in1=xl_t[:, tok0:tok0 + ntok], op=ALU.mult),
                    R=[i_, xlB[ti]], W=[bx_])
                S.op("dve", lambda e, b_=b_, bx_=bx_, s_=s_, ntok=ntok: e.tensor_tensor(
                    out=b_[:, 0:ntok], in0=bx_[:, 0:ntok], in1=s_[:, 0:ntok], op=ALU.mult), R=[bx_, s_], W=[b_])
                init = 0.0 if n_i == 0 else carry[:, 0:1]
                if d == 0:
                    S.op("dve", lambda e, a_=a_, b_=b_, ntok=ntok, tok0=tok0, init=init: e.tensor_tensor_scan(
                        out=R_t[:, tok0:tok0 + ntok], data0=a_[:, 0:ntok], data1=b_[:, 0:ntok], initial=init,
                        op0=ALU.mult, op1=ALU.add), R=[a_, b_, carry], W=[RB[ti]])
                    S.op("dve", lambda e, o=tok0 + ntok - 1: e.tensor_copy(out=carry[:], in_=R_t[:, o:o + 1]),
                         R=[RB[ti]], W=[carry])
                else:
                    S.op("dve", lambda e, a_=a_, b_=b_, h_=h_, ntok=ntok, init=init: e.tensor_tensor_scan(
                        out=h_[:, 0:ntok][:, ::-1], data0=a_[:, 0:ntok][:, ::-1], data1=b_[:, 0:ntok][:, ::-1], initial=init,
                        op0=ALU.mult, op1=ALU.add), R=[a_, b_, carry], W=[h_])
                    S.op("dve", lambda e, h_=h_: e.tensor_copy(out=carry[:], in_=h_[:, 0:1]), R=[h_], W=[carry])
                    g_ = gt[pi]
                    S.dma("sp", [(g_[:, 0:ntok], zsc.ap()[3 + c, :, tok0:tok0 + ntok])], W=[g_])
                    S.op("pool", lambda e, h_=h_, ntok=ntok, tok0=tok0: e.tensor_tensor(
                        out=h_[:, 0:ntok], in0=h_[:, 0:ntok], in1=R_t[:, tok0:tok0 + ntok], op=ALU.add),
                        R=[h_, RB[ti]], W=[h_])
                    m_ = mo[pi]
                    S.op("pool", lambda e, h_=h_, m_=m_, g_=g_, ntok=ntok: e.tensor_tensor(
                        out=m_[:, 0:ntok], in0=h_[:, 0:ntok], in1=g_[:, 0:ntok], op=ALU.mult), R=[h_, g_], W=[m_])
                    if ti == 0:
                        dst = cc_in_ctx.ap()[c * 128:(c + 1) * 128, :]
                    else:
                        dst = cc_in_lat.ap()[ti - 1].rearrange("p (b c t) -> p b c t", b=2, c=3)[:, :, c, :]
                    src_ = m_[:, 0:ntok] if ti == 0 else m_[:, 0:ntok].rearrange("p (b t) -> p b t", t=512)
                    S.dma("sp", [(dst, src_)], R=[m_], owner=m_)

            for d in range(2):
                order = list(range(len(tiles)))
                if d == 1:
                    order = [0] + order[:0:-1]
                k_ = 0
                while k_ < len(order):
                    gens = [tile_gen(d, k_ + j_, order[k_ + j_]) for j_ in range(min(2, len(order) - k_))]
                    for stage_ in range(4):
                        for g_ in gens:
                            next(g_, None)
                    k_ += len(gens)
        S.barrier()
        S.flush()


def phase_A3(S, P, zsc, cc_in_lat, cc_in_ctx):
    I = P.ins
    NL = 1.0 / np.sqrt(float(SL) * 128.0)
    NC_ = 1.0 / np.sqrt(float(SC) * 128.0)
    with contextlib.ExitStack() as st:
        S.stack = st
        ps = [S.psum("psA3_%d" % i, [128, 512], F32) for i in range(8)]
        def ld(name, shape, src, dt=F32):
            t = S.sbuf(name, shape, dt)
            S.dma("sp", [(t[:], src)], W=[t])
            return t
        stg = S.sbuf("stg", [128, 512], F32)
        def ldbf(name, shape, src):
            n = int(np.prod(shape[1:]))
            fv = stg[:, 0:n]
            if len(shape) == 3:
                fv = fv.rearrange("p (a b) -> p a b", a=shape[1])
            S.dma("sp", [(fv, src)], W=[stg])
            b = S.sbuf(name, shape, BF16)
            S.op("dve", lambda e: e.tensor_copy(out=b[:], in_=fv), R=[stg], W=[b])
            return b
        c128f = ld("c128f", [128, 128], I["c128"].ap())
        s128nf = ld("s128nf", [128, 128], I["s128n"].ap())
        fwf = ld("fwf", [128, 128], I["fw"].ap())
        fb = ld("fbb", [128, 1], I["fb"].ap())
        c128 = ldbf("c128b", [128, 128], I["c128"].ap())
        s128 = ldbf("s128b", [128, 128], I["s128"].ap())
        cs1 = ldbf("cs1", [128, 256], I["cs1"].ap())
        cs2 = ldbf("cs2", [128, 256], I["cs2"].ap())
        tt1 = ld("tt1", [128, 256], I["tt1"].ap())
        tt2 = ld("tt2", [128, 256], I["tt2"].ap())
        c256 = ldbf("c256", [128, 2, 256], I["c256"].ap())
        s256 = ldbf("s256", [128, 2, 256], I["s256"].ap())
        mcat = S.sbuf("mcat", [128, 256], BF16)
        S.op("pe", lambda e: e.matmul(ps[0][:, 0:128], lhsT=c128f[:], rhs=fwf[:], start=True, stop=True), R=[c128f, fwf], W=[ps[0]])
        S.op("pe", lambda e: e.matmul(ps[0][:, 128:256], lhsT=s128nf[:], rhs=fwf[:], start=True, stop=True), R=[s128nf, fwf], W=[ps[0]])
        S.op("dve", lambda e: e.tensor_copy(out=mcat[:], in_=ps[0][:, 0:256]), R=[ps[0]], W=[mcat])
        zf = S.sbuf("zf", [128, TA], BF16)
        S.dma("sp", [(zf[:], zsc.ap()[2])], W=[zf])
        gf = S.sbuf("gf", [128, TA], BF16)
        S.dma("sp", [(gf[:], zsc.ap()[5])], W=[gf])
        pc = S.sbuf("pc", [128, 2, 256], BF16)
        for lc in range(2):
            S.op("pe", lambda e, lc=lc: e.matmul(ps[1][:, lc * 256:(lc + 1) * 256], lhsT=zf[:, lc * 128:(lc + 1) * 128], rhs=mcat[:],
                                                 start=True, stop=True), R=[zf, mcat], W=[ps[1]])
        S.op("dve", lambda e: e.tensor_copy(out=pc[:].rearrange("p a b -> p (a b)"), in_=ps[1][:, :]), R=[ps[1]], W=[pc])
        n = 0
        for lc in range(2):
            for (half, tab) in ((0, c256), (1, s256)):
                S.op("pe", lambda e, lc=lc, half=half, tab=tab, n=n: e.matmul(
                    ps[2][:, 0:256], lhsT=pc[:, lc, half * 128:(half + 1) * 128], rhs=tab[:, lc, :],
                    start=(n == 0), stop=(n == 3)), R=[pc, tab], W=[ps[2]])
                n += 1
        tc_ = S.sbuf("tc_", [128, 256], F32)
        S.op("act", lambda e: e.activation(out=tc_[:], in_=ps[2][:, 0:256], func=AF.Identity, bias=fb[:, 0:1], scale=NC_),
             R=[ps[2], fb], W=[tc_])
        gfc = S.view(gf.t, "gfc")
        S.op("dve", lambda e: e.tensor_tensor(out=gf[:, 0:SC], in0=tc_[:], in1=gf[:, 0:SC], op=ALU.mult), R=[tc_, gf], W=[gf])
        X = S.sbuf("X", [128, 128, 128], BF16)
        Bp = S.sbuf("Bp", [128, 2, 128, 128], BF16)
        zl = zf[:, SC:TA].rearrange("p (a b) -> p a b", b=128)
        mch = S.sbuf("mch", [128, 2, 2, 64], BF16)
        S.op("dve", lambda e: e.tensor_copy(
            out=mch[:], in_=ps[0][:, 0:256].rearrange("p (ri jh jj) -> p jh ri jj", ri=2, jh=2)), R=[ps[0]], W=[mch])
        t1 = [S.sbuf("t1_%d" % i, [128, 256], F32) for i in range(2)]
        t2 = [S.sbuf("t2_%d" % i, [128, 256], F32) for i in range(2)]
        psP = ps[0:2]
        psA = ps[2:6]
        for jh in range(2):
            for l2 in range(0, 128, 4):
                pb = psP[(l2 // 4) % 2]
                for q in range(4):
                    S.op("pe", lambda e, pb=pb, q=q, l2=l2, jh=jh: e.matmul(
                        pb[:, q * 128:(q + 1) * 128], lhsT=zl[:, :, l2 + q],
                        rhs=mch[:, jh, :, :].rearrange("p a b -> p (a b)"), start=True, stop=True),
                        R=[zf, mch], W=[pb])
                if (l2 // 4) % 2 == 0:
                    S.op("act", lambda e, pb=pb, l2=l2: e.activation(
                        out=X[:, l2:l2 + 4, :].rearrange("p a b -> p (a b)"), in_=pb[:, :], func=AF.Copy), R=[pb], W=[X])
                else:
                    S.op("dve", lambda e, pb=pb, l2=l2: e.tensor_copy(
                        out=X[:, l2:l2 + 4, :].rearrange("p a b -> p (a b)"), in_=pb[:, :]), R=[pb], W=[X])
            for jj in range(64):
                j = jh * 64 + jj
                pb = psA[j % 4]
                S.op("pe", lambda e, pb=pb, jj=jj: e.matmul(pb[:, 0:256], lhsT=X[:, :, jj], rhs=cs1[:], start=True, stop=False),
                     R=[X, cs1], W=[pb])
                S.op("pe", lambda e, pb=pb, jj=jj: e.matmul(pb[:, 0:256], lhsT=X[:, :, 64 + jj], rhs=cs2[:], start=False, stop=True),
                     R=[X, cs2], W=[pb])
                a1, a2 = t1[j % 2], t2[j % 2]
                S.op("dve", lambda e, pb=pb, a1=a1: e.tensor_tensor(out=a1[:], in0=pb[:, 0:256], in1=tt1[:], op=ALU.mult),
                     R=[pb, tt1], W=[a1])
                S.op("dve", lambda e, pb=pb, a2=a2: e.tensor_tensor(
                    out=a2[:].rearrange("p (h k) -> p h k", h=2),
                    in0=pb[:, 0:256].rearrange("p (h k) -> p h k", h=2)[:, ::-1, :],
                    in1=tt2[:].rearrange("p (h k) -> p h k", h=2), op=ALU.mult), R=[pb, tt2], W=[a2])
                S.op("pool", lambda e, a1=a1, a2=a2, j=j: e.tensor_tensor(
                    out=Bp[:, :, :, j], in0=a1[:].rearrange("p (h k) -> p h k", h=2),
                    in1=a2[:].rearrange("p (h k) -> p h k", h=2), op=ALU.add), R=[a1, a2], W=[Bp])
        tb = [S.sbuf("tb%d" % i, [128, 4, 128], F32) for i in range(2)]
        psB = ps[6:8]
        gl = gf[:, SC:TA].rearrange("p (k2 k1) -> p k1 k2", k1=128)
        for k1 in range(0, 128, 4):
            pb = psB[(k1 // 4) % 2]
            for q in range(4):
                S.op("pe", lambda e, pb=pb, q=q, k1=k1: e.matmul(
                    pb[:, q * 128:(q + 1) * 128], lhsT=Bp[:, 0, k1 + q, :], rhs=c128[:], start=True, stop=False),
                    R=[Bp, c128], W=[pb])
                S.op("pe", lambda e, pb=pb, q=q, k1=k1: e.matmul(
                    pb[:, q * 128:(q + 1) * 128], lhsT=Bp[:, 1, k1 + q, :], rhs=s128[:], start=False, stop=True),
                    R=[Bp, s128], W=[pb])
            tt = tb[(k1 // 4) % 2]
            S.op("act", lambda e, pb=pb, tt=tt: e.activation(
                out=tt[:].rearrange("p a b -> p (a b)"), in_=pb[:, :], func=AF.Identity, bias=fb[:, 0:1], scale=NL),
                R=[pb, fb], W=[tt])
            S.op("dve", lambda e, tt=tt, k1=k1: e.tensor_tensor(
                out=gl[:, k1:k1 + 4, :], in0=tt[:], in1=gl[:, k1:k1 + 4, :], op=ALU.mult), R=[tt, gf], W=[gf])
        prs = [(cc_in_ctx.ap()[256:384, :], gf[:, 0:SC])]
        for i in range(16):
            prs.append((cc_in_lat.ap()[i].rearrange("p (b c t) -> p b c t", b=2, c=3)[:, :, 2, :],
                        gf[:, SC + i * 1024:SC + (i + 1) * 1024].rearrange("p (b t) -> p b t", t=512)))
        S.dma("sp", prs, R=[gf], owner=gf)
        S.barrier()
        S.flush()


def ln_epilogue(S, name, yps, resid, gate_row, lng, lnb, out_t, tmp, small):
    t = tmp
    for hf in range(2):
        S.op("dve", lambda e, hf=hf: e.tensor_tensor(out=t[:, hf * 512:(hf + 1) * 512], in0=yps[hf][:, :],
                                                     in1=gate_row[:, hf * 512:(hf + 1) * 512], op=ALU.mult),
             R=[yps[hf], gate_row.b], W=[t.b])
    S.op("dve", lambda e: e.scalar_tensor_tensor(out=t[:, :], in0=resid[:, :], scalar=ALPHA, in1=t[:, :], op0=ALU.mult, op1=ALU.add),
         R=[resid.b, t.b], W=[t.b])
    st6, mv, rs = small
    for hf in range(2):
        S.op("dve", lambda e, hf=hf: e.bn_stats(out=st6[:, hf, :], in_=t[:, hf * 512:(hf + 1) * 512]), R=[t.b], W=[st6])
    S.op("dve", lambda e: e.bn_aggr(out=mv[:, :], in_=st6[:].rearrange("p a b -> p (a b)")), R=[st6], W=[mv])
    S.op("act", lambda e: e.activation(out=rs[:, :], in_=mv[:, 1:2], func=AF.Sqrt, bias=LN_EPS, scale=1.0), R=[mv], W=[rs])
    S.op("dve", lambda e: e.reciprocal(out=rs[:, :], in_=rs[:, :]), R=[rs], W=[rs])
    S.op("dve", lambda e: e.tensor_scalar(out=t[:, :], in0=t[:, :], scalar1=mv[:, 0:1], scalar2=rs[:, 0:1],
                                          op0=ALU.subtract, op1=ALU.mult), R=[t.b, mv, rs], W=[t.b])
    S.op("pool", lambda e: e.tensor_tensor(out=t[:, :], in0=t[:, :], in1=lng[:, :], op=ALU.mult), R=[t.b, lng], W=[t.b])
    S.op("pool", lambda e: e.tensor_tensor(out=out_t[:, :], in0=t[:, :], in1=lnb[:, :], op=ALU.add), R=[t.b, lnb], W=[out_t.b])


class V:
    def __init__(self, ap_fn, b):
        self.f = ap_fn
        self.b = b

    def __getitem__(self, idx):
        return self.f()[idx]


def phase_B(S, P, cc, h1sc, gsc1, qsc, cc2_in, gbc1):
    I = P.ins
    cc_in_lat, cc_in_ctx, cc_out_lat, cc_out_ctx = cc
    ccoB = S.view(cc_out_lat, "ccol"); ccocB = S.view(cc_out_ctx, "ccoc")
    RG = [[0, 1, 2, 3], [4, 5, 6, 7]]
    S.special("pool", lambda e: e.collective_compute("AllGather", ALU.bypass, replica_groups=RG,
                                                     ins=[cc_in_ctx.ap()], outs=[cc_out_ctx.ap()]), 1, W=[ccocB])
    for i in range(16):
        S.special("pool", lambda e, i=i: e.collective_compute("AllGather", ALU.bypass, replica_groups=RG,
                                                            ins=[cc_in_lat.ap()[i]], outs=[cc_out_lat.ap()[i]]), 1, W=[ccoB])
    with contextlib.ExitStack() as stB:
        S.stack = stB
        def ld(name, shape, src, dt=F32, q="sp"):
            t = S.sbuf(name, shape, dt)
            S.dma(q, [(t[:], src)], W=[t])
            return t
        mod0 = S.sbuf("mod0B", [128, 16, 2], F32)
        gbc0 = S.sbuf("gbc0B", [128, 2, D], F32)
        mod1 = S.sbuf("mod1B", [128, 16, 2], F32)
        gbc1t = S.sbuf("gbc1B", [128, 2, D], F32)
        wo = S.sbuf("wo0", [128, 12, D], BF16)
        w1 = S.sbuf("w1", [128, 8, 1440], BF16)
        wkr = S.sbuf("wkr", [128, 8, 96], BF16)
        wkrp = S.sbuf("wkrp", [128, 8, 96], BF16)
        wuq = S.sbuf("wuq", [128, 2, 1536], BF16)
        wuqp = S.sbuf("wuqp", [128, 2, 1536], BF16)
        ident = ld("identB", [128, 128], I["ident"].ap())
        with contextlib.ExitStack() as stp:
            S.stack = stp
            psp = [S.psum("psBp_%d" % i, [128, 512], F32) for i in range(3)]
            const = adaln_setup(S, P, psp)
            adaln(S, P, 0, const, mod0, gbc0, q2="act")
            adaln(S, P, 1, const, mod1, gbc1t, q2="act")
            S.dma("sp", [(gbc1.ap(), gbc1t[:, 0, :])], R=[gbc1t], owner=gbc1t)
            stg = const["adawblk"]
            def ldw(w, nk, ncol, src2d, eng="dve"):
                for c0 in range(0, ncol, 256):
                    n = min(256, ncol - c0)
                    sg = stg[(c0 // 256) % 2]
                    S.dma("sp" if (c0 // 256) % 2 == 0 else "act",
                          [(sg[:, 0:nk, 0:n], src2d[:, c0:c0 + n].rearrange("(k p) c -> p k c", p=128))], W=[sg])
                    if eng == "act":
                        S.op("act", lambda e, w=w, sg=sg, c0=c0, n=n: e.activation(out=w[:, :, c0:c0 + n], in_=sg[:, 0:nk, 0:n], func=AF.Copy),
                             R=[sg], W=[w])
                    else:
                        S.op(eng, lambda e, w=w, sg=sg, c0=c0, n=n: e.tensor_copy(out=w[:, :, c0:c0 + n], in_=sg[:, 0:nk, 0:n]), R=[sg], W=[w])
            ldw(wo, 12, D, I["wout0"].ap()) if False else None
            for c0 in range(0, D, 256):
                for kh in range(2):
                    sg = stg[(c0 // 256 + kh) % 2]
                    S.dma("sp" if kh == 0 else "act",
                          [(sg[:, 0:6, :], I["wout0"].ap()[kh * 768:(kh + 1) * 768, c0:c0 + 256].rearrange("(k p) c -> p k c", p=128))], W=[sg])
                    if kh == 0:
                        S.op("dve", lambda e, sg=sg, c0=c0, kh=kh: e.tensor_copy(
                            out=wo[:, kh * 6:(kh + 1) * 6, c0:c0 + 256], in_=sg[:, 0:6, :]), R=[sg], W=[wo])
                    else:
                        S.op("act", lambda e, sg=sg, c0=c0, kh=kh: e.activation(
                            out=wo[:, kh * 6:(kh + 1) * 6, c0:c0 + 256], in_=sg[:, 0:6, :], func=AF.Copy), R=[sg], W=[wo])
            ldw(w1, 8, 1440, I["win1"].ap(), eng="act")
            ldw(wkr, 8, 96, I["wkr"].ap())
            ldw(wkrp, 8, 96, I["wkrp"].ap())
            ldw(wuq, 2, 1536, I["wuq"].ap(), eng="act")
            ldw(wuqp, 2, 1536, I["wuqp"].ap())
            S.barrier()
            S.flush()
        S.stack = stB
        ps = [S.psum("psB_%d" % i, [128, 512], F32) for i in range(8)]
        lng0 = ld("lng0", [128, D], bcast_rows(I["ln_g"], 0, D))
        lnb0 = ld("lnb0", [128, D], bcast_rows(I["ln_b"], 0, D), q="pool")
        qng = ld("qng", [128, 2], I["qng"].ap())
        kvg = ld("kvg", [128, 1], I["kvg"].ap())
        ropec = ld("ropec", [96, TOK], I["ropec"].ap(), dt=BF16)
        ropes = ld("ropes", [96, TOK], I["ropes"].ap(), dt=BF16, q="pool")
        ones = S.sbuf("onesB", [128, 128], BF16)
        S.op("pool", lambda e: e.memset(ones[:], 1.0), W=[ones])
        idxt = []
        S.group_begin()
        for i in range(32):
            idxt.append(ld("idx%d" % i, [128, 1], I["idx"].ap()[i], dt=I32, q="pool"))
        S.group_end()
        mblk = [S.sbuf("mblk%d" % i, [128, 12, 512], BF16) for i in range(2)]
        xres = [S.sbuf("xres%d" % i, [128, D], F32) for i in range(2)]
        h1t = [S.sbuf("h1t%d" % i, [128, D], F32) for i in range(2)]
        st6 = [S.sbuf("st6_%d" % i, [128, 2, 6], F32) for i in range(2)]
        mv = [S.sbuf("mv%d" % i, [128, 2], F32) for i in range(2)]
        rs = [S.sbuf("rs%d" % i, [128, 1], F32) for i in range(2)]
        u1t = [S.sbuf("u1T%d" % i, [128, 8, 512], BF16) for i in range(2)]
        u1B = [[S.view(u1t[i].t, "u1_%d_%d" % (i, j)) for j in range(4)] for i in range(2)]
        qcs = S.sbuf("qcs", [128, 3, 512], F32)
        sqs = S.sbuf("sqs", [128, 3, 512], BF16)
        rsq = S.sbuf("rsq", [128, 2, 512], F32)
        qnT = S.sbuf("qnT", [128, 2, 512], BF16)
        kvo = [S.sbuf("kvo%d" % i, [128, 512], BF16) for i in range(1)]
        kro = [S.sbuf("kro%d" % i, [96, 512], BF16) for i in range(1)]
        krt = S.sbuf("krt", [96, 2, 512], F32)
        gso = [S.sbuf("gso%d" % i, [128, 8, 512], BF16) for i in range(1)]
        qo = [S.sbuf("qo%d" % i, [96, 512], BF16) for i in range(2)]
        qrts = [S.sbuf("qrt%d" % i, [96, 2, 512], F32) for i in range(2)]
        rows_lat = cc_out_lat.ap().rearrange("i q (b x) -> (i q b) x", x=1536)
        psY = [ps[0:2], ps[0:2]]
        psT = ps[2:4]
        psW = ps[4:8]

        def proj_block(bi, ntok, tok0_own, is_ctx):
            u = u1t[bi % 2]
            uB = u1B[bi % 2]
            cnt = [0]
            def mm(wt, c0, ncol, pb):
                for k in range(8):
                    S.op("pe", lambda e, k=k: e.matmul(pb[0:ncol, 0:ntok], lhsT=wt[:, k, c0:c0 + ncol], rhs=u[:, k, 0:ntok],
                                                       start=(k == 0), stop=(k == 7)), R=[wt] + uB, W=[pb])
            def nextps():
                cnt[0] += 1
                return psW[cnt[0] % 4]
            pb = nextps(); mm(w1, 256, 128, pb)
            S.op("act", lambda e, pb=pb: e.activation(out=qcs[:, 2, 0:ntok], in_=pb[:, 0:ntok], func=AF.Copy), R=[pb], W=[qcs])
            S.op("act", lambda e, pb=pb: e.activation(out=sqs[:, 2, 0:ntok], in_=pb[:, 0:ntok], func=AF.Square), R=[pb], W=[sqs])
            pb = nextps()
            S.op("pe", lambda e, pb=pb: e.matmul(pb[:, 0:ntok], lhsT=ones[:], rhs=sqs[:, 2, 0:ntok], start=True, stop=True), R=[ones, sqs], W=[pb])
            S.op("act", lambda e, pb=pb: e.activation(out=rsq[:, 1, 0:ntok], in_=pb[:, 0:ntok], func=AF.Sqrt, bias=RMS_EPS, scale=1.0 / 128.0), R=[pb], W=[rsq])
            S.op("dve", lambda e: e.reciprocal(out=rsq[:, 1, 0:ntok], in_=rsq[:, 1, 0:ntok]), R=[rsq], W=[rsq])
            ko = kvo[0]
            S.op("dve", lambda e: e.scalar_tensor_tensor(out=ko[:, 0:ntok], in0=qcs[:, 2, 0:ntok], scalar=kvg[:, 0:1], in1=rsq[:, 1, 0:ntok],
                                                         op0=ALU.mult, op1=ALU.mult), R=[qcs, kvg, rsq], W=[ko])
            yield
            pb = nextps(); mm(wkr, 0, 96, pb)
            kr_ = kro[0]
            if is_ctx:
                S.op("dve", lambda e, pb=pb: e.tensor_copy(out=kr_[64:96, 0:ntok], in_=pb[64:96, 0:ntok]), R=[pb], W=[kr_])
            else:
                pb2 = nextps(); mm(wkrp, 0, 96, pb2)
                S.op("dve", lambda e, pb=pb: e.tensor_tensor(out=krt[64:96, 0, 0:ntok], in0=pb[64:96, 0:ntok],
                                                             in1=ropec[64:96, tok0_own:tok0_own + ntok], op=ALU.mult), R=[pb, ropec], W=[krt])
                S.op("dve", lambda e, pb2=pb2: e.tensor_tensor(out=krt[64:96, 1, 0:ntok], in0=pb2[64:96, 0:ntok],
                                                               in1=ropes[64:96, tok0_own:tok0_own + ntok], op=ALU.mult), R=[pb2, ropes, krt], W=[krt])
                S.op("pool", lambda e: e.tensor_tensor(out=kr_[64:96, 0:ntok], in0=krt[64:96, 0, 0:ntok], in1=krt[64:96, 1, 0:ntok], op=ALU.add),
                     R=[krt], W=[kr_])
            if is_ctx:
                dk = cc2_in[0].ap()[:, 0:ntok]
            else:
                dk = cc2_in[1].ap()[tok0_own // 2048][:, tok0_own % 2048:tok0_own % 2048 + ntok]
            S.dma("sp", [(dk[0:128, :], ko[:, 0:ntok])], R=[ko], owner=ko)
            S.dma("sp", [(dk[128:160, :], kr_[64:96, 0:ntok])], R=[kr_], owner=kr_)
            yield
            if is_ctx:
                return
            go = gso[0]
            for c in range(8):
                pb = nextps(); mm(w1, 416 + c * 128, 128, pb)
                S.op("act", lambda e, pb=pb, c=c: e.activation(out=go[:, c, 0:ntok], in_=pb[:, 0:ntok], func=AF.Silu), R=[pb], W=[go])
                yield
            S.dma("pool", [(gsc1.ap()[:, :, tok0_own:tok0_own + ntok].rearrange("c p t -> p c t"), go[:, :, 0:ntok])], R=[go], owner=go)
            for c in range(2):
                pb = nextps(); mm(w1, c * 128, 128, pb)
                S.op("act", lambda e, pb=pb, c=c: e.activation(out=qcs[:, c, 0:ntok], in_=pb[:, 0:ntok], func=AF.Copy), R=[pb], W=[qcs])
                S.op("act", lambda e, pb=pb, c=c: e.activation(out=sqs[:, c, 0:ntok], in_=pb[:, 0:ntok], func=AF.Square), R=[pb], W=[sqs])
            pb = nextps()
            for c in range(2):
                S.op("pe", lambda e, pb=pb, c=c: e.matmul(pb[:, 0:ntok], lhsT=ones[:], rhs=sqs[:, c, 0:ntok], start=(c == 0), stop=(c == 1)),
                     R=[ones, sqs], W=[pb])
            S.op("act", lambda e, pb=pb: e.activation(out=rsq[:, 0, 0:ntok], in_=pb[:, 0:ntok], func=AF.Sqrt, bias=RMS_EPS, scale=1.0 / 256.0), R=[pb], W=[rsq])
            S.op("dve", lambda e: e.reciprocal(out=rsq[:, 0, 0:ntok], in_=rsq[:, 0, 0:ntok]), R=[rsq], W=[rsq])
            for c in range(2):
                S.op("dve", lambda e, c=c: e.scalar_tensor_tensor(out=qnT[:, c, 0:ntok], in0=qcs[:, c, 0:ntok], scalar=qng[:, c:c + 1],
                                                                  in1=rsq[:, 0, 0:ntok], op0=ALU.mult, op1=ALU.mult), R=[qcs, qng, rsq], W=[qnT])
            yield
            for h in range(NH):
                pa = nextps()
                for c in range(2):
                    S.op("pe", lambda e, pa=pa, c=c, h=h: e.matmul(pa[0:96, 0:ntok], lhsT=wuq[:, c, h * 96:(h + 1) * 96], rhs=qnT[:, c, 0:ntok],
                                                                 start=(c == 0), stop=(c == 1)), R=[wuq, qnT], W=[pa])
                pp = nextps()
                for c in range(2):
                    S.op("pe", lambda e, pp=pp, c=c, h=h: e.matmul(pp[0:96, 0:ntok], lhsT=wuqp[:, c, h * 96:(h + 1) * 96], rhs=qnT[:, c, 0:ntok],
                                                                 start=(c == 0), stop=(c == 1)), R=[wuqp, qnT], W=[pp])
                q_ = qo[h % 2]
                qrt = qrts[h % 2]
                S.op("act", lambda e, pa=pa, q_=q_: e.activation(out=q_[0:64, 0:ntok], in_=pa[0:64, 0:ntok], func=AF.Copy), R=[pa], W=[q_])
                S.op("dve", lambda e, pa=pa, qrt=qrt: e.tensor_tensor(out=qrt[64:96, 0, 0:ntok], in0=pa[64:96, 0:ntok],
                                                             in1=ropec[64:96, tok0_own:tok0_own + ntok], op=ALU.mult), R=[pa, ropec], W=[qrt])
                S.op("dve", lambda e, pp=pp, qrt=qrt: e.tensor_tensor(out=qrt[64:96, 1, 0:ntok], in0=pp[64:96, 0:ntok],
                                                             in1=ropes[64:96, tok0_own:tok0_own + ntok], op=ALU.mult), R=[pp, ropes, qrt], W=[qrt])
                S.op("pool", lambda e, q_=q_, qrt=qrt: e.tensor_tensor(out=q_[64:96, 0:ntok], in0=qrt[64:96, 0, 0:ntok], in1=qrt[64:96, 1, 0:ntok], op=ALU.add),
                     R=[qrt, q_], W=[q_])
                S.dma("sp" if h % 2 == 0 else "pool", [(qsc.ap()[h, :, tok0_own:tok0_own + ntok], q_[0:96, 0:ntok])], R=[q_], owner=q_)
                yield

        mctx = S.sbuf("mctx", [128, 12, SC], BF16)
        S.dma("sp", [(mctx[:], cc_out_ctx.ap().rearrange("(k p) t -> p k t", p=128))], R=[ccocB], W=[mctx])
        ntile = 2 + TOK // 128

        def b_info(ti):
            is_ctx = ti < 2
            if is_ctx:
                return is_ctx, 1, mctx, ti * 128, I["ctx"].ap()[ti * 128:(ti + 1) * 128, :]
            li = ti - 2
            return is_ctx, 0, mblk[(li // 4) % 2], (li % 4) * 128, I["x_own"].ap()[li * 128:(li + 1) * 128, :]

        def b_gather(blk):
            mb = mblk[blk % 2]
            for r in range(4):
                S.special("pool", lambda e, r=r: e.indirect_dma_start(
                    out=mb[:, 3 * r:3 * r + 3, :].rearrange("p a b -> p (a b)"), out_offset=None, in_=rows_lat,
                    in_offset=bass.IndirectOffsetOnAxis(ap=idxt[blk * 4 + r][:, :], axis=0),
                    bounds_check=16 * 512 * 2 - 1, oob_is_err=False), 16, R=[ccoB, idxt[blk * 4 + r]], W=[mb])

        def b_y(ti):
            is_ctx, j, msrc, mcol, xsrc = b_info(ti)
            if ti == 0:
                b_gather(0)
                b_gather(1)
            if not is_ctx and (ti - 2) % 4 == 0:
                blk = (ti - 2) // 4
                if 1 <= blk and blk + 1 < TOK // 512:
                    b_gather(blk + 1)
            xr = xres[ti % 2]
            S.dma("sp", [(xr[:], xsrc)], W=[xr])
            yps = psY[ti % 2]
            for hf in range(2):
                for kc in range(12):
                    S.op("pe", lambda e, hf=hf, kc=kc: e.matmul(
                        yps[hf][:, :], lhsT=msrc[:, kc, mcol:mcol + 128], rhs=wo[:, kc, hf * 512:(hf + 1) * 512],
                        start=(kc == 0), stop=(kc == 11)), R=[msrc, wo], W=[yps[hf]])

        def b_ep(ti):
            is_ctx, j, msrc, mcol, xsrc = b_info(ti)
            xr, yps = xres[ti % 2], psY[ti % 2]
            hh = h1t[ti % 2]
            hv = V(lambda: hh[:, :], hh)
            ln_epilogue(S, "l0", yps, V(lambda: xr[:, :], xr), V(lambda: gbc0[:, j, :], gbc0), lng0, lnb0,
                        hv, hv, (st6[ti % 2], mv[ti % 2], rs[ti % 2]))
            if not is_ctx:
                S.dma("pool", [(h1sc.ap()[(ti - 2) * 128:(ti - 1) * 128, :], hh[:])], R=[hh], owner=hh)

        def b_tr(ti):
            is_ctx, j, msrc, mcol, xsrc = b_info(ti)
            hh = h1t[ti % 2]
            bi = 0 if is_ctx else 1 + (ti - 2) // 4
            sub = ti if is_ctx else (ti - 2) % 4
            u = u1t[bi % 2]
            for k in range(8):
                pb = psT[k % 2]
                S.op("pe", lambda e, pb=pb, k=k: e.transpose(out=pb[:, 0:128], in_=hh[:, k * 128:(k + 1) * 128], identity=ident[:]),
                     R=[hh, ident], W=[pb])
                S.op("act", lambda e, pb=pb, k=k: e.activation(
                    out=u[:, k, sub * 128:(sub + 1) * 128], in_=pb[:, 0:128], func=AF.Identity,
                    scale=mod1[:, 8 + k, j:j + 1], bias=mod1[:, k, j:j + 1]), R=[pb, mod1], W=[u1B[bi % 2][sub]])
            if is_ctx and ti == 1:
                pgen.append(proj_block(0, 256, 0, True))
            elif (not is_ctx) and sub == 3:
                pgen.append(proj_block(bi, 512, ((ti - 2) // 4) * 512, False))

        pgen = []

        def pump(k):
            while k > 0 and pgen:
                try:
                    next(pgen[0])
                    k -= 1
                except StopIteration:
                    pgen.pop(0)

        b_y(0)
        for ti in range(ntile):
            b_ep(ti)
            pump(3)
            if ti + 1 < ntile:
                b_y(ti + 1)
            pump(3)
            b_tr(ti)
            pump(3)
        pump(1000)
        S.barrier()
        S.flush()


def phase_C(S, P, cc2_in, cc2_out, qsc, gsc1, osc):
    I = P.ins
    NKT = TA // 128
    RG = [[0, 1, 2, 3], [4, 5, 6, 7]]
    c2o = S.view(cc2_out[0], "cc2o")
    S.special("pool", lambda e: e.collective_compute("AllGather", ALU.bypass, replica_groups=RG,
                                                     ins=[cc2_in[0].ap()], outs=[cc2_out[0].ap()]), 1, W=[c2o])
    for i in range(2):
        S.special("pool", lambda e, i=i: e.collective_compute("AllGather", ALU.bypass, replica_groups=RG,
                                                            ins=[cc2_in[1].ap()[i]], outs=[cc2_out[1].ap()[i]]), 1, W=[c2o])
    with contextlib.ExitStack() as st:
        S.stack = st
        psS = [S.psum("psS%d" % i, [128, 1024], F32) for i in range(3)]
        psO = [S.psum("psO%d" % i, [128, 512], F32) for i in range(2)]
        kvn = S.sbuf("kvn", [128, TA], BF16)
        KTt = [S.sbuf("KT%d" % i, [128, TA], BF16).t for i in range(2)]
        KTn = [S.view(KTt[i], "KTn%d" % i) for i in range(2)]
        KTr = [S.view(KTt[i], "KTr%d" % i) for i in range(2)]
        Va = [S.sbuf("Va%d" % i, [128, NKT, 128], BF16) for i in range(2)]
        qT = [S.sbuf("qT%d" % i, [128, TOK], BF16) for i in range(2)]
        pk = [(kvn[:, 0:SC], cc2_out[0].ap()[0:128, :])]
        pr = [[(KTt[i][64:96, 0:SC], cc2_out[0].ap()[128:160, :])] for i in range(2)]
        for r in range(4):
            for hf in range(2):
                c0 = SC + r * TOK + hf * 2048
                pk.append((kvn[:, c0:c0 + 2048], cc2_out[1].ap()[hf, 160 * r:160 * r + 128, :]))
                for i in range(2):
                    pr[i].append((KTt[i][64:96, c0:c0 + 2048], cc2_out[1].ap()[hf, 160 * r + 128:160 * r + 160, :]))
        S.dma("sp", pk, R=[c2o], W=[kvn])
        for i in range(2):
            S.dma("pool", pr[i], R=[c2o], W=[KTr[i]])
            S.op("pool", lambda e, i=i: e.memset(KTt[i][96:128, :], 0.0), W=[KTr[i]])
            S.op("pool", lambda e, i=i: e.memset(KTt[i][96:97, :], 1.0), W=[KTr[i]])
            S.op("pool", lambda e, i=i: e.memset(qT[i][96:128, :], 0.0), W=[qT[i]])
        S.op("pool", lambda e: e.memset(Va[0][:, :, 64:128], 1.0), W=[Va[0]])
        S.op("pool", lambda e: e.memset(Va[1][:, :, 0:64], 1.0), W=[Va[1]])
        wst = S.sbuf("wukvf", [128, 512], F32)
        wukv = S.sbuf("wukv", [128, 2048], BF16)
        for c0 in range(0, 2048, 512):
            S.dma("sp", [(wst[:], I["wukv"].ap()[:, c0:c0 + 512])], W=[wst])
            S.op("dve", lambda e, c0=c0: e.tensor_copy(out=wukv[:, c0:c0 + 512], in_=wst[:]), R=[wst], W=[wukv])
        ones = S.sbuf("onesC", [96, 128], BF16)
        S.op("pool", lambda e: e.memset(ones[:], 1.0), W=[ones])
        PT = [S.sbuf("PT%d" % i, [128, 1024], BF16) for i in range(3)]
        sq = [S.sbuf("sqC%d" % i, [96, 512], BF16) for i in range(2)]
        mx = S.sbuf("mxC", [128, 1], F32)
        qm = [S.sbuf("qmC%d" % i, [128, 1], F32) for i in range(2)]
        km = [S.sbuf("kmC%d" % i, [128, 1], F32) for i in range(2)]
        negc = [S.sbuf("negc%d" % i, [128, 1], F32) for i in range(2)]
        ot = [S.sbuf("otC%d" % i, [128, 512], F32) for i in range(2)]
        dn = [S.sbuf("dnC%d" % i, [128, 512], F32) for i in range(2)]
        gt = [S.sbuf("gtC%d" % i, [128, 512], BF16) for i in range(2)]
        oo = [S.sbuf("ooC%d" % i, [128, 512], BF16) for i in range(2)]
        cnt = [0]

        def build_units(h, split=False):
            par = h % 2
            v0 = 0 if par == 0 else 64
            S.dma("sp", [(qT[par][0:96, :], qsc.ap()[h])], W=[qT[par]])
            S.op("pool", lambda e: e.memset(qm[par][:], 0.0), W=[qm[par]])
            S.op("pool", lambda e: e.memset(km[par][:], 0.0), W=[km[par]])
            yield
            for kb in range(0, TA, 512):
                n = min(512, TA - kb)
                pb = bank[0]
                S.op("pe", lambda e, pb=pb, kb=kb, n=n: e.matmul(pb[0:64, 0:n], lhsT=wukv[:, h * 128:h * 128 + 64], rhs=kvn[:, kb:kb + n],
                                                                 start=True, stop=True), R=[wukv, kvn], W=[pb])
                S.op("act", lambda e, pb=pb, kb=kb, n=n: e.activation(out=KTt[par][0:64, kb:kb + n], in_=pb[0:64, 0:n], func=AF.Copy),
                     R=[pb], W=[KTn[par]])
                yield
            for k0 in range(0, NKT, 8):
                nk = min(8, NKT - k0)
                pb = bank[0]
                for i in range(nk):
                    S.op("pe", lambda e, pb=pb, i=i, k0=k0: e.matmul(pb[:, i * 64:(i + 1) * 64], lhsT=kvn[:, (k0 + i) * 128:(k0 + i + 1) * 128],
                                                                    rhs=wukv[:, h * 128 + 64:h * 128 + 128], start=True, stop=True),
                         R=[wukv, kvn], W=[pb])
                if (k0 // 8) % 2 == 0:
                    S.op("dve", lambda e, pb=pb, k0=k0, nk=nk: e.tensor_copy(
                        out=Va[par][:, k0:k0 + nk, v0:v0 + 64], in_=pb[:, 0:nk * 64].rearrange("p (a b) -> p a b", b=64)), R=[pb], W=[Va[par]])
                else:
                    S.op("act", lambda e, pb=pb, k0=k0, nk=nk: e.activation(
                        out=Va[par][:, k0:k0 + nk, v0:v0 + 64], in_=pb[:, 0:nk * 64].rearrange("p (a b) -> p a b", b=64), func=AF.Copy),
                        R=[pb], W=[Va[par]])
                yield
            if not split:
                for _ in norm_units(h):
                    yield

        def norm_units(h):
            par = h % 2
            work = []
            for (src, srcB, tot, acc) in ((qT[par], [qT[par]], TOK, qm[par]), (KTt[par], [KTn[par], KTr[par]], TA, km[par])):
                for b0 in range(0, tot, 512):
                    work.append((src, srcB, min(512, tot - b0), b0, acc))

            def stage_a(u):
                src, srcB, n, b0, acc = work[u]
                s_ = sq[u % 2]
                S.op("pool", lambda e: e.tensor_tensor(out=s_[:, 0:n], in0=src[0:96, b0:b0 + n], in1=src[0:96, b0:b0 + n], op=ALU.mult),
                     R=srcB, W=[s_])

            def stage_b(u):
                src, srcB, n, b0, acc = work[u]
                s_ = sq[u % 2]
                pb = bank[0]
                S.op("pe", lambda e: e.matmul(pb[:, 0:n], lhsT=ones[:], rhs=s_[:, 0:n], start=True, stop=True), R=[ones, s_], W=[pb])
                S.op("dve", lambda e: e.tensor_reduce(out=mx[:], in_=pb[:, 0:n], op=ALU.max, axis=mybir.AxisListType.X), R=[pb], W=[mx])
                S.op("dve", lambda e: e.tensor_tensor(out=acc[:], in0=acc[:], in1=mx[:], op=ALU.max), R=[acc, mx], W=[acc])

            stage_a(0)
            yield
            for u in range(len(work)):
                if u + 1 < len(work):
                    stage_a(u + 1)
                stage_b(u)
                yield
            nb = negc[par]
            S.op("dve", lambda e: e.tensor_tensor(out=nb[:], in0=qm[par][:], in1=km[par][:], op=ALU.mult), R=[qm[par], km[par]], W=[nb])
            S.op("act", lambda e: e.activation(out=nb[:], in_=nb[:], func=AF.Sqrt), R=[nb], W=[nb])
            S.op("dve", lambda e: e.tensor_scalar(out=qT[par][96:97, :], in0=KTt[par][96:97, 0:TOK], scalar1=nb[96:97, 0:1], scalar2=-1.0,
                                                  op0=ALU.mult, op1=ALU.mult), R=[nb, KTr[par], qT[par]], W=[qT[par]])

        bank = [psO[0]]
        for i_, _ in enumerate(build_units(0)):
            bank[0] = psO[i_ % 2]
        gen = [None]
        GK = 2
        groups = [list(range(k0, min(k0 + GK, NKT))) for k0 in range(0, NKT, GK)]
        NP_ = len(groups)
        NQB = TOK // 512
        pairs = [(h, qb, kp) for h in range(NH) for qb in range(NQB) for kp in range(NP_)]
        pobuf = {}

        def emit_qk(n):
            h, qb, kp = pairs[n]
            par = h % 2
            q0 = qb * 512
            if kp == 0:
                pobuf[(h, qb)] = psO[(h * NQB + qb) % 2]
                if qb == 0 and gen[0] is not None:
                    raise RuntimeError("norm units of this head were not finished in time")
            if gen[0] is not None and qb >= 3 and 4 <= kp <= NP_ - 5 and kp % 4 == 0:
                bank[0] = psO[(h * NQB + qb + 1) % 2]
                try:
                    next(gen[0])
                except StopIteration:
                    gen[0] = None
            sp_ = psS[n % 3]
            kts = groups[kp]
            for i, kt in enumerate(kts):
                S.op("pe", lambda e, i=i, kt=kt: e.matmul(
                    sp_[:, i * 512:(i + 1) * 512], lhsT=KTt[par][:, kt * 128:(kt + 1) * 128], rhs=qT[par][:, q0:q0 + 512],
                    start=True, stop=True), R=[KTn[par], KTr[par], qT[par]], W=[sp_])
            pt = PT[n % 3]
            w = 512 * len(kts)
            S.op("act", lambda e: e.activation(out=pt[:, 0:w], in_=sp_[:, 0:w], func=AF.Exp, scale=ATTN_SCALE), R=[sp_], W=[pt])

        def emit_pv(n):
            h, qb, kp = pairs[n]
            par = h % 2
            q0 = qb * 512
            po = pobuf[(h, qb)]
            pt = PT[n % 3]
            for i, kt in enumerate(groups[kp]):
                S.op("pe", lambda e, i=i, kt=kt: e.matmul(
                    po[:, :], lhsT=Va[par][:, kt, :], rhs=pt[:, i * 512:(i + 1) * 512],
                    start=(kt == 0), stop=(kt == NKT - 1)), R=[Va[par], pt], W=[po])
            if kp != NP_ - 1:
                return
            num = slice(0, 64) if par == 0 else slice(64, 128)
            den = slice(64, 128) if par == 0 else slice(0, 64)
            o_, d_, g_, oo_ = ot[qb % 2], dn[qb % 2], gt[qb % 2], oo[qb % 2]
            S.op("dve", lambda e: e.tensor_copy(out=o_[:, :], in_=po[:, :]), R=[po], W=[o_])
            S.dma("sp", [(d_[num, :], o_[den, :])], R=[o_], W=[d_])
            S.dma("pool", [(g_[num, :], gsc1.ap()[h // 2, num, q0:q0 + 512])], W=[g_])
            S.op("dve", lambda e: e.reciprocal(out=d_[num, :], in_=d_[num, :]), R=[d_], W=[d_])
            S.op("dve", lambda e: e.tensor_tensor(out=o_[num, :], in0=o_[num, :], in1=d_[num, :], op=ALU.mult), R=[o_, d_], W=[o_])
            S.op("pool", lambda e: e.tensor_tensor(out=oo_[num, :], in0=o_[num, :], in1=g_[num, :], op=ALU.mult), R=[o_, g_], W=[oo_])
            S.dma("sp", [(osc.ap()[h // 2, num, q0:q0 + 512], oo_[num, :])], R=[oo_], owner=oo_)
            if qb == 2 and h + 1 < NH:
                par_ = psS[n % 3]
                halves = [Buf(par_.t[:, 0:512], "psSh0"), Buf(par_.t[:, 512:1024], "psSh1")]
                for hb in halves:
                    hb.lw = dict(par_.lw)
                    hb.rd = dict(par_.rd)
                banks4 = [psO[0], psO[1]] + halves
                bank[0] = banks4[0]
                for i_, _ in enumerate(build_units(h + 1, split=True)):
                    bank[0] = banks4[(i_ + 1) % 4]
                for hb in halves:
                    for dd in (hb.lw, hb.rd):
                        for k_, v_ in dd.items():
                            if par_.rd.get(k_, 0) < v_:
                                par_.rd[k_] = v_
                gen[0] = norm_units(h + 1)

        LOOK = 2
        for n in range(len(pairs) + LOOK):
            if n < len(pairs):
                emit_qk(n)
            if n - LOOK >= 0:
                emit_pv(n - LOOK)
        S.barrier()
        S.flush()


def phase_D(S, P, osc, h1sc, gbc1, out):
    I = P.ins
    with contextlib.ExitStack() as st:
        S.stack = st
        ps = [S.psum("psD_%d" % i, [128, 512], F32) for i in range(4)]
        stg = S.sbuf("stgD", [128, 8, 256], F32)
        wo = S.sbuf("wo1", [128, 8, D], BF16)
        for c0 in range(0, D, 256):
            S.dma("sp", [(stg[:], I["wout1"].ap()[:, c0:c0 + 256].rearrange("(k p) c -> p k c", p=128))], W=[stg])
            S.op("dve", lambda e, c0=c0: e.tensor_copy(out=wo[:, :, c0:c0 + 256], in_=stg[:]), R=[stg], W=[wo])
        g1 = S.sbuf("gbc1D", [128, D], F32); S.dma("sp", [(g1[:], gbc1.ap())], W=[g1])
        lng = S.sbuf("lng1", [128, D], F32); S.dma("sp", [(lng[:], bcast_rows(I["ln_g"], D, D))], W=[lng])
        lnb = S.sbuf("lnb1", [128, D], F32); S.dma("pool", [(lnb[:], bcast_rows(I["ln_b"], D, D))], W=[lnb])
        oT = S.sbuf("oT", [128, 8, TOK], BF16)
        S.dma("sp", [(oT[:], osc.ap().rearrange("c p t -> p c t"))], W=[oT])
        hres = [S.sbuf("hres%d" % i, [128, D], F32) for i in range(2)]
        tmpt = [S.sbuf("tmpD%d" % i, [128, D], F32) for i in range(2)]
        outt = [S.sbuf("outD%d" % i, [128, D], F32) for i in range(2)]
        st6 = [S.sbuf("st6D%d" % i, [128, 2, 6], F32) for i in range(2)]
        mv = [S.sbuf("mvD%d" % i, [128, 2], F32) for i in range(2)]
        rs = [S.sbuf("rsD%d" % i, [128, 1], F32) for i in range(2)]
        psY = [ps[0:2], ps[2:4]]
        for ti in range(TOK // 128):
            hr = hres[ti % 2]
            S.dma("pool", [(hr[:], h1sc.ap()[ti * 128:(ti + 1) * 128, :])], W=[hr])
            yps = psY[ti % 2]
            for hf in range(2):
                for kc in range(8):
                    S.op("pe", lambda e, hf=hf, kc=kc, ti=ti, yps=yps: e.matmul(
                        yps[hf][:, :], lhsT=oT[:, kc, ti * 128:(ti + 1) * 128], rhs=wo[:, kc, hf * 512:(hf + 1) * 512],
                        start=(kc == 0), stop=(kc == 7)), R=[oT, wo], W=[yps[hf]])
            tt, ou = tmpt[ti % 2], outt[ti % 2]
            ln_epilogue(S, "l1", yps, V(lambda hr=hr: hr[:, :], hr), V(lambda: g1[:, :], g1), lng, lnb,
                        V(lambda ou=ou: ou[:, :], ou), V(lambda tt=tt: tt[:, :], tt), (st6[ti % 2], mv[ti % 2], rs[ti % 2]))
            S.dma("sp", [(out.ap()[ti * 128:(ti + 1) * 128, :], ou[:])], R=[ou], owner=ou)
        S.barrier()
        S.flush()


def build(stage="full"):
    P = Prog(stage)
    nc = P.nc
    declare_inputs(P)
    I = P.ins
    zsc = P.scratch("zsc", [6, 128, TA], BF16)
    cc_in_lat = P.scratch("cc_in_lat", [16, 128, 3072], BF16)
    cc_in_ctx = P.scratch("cc_in_ctx", [384, SC], BF16)
    cc_out_lat = P.scratch("cc_out_lat", [16, 512, 3072], BF16)
    cc_out_ctx = P.scratch("cc_out_ctx", [1536, SC], BF16)
    if stage == "A":
        dbg_lat = P.out("dbg_m_lat", [384, SL], BF16)
        dbg_ctx = P.out("dbg_m_ctx", [384, SC], BF16)
        dbg_z = P.out("dbg_z", [6, 128, TA], BF16)
        dbg_mod = P.out("dbg_mod", [128, 16, 2], F32)
        dbg_gbc = P.out("dbg_gbc", [128, 2, D], F32)

    with contextlib.ExitStack() as top:
        S = Sync(nc, top)
        gbc1 = P.scratch("gbc1sc", [128, D], F32)
        zscB = S.view(zsc, "zsc")
        ccinB = S.view(cc_in_lat, "ccin")
        ccincB = S.view(cc_in_ctx, "ccinc")

        with contextlib.ExitStack() as st:
            S.stack = st
            ps = [S.psum("psA1_%d" % i, [128, 512], F32) for i in range(6)]
            psTb = [S.psum("psA1T_%d" % i, [128, 512], BF16) for i in range(2)]
            ident = S.sbuf("ident", [128, 128], BF16)
            S.dma("pool", [(ident[:], I["ident"].ap())], W=[ident])
            const = adaln_setup(S, P, ps)
            mod0 = S.sbuf("mod0", [128, 16, 2], F32)
            gbc0 = S.sbuf("gbc0", [128, 2, D], F32)
            adaln(S, P, 0, const, mod0, gbc0)
            if stage == "A":
                S.dma("sp", [(dbg_mod.ap(), mod0[:])], R=[mod0])
                S.dma("sp", [(dbg_gbc.ap(), gbc0[:])], R=[gbc0])

            win = S.sbuf("win", [128, 8, 768], BF16)
            for hf in range(2):
                ws = S.sbuf("wst%d" % hf, [128, 8, 384], F32)
                S.dma("sp" if hf == 0 else "pool",
                      [(ws[:], I["win0"].ap()[:, hf * 384:(hf + 1) * 384].rearrange("(k p) c -> p k c", p=128))], W=[ws])
                S.op("dve" if hf == 0 else "pool", lambda e, ws=ws, hf=hf: e.tensor_copy(
                    out=win[:, :, hf * 384:(hf + 1) * 384], in_=ws[:]), R=[ws], W=[win])

            xin = [S.sbuf("xin%d" % i, [128, 4, D], BF16) for i in range(3)]
            uTt = [S.sbuf("uT%d" % i, [128, 8, 512], BF16).t for i in range(2)]
            uT = [[S.view(uTt[i], "uT%d_%d" % (i, k)) for k in range(8)] for i in range(2)]
            zstt = [S.sbuf("zst%d" % i, [128, 6, 512], BF16).t for i in range(2)]
            zst = [[S.view(zstt[i], "zst%d_%d" % (i, c)) for c in range(6)] for i in range(2)]
            psT = psTb
            psZ = ps[3:6]
            ntiles = 1 + SL // 512

            def tinfo(t):
                if t == 0:
                    return SC, 0, 1, I["ctx"].ap().rearrange("(j p) d -> p j d", p=128)
                return 512, SC + (t - 1) * 512, 0, I["x_all"].ap()[(t - 1) * 512:t * 512, :].rearrange("(j p) d -> p j d", p=128)

            def a1_load(t):
                ntok, tok0, j, src = tinfo(t)
                xi = xin[t % 3]
                S.dma("pool", [(xi[:, 0:ntok // 128, :], src)], W=[xi])

            def a1_T(t, k):
                ntok, tok0, j, src = tinfo(t)
                xi = xin[t % 3]
                bank = psT[k % 2]
                for jj in range(ntok // 128):
                    S.op("pe", lambda e, jj=jj: e.transpose(
                        out=bank[:, jj * 128:(jj + 1) * 128], in_=xi[:, jj, k * 128:(k + 1) * 128], identity=ident[:]),
                        R=[xi, ident], W=[bank])
                u = uT[t % 2][k]
                if k % 2 == 0:
                    S.op("act", lambda e: e.activation(
                        out=u[:, k, 0:ntok], in_=bank[:, 0:ntok], func=AF.Identity,
                        scale=mod0[:, 8 + k, j:j + 1], bias=mod0[:, k, j:j + 1]), R=[bank, mod0], W=[u])
                else:
                    S.op("dve", lambda e: e.tensor_scalar(
                        out=u[:, k, 0:ntok], in0=bank[:, 0:ntok], scalar1=mod0[:, 8 + k, j:j + 1],
                        scalar2=mod0[:, k, j:j + 1], op0=ALU.mult, op1=ALU.add), R=[bank, mod0], W=[u])

            def a1_Z(t, c):
                ntok, tok0, j, src = tinfo(t)
                zb = psZ[c % 3]
                for k in range(8):
                    u = uT[t % 2][k]
                    S.op("pe", lambda e, u=u, k=k: e.matmul(
                        zb[:, 0:ntok], lhsT=win[:, k, c * 128:(c + 1) * 128], rhs=u[:, k, 0:ntok],
                        start=(k == 0), stop=(k == 7)), R=[win, u], W=[zb])
                zs = zst[t % 2][c]
                if c < 3:
                    S.op("dve", lambda e: e.tensor_copy(out=zs[:, c, 0:ntok], in_=zb[:, 0:ntok]), R=[zb], W=[zs])
                else:
                    S.op("act", lambda e: e.activation(out=zs[:, c, 0:ntok], in_=zb[:, 0:ntok], func=AF.Silu), R=[zb], W=[zs])
                if c == 5:
                    zt = zst[t % 2][0].t
                    S.dma("sp" if t % 2 == 1 else "act",
                          [(zsc.ap()[:, :, tok0:tok0 + ntok].rearrange("c p t -> p c t"), zt[:, :, 0:ntok])],
                          R=zst[t % 2], W=[], owner=zst[t % 2][0])

            a1_load(0)
            a1_load(1)
            a1_load(2)
            for k in range(8):
                a1_T(0, k)
            for t in range(ntiles):
                for k in range(8):
                    if t + 1 < ntiles:
                        a1_T(t + 1, k)
                    if k < 6:
                        a1_Z(t, k)
                if t + 3 < ntiles:
                    a1_load(t + 3)
            S.barrier()
            S.flush()
        phase_A2(S, P, zsc, cc_in_lat, cc_in_ctx)
        phase_A3(S, P, zsc, cc_in_lat, cc_in_ctx)
        if stage == "A":
            with contextlib.ExitStack() as st:
                S.stack = st
                for c in range(3):
                    bt = S.sbuf("dbgm%d" % c, [128, SC], BF16)
                    S.dma("sp", [(bt[:], cc_in_ctx.ap()[c * 128:(c + 1) * 128, :])], W=[bt])
                    S.dma("sp", [(dbg_ctx.ap()[c * 128:(c + 1) * 128, :], bt[:])], R=[bt])
                    bl = S.sbuf("dbgl%d" % c, [128, SL], BF16)
                    S.dma("sp", [(bl[:, i * 1024:(i + 1) * 1024].rearrange("p (b t) -> p b t", t=512),
                                  cc_in_lat.ap()[i].rearrange("p (b c t) -> p b c t", b=2, c=3)[:, :, c, :]) for i in range(16)], W=[bl])
                    S.dma("sp", [(dbg_lat.ap()[c * 128:(c + 1) * 128, :], bl[:])], R=[bl])
                S.barrier()
                S.flush()
        if stage == "A":
            with contextlib.ExitStack() as st:
                S.stack = st
                for c in range(6):
                    bt = S.sbuf("dbgb%d" % c, [128, TA], BF16)
                    S.dma("sp", [(bt[:], zsc.ap()[c])], W=[bt])
                    S.dma("sp", [(dbg_z.ap()[c], bt[:])], R=[bt])
                S.barrier()
                S.flush()
        if stage == "A":
            return P
        S.stack = top
        h1sc = P.scratch("h1sc", [TOK, D], F32)
        gsc1 = P.scratch("gsc1", [8, 128, TOK], BF16)
        qsc = P.scratch("qsc", [NH, 96, TOK], BF16)
        cc2_in = (P.scratch("cc2c_in", [160, SC], BF16), P.scratch("cc2l_in", [2, 160, 2048], BF16))
        cc2_out = (P.scratch("cc2c_out", [640, SC], BF16), P.scratch("cc2l_out", [2, 640, 2048], BF16))
        osc = P.scratch("osc", [8, 128, TOK], BF16)
        outT = P.out("out", [TOK, D], F32)
        phase_B(S, P, (cc_in_lat, cc_in_ctx, cc_out_lat, cc_out_ctx), h1sc, gsc1, qsc, cc2_in, gbc1)
        if stage == "B":
            dbg_h1 = P.out("dbg_h1", [TOK, D], F32)
            dbg_q = P.out("dbg_q", [NH, 96, TOK], BF16)
            dbg_kv = P.out("dbg_kv", [160, SC + TOK], BF16)
            with contextlib.ExitStack() as st:
                S.stack = st
                bts = [S.sbuf("dbh%d" % i, [128, 4, D], F32) for i in range(2)]
                for i in range(TOK // 512):
                    bt = bts[i % 2]
                    S.dma("sp", [(bt[:], h1sc.ap()[i * 512:(i + 1) * 512, :].rearrange("(j p) d -> p j d", p=128))], W=[bt])
                    S.dma("sp", [(dbg_h1.ap()[i * 512:(i + 1) * 512, :].rearrange("(j p) d -> p j d", p=128), bt[:])], R=[bt])
                bqs = [S.sbuf("dbq%d" % i, [96, TOK], BF16) for i in range(2)]
                for h in range(NH):
                    bt = bqs[h % 2]
                    S.dma("sp", [(bt[:], qsc.ap()[h])], W=[bt])
                    S.dma("sp", [(dbg_q.ap()[h], bt[:])], R=[bt])
                for (r0, n) in ((0, 128), (128, 32)):
                    bt = S.sbuf("dbk%d" % r0, [n, SC + TOK], BF16)
                    S.dma("sp", [(bt[:, 0:SC], cc2_in[0].ap()[r0:r0 + n, :]),
                                 (bt[:, SC:SC + 2048], cc2_in[1].ap()[0, r0:r0 + n, :]),
                                 (bt[:, SC + 2048:SC + 4096], cc2_in[1].ap()[1, r0:r0 + n, :])], W=[bt])
                    S.dma("sp", [(dbg_kv.ap()[r0:r0 + n, :], bt[:])], R=[bt])
                S.barrier()
                S.flush()
            return P
        phase_C(S, P, cc2_in, cc2_out, qsc, gsc1, osc)
        phase_D(S, P, osc, h1sc, gbc1, outT)
        return P


def _fm(v, nchunk):
    return np.ascontiguousarray(np.asarray(v, np.float32).reshape(nchunk, 128).T)


def _consts():
    n = np.arange(128)
    ang = 2.0 * np.pi * np.outer(n, n) / 128.0
    c128 = np.cos(ang).astype(np.float32)
    s128 = np.sin(ang).astype(np.float32)
    angt = 2.0 * np.pi * np.outer(n, n) / 16384.0
    tr = np.cos(angt).astype(np.float32)
    sn = np.sin(angt).astype(np.float32)
    m = np.arange(256)
    ang256 = 2.0 * np.pi * np.outer(m, m) / 256.0
    c256 = np.cos(ang256).astype(np.float32).reshape(2, 128, 256).transpose(1, 0, 2)
    s256 = np.sin(ang256).astype(np.float32).reshape(2, 128, 256).transpose(1, 0, 2)
    return dict(
        ident=np.eye(128, dtype=np.float32), c128=c128, s128=s128, s128n=-s128,
        cs1=np.concatenate([c128, -s128], 1), cs2=np.concatenate([s128, c128], 1),
        tt1=np.concatenate([tr, tr], 1), tt2=np.concatenate([sn, -sn], 1),
        c256=np.ascontiguousarray(c256), s256=np.ascontiguousarray(s256))


def _rope_tables(tc):
    t = np.arange(TOK) + TOK * tc
    inv = (10000.0 ** (-np.arange(8, dtype=np.float32) / 8.0)).astype(np.float32)
    row = (t // 64).astype(np.float32)
    col = (t % 64).astype(np.float32)
    ar = row[None, :] * inv[:, None]
    ac = col[None, :] * inv[:, None]
    cosf = np.concatenate([np.cos(ar), np.cos(ar), np.cos(ac), np.cos(ac)], 0)
    sinf = np.concatenate([-np.sin(ar), np.sin(ar), -np.sin(ac), np.sin(ac)], 0)
    c = np.zeros((96, TOK), np.float32)
    s = np.zeros((96, TOK), np.float32)
    c[64:96] = cosf
    s[64:96] = sinf
    return c, s


_ROPE_PERM = np.concatenate([np.arange(8, 16), np.arange(0, 8), np.arange(24, 32), np.arange(16, 24)])


def prep_inputs(inp):
    f = lambda a: np.ascontiguousarray(np.asarray(a, np.float32))
    x, c, ctx, c_ctx = f(inp["x"]), f(inp["c"]), f(inp["ctx"]), f(inp["c_ctx"])
    ada_w, ada_b = f(inp["ada_w"]), f(inp["ada_b"])
    w_in_rf = f(inp["w_in_rf"])[0]
    conv_w, conv_b = f(inp["conv_w"])[0], f(inp["conv_b"])[0]
    gw, gb, lam = f(inp["lru_gate_w"])[0], f(inp["lru_gate_b"])[0], f(inp["lru_lambda"])[0]
    fnw, fnb = f(inp["fnet_w"])[0], f(inp["fnet_b"])[0]
    w_out_rf = f(inp["w_out_rf"])[0]
    w_in_mla = f(inp["w_in_mla"])[0]
    qng, kvg = f(inp["q_norm_g"])[0], f(inp["kv_norm_g"])[0]
    w_uq, w_ukv, w_out_mla = f(inp["w_uq"])[0], f(inp["w_ukv"])[0], f(inp["w_out_mla"])[0]
    cs = _consts()
    ada_bf = np.ascontiguousarray(ada_b.reshape(2, 24, 128).transpose(2, 0, 1))
    ada_bg = np.ascontiguousarray(ada_b[:, 2048:3072])
    perm_rows = np.concatenate([np.concatenate([np.arange(256 * r, 256 * r + 256),
                                                1024 + np.arange(128 * r, 128 * r + 128)]) for r in range(4)])
    wout0 = np.ascontiguousarray(w_out_rf[perm_rows])
    kr_cols = 384 + _ROPE_PERM
    wkr = np.ascontiguousarray(np.concatenate([w_in_mla[:, 256:320], w_in_mla[:, 384:416]], 1))
    wkrp = np.ascontiguousarray(np.concatenate([w_in_mla[:, 256:320], w_in_mla[:, kr_cols]], 1))
    qperm = np.arange(1536).reshape(16, 96)
    qperm[:, 64:96] = qperm[:, 64:96][:, _ROPE_PERM]
    wuqp = np.ascontiguousarray(w_uq[:, qperm.reshape(-1)])
    maps = []
    for core in range(8):
        b, g = core // 4, core % 4
        cols = np.concatenate([np.arange(256 * g, 256 * g + 256), 1024 + np.arange(128 * g, 128 * g + 128),
                               1536 + np.arange(256 * g, 256 * g + 256), 2560 + np.arange(128 * g, 128 * g + 128)])
        ch = np.arange(256 * g, 256 * g + 256)
        rc, rs = _rope_tables(g)
        idx = np.zeros((32, 128, 1), np.int32)
        for blk in range(8):
            for r in range(4):
                gblk = g * 8 + blk
                idx[blk * 4 + r, :, 0] = ((gblk // 2) * 512 + r * 128 + np.arange(128)) * 2 + gblk % 2
        m = dict(
            x_all=x[b], x_own=np.ascontiguousarray(x[b, TOK * g:TOK * (g + 1)]), ctx=ctx[b],
            cvec=np.ascontiguousarray(np.stack([c[b], c_ctx], -1).reshape(8, 128, 2).transpose(1, 0, 2)),
            ada_w=ada_w, ada_bf=ada_bf, ada_bg=ada_bg, ln_g=f(inp["ln_g"]), ln_b=f(inp["ln_b"]),
            win0=np.ascontiguousarray(w_in_rf[:, cols]),
            convw=np.ascontiguousarray(conv_w[:, ch].reshape(4, 2, 128).transpose(2, 1, 0)),
            convb=_fm(conv_b[ch], 2),
            gatew=np.ascontiguousarray(gw[:, :, 4 * g:4 * g + 4]),
            gateb=np.ascontiguousarray(gb[:, :, ch].reshape(2, 2, 2, 128).transpose(3, 0, 1, 2)),
            lam=np.ascontiguousarray(lam[:, ch].reshape(2, 2, 128).transpose(2, 0, 1)),
            fw=fnw[g], fb=np.ascontiguousarray(fnb[128 * g:128 * g + 128, None]),
            wout0=wout0, idx=np.ascontiguousarray(idx),
            win1=w_in_mla, qng=_fm(qng, 2), kvg=_fm(kvg, 1),
            wuq=w_uq, wuqp=wuqp, wukv=w_ukv, wout1=w_out_mla,
            ropec=rc.astype(ml_dtypes.bfloat16), ropes=rs.astype(ml_dtypes.bfloat16), wkr=wkr, wkrp=wkrp, **cs)
        maps.append(m)
    return maps


_CACHE = {}


def kernel(**inputs):
    maps = prep_inputs(inputs)
    if "full" not in _CACHE:
        _CACHE["full"] = build("full")
    P = _CACHE["full"]
    res = run_bass_kernel_spmd(P.nc, maps, core_ids=list(range(8)))
    out = np.zeros((2, SL, D), np.float32)
    for core in range(8):
        b, g = core // 4, core % 4
        out[b, TOK * g:TOK * (g + 1)] = res.results[core]["out"]
    return out
```

```python
import contextlib
import numpy as np
import ml_dtypes
import concourse.bass as bass
import concourse.mybir as mybir
from concourse.bass_utils import run_bass_kernel_spmd

F32 = mybir.dt.float32
BF16 = mybir.dt.bfloat16
I32 = mybir.dt.int32
AF = mybir.ActivationFunctionType
ALU = mybir.AluOpType

D = 1024
SL = 16384
SC = 256
TA = SL + SC
TOK = 4096
NH = 16
ALPHA = 4.0 ** 0.25
LN_EPS = 1e-6
RMS_EPS = 1e-6
ATTN_SCALE = 96.0 ** -0.5


class Buf:
    __slots__ = ("t", "lw", "rd", "dsem", "dcnt", "name")

    def __init__(self, t, name=""):
        self.t = t
        self.lw = {}
        self.rd = {}
        self.dsem = None
        self.dcnt = 0
        self.name = name

    def __getitem__(self, idx):
        return self.t[idx]


class Sync:
    ENG = ("pe", "act", "dve", "pool", "sp")
    NPOOL = 64

    def __init__(self, nc, stack, same_engine_wait=True):
        self.nc = nc
        self.stack = stack
        self.sems = {}
        self.cnt = {}
        for e in ("pe", "act", "dve", "pool"):
            self.sems[e] = stack.enter_context(nc.semaphore("c_" + e))
            self.cnt[e] = 0
        self.free = {"hw": [], "sw": []}
        for i in range(self.NPOOL):
            k = "d%d" % i
            self.sems[k] = stack.enter_context(nc.semaphore(k))
            self.cnt[k] = 0
            self.free["hw" if i < 38 else "sw"].append(k)
        self.known = {e: {} for e in self.ENG}
        self.same = same_engine_wait
        self.nwaits = 0
        self.nins = 0
        self.nalloc = 0
        self.owners = []
        self.group = None
        self.prog = {e: [] for e in self.ENG}

    def sbuf(self, name, shape, dt):
        self.nalloc += 1
        t = self.stack.enter_context(self.nc.sbuf_tensor("s%d_%s" % (self.nalloc, name), list(shape), dt))
        return Buf(t, name)

    def psum(self, name, shape, dt):
        self.nalloc += 1
        t = self.stack.enter_context(self.nc.psum_tensor("p%d_%s" % (self.nalloc, name), list(shape), dt))
        return Buf(t, name)

    def view(self, t, name=""):
        return Buf(t, name)

    def _dsem(self, b, q):
        kind = "sw" if q == "pool" else "hw"
        if b.dsem is None:
            b.dsem = self.free[kind].pop()
            self.owners.append(b)
        elif (int(b.dsem[1:]) >= 38) != (kind == "sw"):
            raise RuntimeError("buffer %s mixes software- and hardware-DGE DMAs on one semaphore" % b.name)
        return b.dsem

    def release(self):
        for b in self.owners:
            self.free["sw" if int(b.dsem[1:]) >= 38 else "hw"].append(b.dsem)
            b.dsem = None
        self.owners = []

    def group_begin(self):
        self.group = (Buf(None, "grp"), [])

    def group_end(self):
        g, bufs = self.group
        self.group = None
        if g.dsem is None:
            return
        k = g.dsem
        for b in bufs:
            b.lw = {k: self.cnt[k]}

    def _waits(self, q, R, W):
        need = {}
        for b in R:
            for k, v in b.lw.items():
                if need.get(k, 0) < v:
                    need[k] = v
        for b in W:
            for d in (b.lw, b.rd):
                for k, v in d.items():
                    if need.get(k, 0) < v:
                        need[k] = v
        kn = self.known[q]
        for k, v in need.items():
            if k == q and (q == "pe" or not self.same):
                continue
            if kn.get(k, 0) >= v:
                continue
            self.prog[q].append(("w", self.sems[k], v))
            kn[k] = v
            self.nwaits += 1

    def _post(self, ev, R, W):
        k, v = ev
        for b in W:
            b.lw = {k: v}
            b.rd = {}
        for b in R:
            if b not in W:
                b.rd[k] = v

    def op(self, q, fn, R=(), W=()):
        self._waits(q, R, W)
        self.cnt[q] += 1
        self.prog[q].append(("i", fn, self.sems[q], 1))
        self._post((q, self.cnt[q]), R, W)
        self.nins += 1

    def dma(self, q, pairs, R=(), W=(), owner=None):
        if self.group is not None and owner is None:
            owner = self.group[0]
            self.group[1].extend(W)
        if owner is None:
            owner = W[0] if W else R[0]
        k = self._dsem(owner, q)
        self._waits(q, R, W)
        for (o, i) in pairs:
            self.prog[q].append(("i", (lambda e, o=o, i=i: e.dma_start(out=o, in_=i)), self.sems[k], 16))
            self.cnt[k] += 16
        self._post((k, self.cnt[k]), R, W)
        self.nins += len(pairs)

    def special(self, q, fn, inc, R=(), W=(), owner=None):
        if owner is None:
            owner = W[0]
        k = self._dsem(owner, q)
        self._waits(q, R, W)
        self.prog[q].append(("i", fn, self.sems[k], inc))
        self.cnt[k] += inc
        self._post((k, self.cnt[k]), R, W)
        self.nins += 1

    def barrier(self):
        for q in self.ENG:
            kn = self.known[q]
            for k, v in self.cnt.items():
                if v and kn.get(k, 0) < v:
                    self.prog[q].append(("w", self.sems[k], v))
                    kn[k] = v
                    self.nwaits += 1
        self.release()

    def flush(self):
        prog = self.prog
        self.prog = {e: [] for e in self.ENG}

        def run(eng, items):
            for it in items:
                if it[0] == "w":
                    eng.wait_ge(it[1], it[2])
                else:
                    it[1](eng).then_inc(it[2], it[3])

        with self.nc.Block() as block:
            @block.tensor
            def _(e):
                run(e, prog["pe"])

            @block.scalar
            def _(e):
                run(e, prog["act"])

            @block.vector
            def _(e):
                run(e, prog["dve"])

            @block.gpsimd
            def _(e):
                run(e, prog["pool"])

            @block.sync
            def _(e):
                run(e, prog["sp"])


def bcast_rows(dram_ap_1d_tensor, offset, n, parts=128):
    return bass.AP(dram_ap_1d_tensor, offset, [[0, parts], [1, n]])


class Prog:
    def __init__(self, stage="full"):
        self.stage = stage
        self.nc = bass.Bass("TRN2", target_bir_lowering=False)
        self.ins = {}
        self.outs = {}

    def inp(self, name, shape, dt=F32):
        t = self.nc.dram_tensor(name, list(shape), dt, kind="ExternalInput")
        self.ins[name] = t
        return t

    def out(self, name, shape, dt=F32):
        t = self.nc.dram_tensor(name, list(shape), dt, kind="ExternalOutput")
        self.outs[name] = t
        return t

    def scratch(self, name, shape, dt):
        return self.nc.dram_tensor(name, list(shape), dt)


def declare_inputs(P):
    i = P.inp
    i("x_all", [SL, D]); i("x_own", [TOK, D]); i("ctx", [SC, D])
    i("cvec", [128, 8, 2])
    i("ada_w", [2, D, 3 * D]); i("ada_bf", [128, 2, 24]); i("ada_bg", [2, D])
    i("ln_g", [2, D]); i("ln_b", [2, D])
    i("win0", [D, 768]); i("convw", [128, 2, 4]); i("convb", [128, 2])
    i("gatew", [2, 2, 4, 64, 64]); i("gateb", [128, 2, 2, 2]); i("lam", [128, 2, 2])
    i("fw", [128, 128]); i("fb", [128, 1])
    i("ident", [128, 128]); i("c128", [128, 128]); i("s128", [128, 128]); i("s128n", [128, 128])
    i("cs1", [128, 256]); i("cs2", [128, 256]); i("tt1", [128, 256]); i("tt2", [128, 256])
    i("c256", [128, 2, 256]); i("s256", [128, 2, 256])
    i("wout0", [1536, D])
    i("idx", [32, 128, 1], I32)
    i("win1", [D, 1440]); i("wkr", [D, 96]); i("wkrp", [D, 96])
    i("qng", [128, 2]); i("kvg", [128, 1])
    i("wuq", [256, 1536]); i("wuqp", [256, 1536]); i("wukv", [128, 2048]); i("wout1", [D, D])
    i("ropec", [96, TOK], BF16); i("ropes", [96, TOK], BF16)


def adaln(S, P, layer, const, mod, gbc, q2="pool"):
    ada_w = P.ins["ada_w"]
    abg = const["abg"]
    S.dma("sp", [(abg[:], bcast_rows(P.ins["ada_bg"], layer * D, D))], W=[abg])
    psm = const["ps"][0]
    wb = const["adawblk"]
    scf, scbc, abf = const["scf"], const["scbc"], const["ada_bf"]
    for cb in range(12):
        w = wb[cb % 2]
        src = ada_w.ap()[layer, :, cb * 256:(cb + 1) * 256].rearrange("(k p) c -> p k c", p=128)
        S.dma("sp" if cb % 2 == 0 else q2, [(w[:], src)], W=[w])
        if cb < 8:
            for fi in range(2):
                fc = cb * 2 + fi
                for k in range(8):
                    S.op("pe", lambda e, w=w, k=k, fi=fi, fc=fc: e.matmul(
                        psm[:, fc * 2:fc * 2 + 2], lhsT=w[:, k, fi * 128:(fi + 1) * 128], rhs=scf[:, k, :],
                        start=(k == 0), stop=(k == 7)), R=[w, scf], W=[psm])
        else:
            for j in range(2):
                pg = const["ps"][1 + j]
                for k in range(8):
                    S.op("pe", lambda e, w=w, k=k, j=j, pg=pg: e.matmul(
                        pg[:, 0:256], lhsT=scbc[:, k, j, :], rhs=w[:, k, :], start=(k == 0), stop=(k == 7)),
                        R=[w, scbc], W=[pg])
                c0 = (cb - 8) * 256
                S.op("dve", lambda e, j=j, pg=pg, c0=c0: e.tensor_tensor(
                    out=gbc[:, j, c0:c0 + 256], in0=pg[:, 0:256], in1=abg[:, c0:c0 + 256], op=ALU.add),
                    R=[pg, abg], W=[gbc])
        if cb == 7:
            S.op("dve", lambda e: e.tensor_tensor(
                out=mod[:, :, :], in0=psm[:, 0:32].rearrange("p (f j) -> p f j", j=2),
                in1=abf[:, layer, 0:16].unsqueeze(2).to_broadcast([128, 16, 2]), op=ALU.add),
                R=[psm, abf], W=[mod])
            S.op("dve", lambda e: e.tensor_scalar(
                out=mod[:, 8:16, :], in0=mod[:, 8:16, :], scalar1=1.0, scalar2=None, op0=ALU.add),
                R=[mod], W=[mod])


def adaln_setup(S, P, ps):
    I = P.ins
    cv = S.sbuf("cv", [128, 8, 2], F32)
    S.dma("sp", [(cv[:], I["cvec"].ap())], W=[cv])
    scf = S.sbuf("scf", [128, 8, 2], F32)
    S.op("act", lambda e: e.activation(out=scf[:], in_=cv[:], func=AF.Silu), R=[cv], W=[scf])
    scbc = S.sbuf("scbc", [128, 8, 2, 128], F32)
    S.op("dve", lambda e: e.tensor_copy(
        out=scbc[:].rearrange("p k j m -> p (k j) m"),
        in_=scf[:].rearrange("p k j -> p (k j)").unsqueeze(2).to_broadcast([128, 16, 128])), R=[scf], W=[scbc])
    abf = S.sbuf("abf", [128, 2, 24], F32)
    S.dma("sp", [(abf[:], I["ada_bf"].ap())], W=[abf])
    abg = S.sbuf("abg", [128, D], F32)
    return dict(ps=ps, scf=scf, scbc=scbc, ada_bf=abf, abg=abg,
                adawblk=[S.sbuf("adaw%d" % i, [128, 8, 256], F32) for i in range(2)])


def phase_A2(S, P, zsc, cc_in_lat, cc_in_ctx):
    I = P.ins
    TS = 1024
    with contextlib.ExitStack() as st:
        S.stack = st
        ps = [S.psum("psA2_%d" % i, [128, 512], F32) for i in range(8)]
        cw = S.sbuf("cw", [128, 2, 4], F32); S.dma("sp", [(cw[:], I["convw"].ap())], W=[cw])
        cb = S.sbuf("cb", [128, 2], F32); S.dma("sp", [(cb[:], I["convb"].ap())], W=[cb])
        gb = S.sbuf("gb", [128, 2, 2, 2], F32); S.dma("sp", [(gb[:], I["gateb"].ap())], W=[gb])
        lam = S.sbuf("lam", [128, 2, 2], F32); S.dma("sp", [(lam[:], I["lam"].ap())], W=[lam])
        identf = S.sbuf("identf", [128, 128], F32); S.dma("sp", [(identf[:], I["ident"].ap())], W=[identf])
        sp_ = S.sbuf("sp_", [128, 2, 2], F32)
        S.op("act", lambda e: e.activation(out=sp_[:], in_=lam[:], func=AF.Exp, scale=-1.0), R=[lam], W=[sp_])
        S.op("act", lambda e: e.activation(out=sp_[:], in_=sp_[:], func=AF.Ln, bias=1.0, scale=1.0), R=[sp_], W=[sp_])
        sc8 = S.sbuf("sc8", [128, 2, 2], F32)
        sc16 = S.sbuf("sc16", [128, 2, 2], F32)
        S.op("dve", lambda e: e.tensor_scalar(out=sc8[:], in0=sp_[:], scalar1=-8.0, scalar2=None, op0=ALU.mult), R=[sp_], W=[sc8])
        S.op("dve", lambda e: e.tensor_scalar(out=sc16[:], in0=sp_[:], scalar1=-16.0, scalar2=None, op0=ALU.mult), R=[sp_], W=[sc16])
        gwf = S.sbuf("gwf", [128, 8, 128], F32)
        S.op("pool", lambda e: e.memset(gwf[:], 0.0), W=[gwf])
        pairs = []
        for c in range(2):
            for d in range(2):
                for kd in range(2):
                    for hh in range(2):
                        pairs.append((gwf[hh * 64:(hh + 1) * 64, (c * 2 + d) * 2 + kd, hh * 64:(hh + 1) * 64],
                                      I["gatew"].ap()[d, kd, 2 * c + hh]))
        S.dma("sp", pairs, W=[gwf])
        gw = S.sbuf("gw", [128, 8, 128], BF16)
        S.op("dve", lambda e: e.tensor_copy(out=gw[:], in_=gwf[:]), R=[gwf], W=[gw])
        dg = S.sbuf("dg", [128, 8, 128], BF16)
        for c in range(2):
            for k in range(4):
                S.op("dve", lambda e, c=c, k=k: e.tensor_scalar(
                    out=dg[:, c * 4 + k, :], in0=identf[:], scalar1=cw[:, c, k:k + 1], scalar2=None, op0=ALU.mult),
                    R=[identf, cw], W=[dg])
        XW = TA + 8
        xv = S.sbuf("xv", [128, XW], BF16)
        S.op("pool", lambda e: e.memset(xv[:], 0.0), W=[xv])
        xl_t = S.sbuf("xl", [128, TA], BF16).t
        R_t = S.sbuf("Rr", [128, TA], F32).t
        tiles = [(0, SC)] + [(SC + i * TS, TS) for i in range(SL // TS)]
        xlB = [S.view(xl_t, "xl%d" % i) for i in range(len(tiles))]
        RB = [S.view(R_t, "R%d" % i) for i in range(len(tiles))]
        tmp = {}
        for nm in ("r", "i", "a", "a2", "h"):
            tmp[nm] = [S.sbuf("t_%s%d" % (nm, i), [128, TS], F32) for i in range(3)]
        gt = [S.sbuf("gt%d" % i, [128, TS], BF16) for i in range(2)]
        mo = [S.sbuf("mo%d" % i, [128, TS], BF16) for i in range(2)]
        carry = S.sbuf("carry", [128, 1], F32)
        psR = [S.view(ps[0].t, "psR0"), S.view(ps[2].t, "psR1")]
        for c in range(2):
            S.dma("sp", [(xv[:, 1:1 + SC], zsc.ap()[c, :, 0:SC]), (xv[:, 260:260 + SL], zsc.ap()[c, :, SC:TA])], W=[xv])
            for ti, (tok0, ntok) in enumerate(tiles):
                base = tok0 if ti == 0 else 259 + (tok0 - SC)
                for h0 in range(0, ntok, 512):
                    n = min(512, ntok - h0)
                    pb = ps[4 + (h0 // 512) % 2]
                    for k in range(4):
                        S.op("pe", lambda e, pb=pb, k=k, c=c, n=n, o=base + h0 + k: e.matmul(
                            pb[:, 0:n], lhsT=dg[:, c * 4 + k, :], rhs=xv[:, o:o + n], start=(k == 0), stop=(k == 3)),
                            R=[dg, xv], W=[pb])
                    S.op("act", lambda e, pb=pb, n=n, c=c, o=tok0 + h0: e.activation(
                        out=xl_t[:, o:o + n], in_=pb[:, 0:n], func=AF.Identity, bias=cb[:, c:c + 1], scale=1.0),
                        R=[pb, cb], W=[xlB[ti]])
            def tile_gen(d, n_i, ti):
                tok0, ntok = tiles[ti]
                pi = n_i % 2
                pr, pim = ps[pi * 4:pi * 4 + 2], ps[pi * 4 + 2:pi * 4 + 4]
                for h0 in range(0, ntok, 512):
                    n = min(512, ntok - h0)
                    for kd, pp in ((0, pr), (1, pim)):
                        pb = pp[h0 // 512]
                        S.op("pe", lambda e, pb=pb, n=n, kd=kd, c=c, d=d, o=tok0 + h0: e.matmul(
                            pb[:, 0:n], lhsT=gw[:, (c * 2 + d) * 2 + kd, :], rhs=xl_t[:, o:o + n], start=True, stop=True),
                            R=[gw, xlB[ti]], W=[pb])
                r_, i_, a_, a2_, h_ = (tmp[k][n_i % 3] for k in ("r", "i", "a", "a2", "h"))
                s_, bx_, b_ = a2_, i_, i_
                for h0 in range(0, ntok, 512):
                    n = min(512, ntok - h0)
                    S.op("act", lambda e, n=n, h0=h0, pb=pr[h0 // 512], r_=r_, d=d, c=c: e.activation(
                        out=r_[:, h0:h0 + n], in_=pb[:, 0:n], func=AF.Sigmoid, bias=gb[:, d, 0, c:c + 1], scale=1.0),
                        R=[pr[h0 // 512], gb], W=[r_])
                    S.op("act", lambda e, n=n, h0=h0, pb=pim[h0 // 512], i_=i_, d=d, c=c: e.activation(
                        out=i_[:, h0:h0 + n], in_=pb[:, 0:n], func=AF.Sigmoid, bias=gb[:, d, 1, c:c + 1], scale=1.0),
                        R=[pim[h0 // 512], gb], W=[i_])
                yield
                S.op("act", lambda e, a_=a_, r_=r_, ntok=ntok, d=d, c=c: e.activation(
                    out=a_[:, 0:ntok], in_=r_[:, 0:ntok], func=AF.Exp, scale=sc8[:, d, c:c + 1]), R=[r_, sc8], W=[a_])
                S.op("act", lambda e, a2_=a2_, r_=r_, ntok=ntok, d=d, c=c: e.activation(
                    out=a2_[:, 0:ntok], in_=r_[:, 0:ntok], func=AF.Exp, scale=sc16[:, d, c:c + 1]), R=[r_, sc16], W=[a2_])
                yield
                S.op("act", lambda e, s_=s_, a2_=a2_, ntok=ntok: e.activation(
                    out=s_[:, 0:ntok], in_=a2_[:, 0:ntok], func=AF.Sqrt, bias=1.0, scale=-1.0), R=[a2_], W=[s_])
                yield
                S.op("pool", lambda e, bx_=bx_, i_=i_, ntok=ntok, tok0=tok0: e.tensor_tensor(
                    out=bx_[:, 0:ntok], in0=i_[:, 0:ntok], in1=xl_t[:, tok0:tok0 + ntok], op=ALU.mult),
                    R=[i_, xlB[ti]], W=[bx_])
                S.op("dve", lambda e, b_=b_, bx_=bx_, s_=s_, ntok=ntok: e.tensor_tensor(
                    out=b_[:, 0:ntok], in0=bx_[:, 0:ntok], in1=s_[:, 0:ntok], op=ALU.mult), R=[bx_, s_], W=[b_])
                init = 0.0 if n_i == 0 else carry[:, 0:1]
                if d == 0:
                    S.op("dve", lambda e, a_=a_, b_=b_, ntok=ntok, tok0=tok0, init=init: e.tensor_tensor_scan(
                        out=R_t[:, tok0:tok0 + ntok], data0=a_[:, 0:ntok], data1=b_[:, 0:ntok], initial=init,
                        op0=ALU.mult, op1=ALU.add), R=[a_, b_, carry], W=[RB[ti]])
                    S.op("dve", lambda e, o=tok0 + ntok - 1: e.tensor_copy(out=carry[:], in_=R_t[:, o:o + 1]),
                         R=[RB[ti]], W=[carry])
                else:
                    S.op("dve", lambda e, a_=a_, b_=b_, h_=h_, ntok=ntok, init=init: e.tensor_tensor_scan(
                        out=h_[:, 0:ntok][:, ::-1], data0=a_[:, 0:ntok][:, ::-1], data1=b_[:, 0:ntok][:, ::-1], initial=init,
                        op0=ALU.mult, op1=ALU.add), R=[a_, b_, carry], W=[h_])
                    S.op("dve", lambda e, h_=h_: e.tensor_copy(out=carry[:], in_=h_[:, 0:1]), R=[h_], W=[carry])
                    g_ = gt[pi]
                    S.dma("sp", [(g_[:, 0:ntok], zsc.ap()[3 + c, :, tok0:tok0 + ntok])], W=[g_])
                    S.op("pool", lambda e, h_=h_, ntok=ntok, tok0=tok0: e.tensor_tensor(
                        out=h_[:, 0:ntok], in0=h_[:, 0:ntok], in1=R_t[:, tok0:tok0 + ntok], op=ALU.add),
                        R=[h_, RB[ti]], W=[h_])
                    m_ = mo[pi]
                    S.op("pool", lambda e, h_=h_, m_=m_, g_=g_, ntok=ntok: e.tensor_tensor(
                        out=m_[:, 0:ntok], in0=h_[:, 0:ntok], in1=g_[:, 0:ntok], op=ALU.mult), R=[h_, g_], W=[m_])
                    if ti == 0:
                        dst = cc_in_ctx.ap()[c * 128:(c + 1) * 128, :]
                    else:
                        dst = cc_in_lat.ap()[ti - 1].rearrange("p (b c t) -> p b c t", b=2, c=3)[:, :, c, :]
                    src_ = m_[:, 0:ntok] if ti == 0 else m_[:, 0:ntok].rearrange("p (b t) -> p b t", t=512)
                    S.dma("sp", [(dst, src_)], R=[m_], owner=m_)

            for d in range(2):
                order = list(range(len(tiles)))
                if d == 1:
                    order = [0] + order[:0:-1]
                k_ = 0
                while k_ < len(order):
                    gens = [tile_gen(d, k_ + j_, order[k_ + j_]) for j_ in range(min(2, len(order) - k_))]
                    for stage_ in range(4):
                        for g_ in gens:
                            next(g_, None)
                    k_ += len(gens)
        S.barrier()
        S.flush()


def phase_A3(S, P, zsc, cc_in_lat, cc_in_ctx):
    I = P.ins
    NL = 1.0 / np.sqrt(float(SL) * 128.0)
    NC_ = 1.0 / np.sqrt(float(SC) * 128.0)
    with contextlib.ExitStack() as st:
        S.stack = st
        ps = [S.psum("psA3_%d" % i, [128, 512], F32) for i in range(8)]
        def ld(name, shape, src, dt=F32):
            t = S.sbuf(name, shape, dt)
            S.dma("sp", [(t[:], src)], W=[t])
            return t
        stg = S.sbuf("stg", [128, 512], F32)
        def ldbf(name, shape, src):
            n = int(np.prod(shape[1:]))
            fv = stg[:, 0:n]
            if len(shape) == 3:
                fv = fv.rearrange("p (a b) -> p a b", a=shape[1])
            S.dma("sp", [(fv, src)], W=[stg])
            b = S.sbuf(name, shape, BF16)
            S.op("dve", lambda e: e.tensor_copy(out=b[:], in_=fv), R=[stg], W=[b])
            return b
        c128f = ld("c128f", [128, 128], I["c128"].ap())
        s128nf = ld("s128nf", [128, 128], I["s128n"].ap())
        fwf = ld("fwf", [128, 128], I["fw"].ap())
        fb = ld("fbb", [128, 1], I["fb"].ap())
        c128 = ldbf("c128b", [128, 128], I["c128"].ap())
        s128 = ldbf("s128b", [128, 128], I["s128"].ap())
        cs1 = ldbf("cs1", [128, 256], I["cs1"].ap())
        cs2 = ldbf("cs2", [128, 256], I["cs2"].ap())
        tt1 = ld("tt1", [128, 256], I["tt1"].ap())
        tt2 = ld("tt2", [128, 256], I["tt2"].ap())
        c256 = ldbf("c256", [128, 2, 256], I["c256"].ap())
        s256 = ldbf("s256", [128, 2, 256], I["s256"].ap())
        mcat = S.sbuf("mcat", [128, 256], BF16)
        S.op("pe", lambda e: e.matmul(ps[0][:, 0:128], lhsT=c128f[:], rhs=fwf[:], start=True, stop=True), R=[c128f, fwf], W=[ps[0]])
        S.op("pe", lambda e: e.matmul(ps[0][:, 128:256], lhsT=s128nf[:], rhs=fwf[:], start=True, stop=True), R=[s128nf, fwf], W=[ps[0]])
        S.op("dve", lambda e: e.tensor_copy(out=mcat[:], in_=ps[0][:, 0:256]), R=[ps[0]], W=[mcat])
        zf = S.sbuf("zf", [128, TA], BF16)
        S.dma("sp", [(zf[:], zsc.ap()[2])], W=[zf])
        gf = S.sbuf("gf", [128, TA], BF16)
        S.dma("sp", [(gf[:], zsc.ap()[5])], W=[gf])
        pc = S.sbuf("pc", [128, 2, 256], BF16)
        for lc in range(2):
            S.op("pe", lambda e, lc=lc: e.matmul(ps[1][:, lc * 256:(lc + 1) * 256], lhsT=zf[:, lc * 128:(lc + 1) * 128], rhs=mcat[:],
                                                 start=True, stop=True), R=[zf, mcat], W=[ps[1]])
        S.op("dve", lambda e: e.tensor_copy(out=pc[:].rearrange("p a b -> p (a b)"), in_=ps[1][:, :]), R=[ps[1]], W=[pc])
        n = 0
        for lc in range(2):
            for (half, tab) in ((0, c256), (1, s256)):
                S.op("pe", lambda e, lc=lc, half=half, tab=tab, n=n: e.matmul(
                    ps[2][:, 0:256], lhsT=pc[:, lc, half * 128:(half + 1) * 128], rhs=tab[:, lc, :],
                    start=(n == 0), stop=(n == 3)), R=[pc, tab], W=[ps[2]])
                n += 1
        tc_ = S.sbuf("tc_", [128, 256], F32)
        S.op("act", lambda e: e.activation(out=tc_[:], in_=ps[2][:, 0:256], func=AF.Identity, bias=fb[:, 0:1], scale=NC_),
             R=[ps[2], fb], W=[tc_])
        gfc = S.view(gf.t, "gfc")
        S.op("dve", lambda e: e.tensor_tensor(out=gf[:, 0:SC], in0=tc_[:], in1=gf[:, 0:SC], op=ALU.mult), R=[tc_, gf], W=[gf])
        X = S.sbuf("X", [128, 128, 128], BF16)
        Bp = S.sbuf("Bp", [128, 2, 128, 128], BF16)
        zl = zf[:, SC:TA].rearrange("p (a b) -> p a b", b=128)
        mch = S.sbuf("mch", [128, 2, 2, 64], BF16)
        S.op("dve", lambda e: e.tensor_copy(
            out=mch[:], in_=ps[0][:, 0:256].rearrange("p (ri jh jj) -> p jh ri jj", ri=2, jh=2)), R=[ps[0]], W=[mch])
        t1 = [S.sbuf("t1_%d" % i, [128, 256], F32) for i in range(2)]
        t2 = [S.sbuf("t2_%d" % i, [128, 256], F32) for i in range(2)]
        psP = ps[0:2]
        psA = ps[2:6]
        for jh in range(2):
            for l2 in range(0, 128, 4):
                pb = psP[(l2 // 4) % 2]
                for q in range(4):
                    S.op("pe", lambda e, pb=pb, q=q, l2=l2, jh=jh: e.matmul(
                        pb[:, q * 128:(q + 1) * 128], lhsT=zl[:, :, l2 + q],
                        rhs=mch[:, jh, :, :].rearrange("p a b -> p (a b)"), start=True, stop=True),
                        R=[zf, mch], W=[pb])
                if (l2 // 4) % 2 == 0:
                    S.op("act", lambda e, pb=pb, l2=l2: e.activation(
                        out=X[:, l2:l2 + 4, :].rearrange("p a b -> p (a b)"), in_=pb[:, :], func=AF.Copy), R=[pb], W=[X])
                else:
                    S.op("dve", lambda e, pb=pb, l2=l2: e.tensor_copy(
                        out=X[:, l2:l2 + 4, :].rearrange("p a b -> p (a b)"), in_=pb[:, :]), R=[pb], W=[X])
            for jj in range(64):
                j = jh * 64 + jj
                pb = psA[j % 4]
                S.op("pe", lambda e, pb=pb, jj=jj: e.matmul(pb[:, 0:256], lhsT=X[:, :, jj], rhs=cs1[:], start=True, stop=False),
                     R=[X, cs1], W=[pb])
                S.op("pe", lambda e, pb=pb, jj=jj: e.matmul(pb[:, 0:256], lhsT=X[:, :, 64 + jj], rhs=cs2[:], start=False, stop=True),
                     R=[X, cs2], W=[pb])
                a1, a2 = t1[j % 2], t2[j % 2]
                S.op("dve", lambda e, pb=pb, a1=a1: e.tensor_tensor(out=a1[:], in0=pb[:, 0:256], in1=tt1[:], op=ALU.mult),
                     R=[pb, tt1], W=[a1])
                S.op("dve", lambda e, pb=pb, a2=a2: e.tensor_tensor(
                    out=a2[:].rearrange("p (h k) -> p h k", h=2),
                    in0=pb[:, 0:256].rearrange("p (h k) -> p h k", h=2)[:, ::-1, :],
                    in1=tt2[:].rearrange("p (h k) -> p h k", h=2), op=ALU.mult), R=[pb, tt2], W=[a2])
                S.op("pool", lambda e, a1=a1, a2=a2, j=j: e.tensor_tensor(
                    out=Bp[:, :, :, j], in0=a1[:].rearrange("p (h k) -> p h k", h=2),
                    in1=a2[:].rearrange("p (h k) -> p h k", h=2), op=ALU.add), R=[a1, a2], W=[Bp])
        tb = [S.sbuf("tb%d" % i, [128, 4, 128], F32) for i in range(2)]
        psB = ps[6:8]
        gl = gf[:, SC:TA].rearrange("p (k2 k1) -> p k1 k2", k1=128)
        for k1 in range(0, 128, 4):
            pb = psB[(k1 // 4) % 2]
            for q in range(4):
                S.op("pe", lambda e, pb=pb, q=q, k1=k1: e.matmul(
                    pb[:, q * 128:(q + 1) * 128], lhsT=Bp[:, 0, k1 + q, :], rhs=c128[:], start=True, stop=False),
                    R=[Bp, c128], W=[pb])
                S.op("pe", lambda e, pb=pb, q=q, k1=k1: e.matmul(
                    pb[:, q * 128:(q + 1) * 128], lhsT=Bp[:, 1, k1 + q, :], rhs=s128[:], start=False, stop=True),
                    R=[Bp, s128], W=[pb])
            tt = tb[(k1 // 4) % 2]
            S.op("act", lambda e, pb=pb, tt=tt: e.activation(
                out=tt[:].rearrange("p a b -> p (a b)"), in_=pb[:, :], func=AF.Identity, bias=fb[:, 0:1], scale=NL),
                R=[pb, fb], W=[tt])
            S.op("dve", lambda e, tt=tt, k1=k1: e.tensor_tensor(
                out=gl[:, k1:k1 + 4, :], in0=tt[:], in1=gl[:, k1:k1 + 4, :], op=ALU.mult), R=[tt, gf], W=[gf])
        prs = [(cc_in_ctx.ap()[256:384, :], gf[:, 0:SC])]
        for i in range(16):
            prs.append((cc_in_lat.ap()[i].rearrange("p (b c t) -> p b c t", b=2, c=3)[:, :, 2, :],
                        gf[:, SC + i * 1024:SC + (i + 1) * 1024].rearrange("p (b t) -> p b t", t=512)))
        S.dma("sp", prs, R=[gf], owner=gf)
        S.barrier()
        S.flush()


def ln_epilogue(S, name, yps, resid, gate_row, lng, lnb, out_t, tmp, small):
    t = tmp
    for hf in range(2):
        S.op("dve", lambda e, hf=hf: e.tensor_tensor(out=t[:, hf * 512:(hf + 1) * 512], in0=yps[hf][:, :],
                                                     in1=gate_row[:, hf * 512:(hf + 1) * 512], op=ALU.mult),
             R=[yps[hf], gate_row.b], W=[t.b])
    S.op("dve", lambda e: e.scalar_tensor_tensor(out=t[:, :], in0=resid[:, :], scalar=ALPHA, in1=t[:, :], op0=ALU.mult, op1=ALU.add),
         R=[resid.b, t.b], W=[t.b])
    st6, mv, rs = small
    for hf in range(2):
        S.op("dve", lambda e, hf=hf: e.bn_stats(out=st6[:, hf, :], in_=t[:, hf * 512:(hf + 1) * 512]), R=[t.b], W=[st6])
    S.op("dve", lambda e: e.bn_aggr(out=mv[:, :], in_=st6[:].rearrange("p a b -> p (a b)")), R=[st6], W=[mv])
    S.op("act", lambda e: e.activation(out=rs[:, :], in_=mv[:, 1:2], func=AF.Sqrt, bias=LN_EPS, scale=1.0), R=[mv], W=[rs])
    S.op("dve", lambda e: e.reciprocal(out=rs[:, :], in_=rs[:, :]), R=[rs], W=[rs])
    S.op("dve", lambda e: e.tensor_scalar(out=t[:, :], in0=t[:, :], scalar1=mv[:, 0:1], scalar2=rs[:, 0:1],
                                          op0=ALU.subtract, op1=ALU.mult), R=[t.b, mv, rs], W=[t.b])
    S.op("pool", lambda e: e.tensor_tensor(out=t[:, :], in0=t[:, :], in1=lng[:, :], op=ALU.mult), R=[t.b, lng], W=[t.b])
    S.op("pool", lambda e: e.tensor_tensor(out=out_t[:, :], in0=t[:, :], in1=lnb[:, :], op=ALU.add), R=[t.b, lnb], W=[out_t.b])


class V:
    def __init__(self, ap_fn, b):
        self.f = ap_fn
        self.b = b

    def __getitem__(self, idx):
        return self.f()[idx]


def phase_B(S, P, cc, h1sc, gsc1, qsc, cc2_in, gbc1):
    I = P.ins
    cc_in_lat, cc_in_ctx, cc_out_lat, cc_out_ctx = cc
    ccoB = S.view(cc_out_lat, "ccol"); ccocB = S.view(cc_out_ctx, "ccoc")
    RG = [[0, 1, 2, 3], [4, 5, 6, 7]]
    S.special("pool", lambda e: e.collective_compute("AllGather", ALU.bypass, replica_groups=RG,
                                                     ins=[cc_in_ctx.ap()], outs=[cc_out_ctx.ap()]), 1, W=[ccocB])
    for i in range(16):
        S.special("pool", lambda e, i=i: e.collective_compute("AllGather", ALU.bypass, replica_groups=RG,
                                                            ins=[cc_in_lat.ap()[i]], outs=[cc_out_lat.ap()[i]]), 1, W=[ccoB])
    with contextlib.ExitStack() as stB:
        S.stack = stB
        def ld(name, shape, src, dt=F32, q="sp"):
            t = S.sbuf(name, shape, dt)
            S.dma(q, [(t[:], src)], W=[t])
            return t
        mod0 = S.sbuf("mod0B", [128, 16, 2], F32)
        gbc0 = S.sbuf("gbc0B", [128, 2, D], F32)
        mod1 = S.sbuf("mod1B", [128, 16, 2], F32)
        gbc1t = S.sbuf("gbc1B", [128, 2, D], F32)
        wo = S.sbuf("wo0", [128, 12, D], BF16)
        w1 = S.sbuf("w1", [128, 8, 1440], BF16)
        wkr = S.sbuf("wkr", [128, 8, 96], BF16)
        wkrp = S.sbuf("wkrp", [128, 8, 96], BF16)
        wuq = S.sbuf("wuq", [128, 2, 1536], BF16)
        wuqp = S.sbuf("wuqp", [128, 2, 1536], BF16)
        ident = ld("identB", [128, 128], I["ident"].ap())
        with contextlib.ExitStack() as stp:
            S.stack = stp
            psp = [S.psum("psBp_%d" % i, [128, 512], F32) for i in range(3)]
            const = adaln_setup(S, P, psp)
            adaln(S, P, 0, const, mod0, gbc0, q2="act")
            adaln(S, P, 1, const, mod1, gbc1t, q2="act")
            S.dma("sp", [(gbc1.ap(), gbc1t[:, 0, :])], R=[gbc1t], owner=gbc1t)
            stg = const["adawblk"]
            def ldw(w, nk, ncol, src2d, eng="dve"):
                for c0 in range(0, ncol, 256):
                    n = min(256, ncol - c0)
                    sg = stg[(c0 // 256) % 2]
                    S.dma("sp" if (c0 // 256) % 2 == 0 else "act",
                          [(sg[:, 0:nk, 0:n], src2d[:, c0:c0 + n].rearrange("(k p) c -> p k c", p=128))], W=[sg])
                    if eng == "act":
                        S.op("act", lambda e, w=w, sg=sg, c0=c0, n=n: e.activation(out=w[:, :, c0:c0 + n], in_=sg[:, 0:nk, 0:n], func=AF.Copy),
                             R=[sg], W=[w])
                    else:
                        S.op(eng, lambda e, w=w, sg=sg, c0=c0, n=n: e.tensor_copy(out=w[:, :, c0:c0 + n], in_=sg[:, 0:nk, 0:n]), R=[sg], W=[w])
            ldw(wo, 12, D, I["wout0"].ap()) if False else None
            for c0 in range(0, D, 256):
                for kh in range(2):
                    sg = stg[(c0 // 256 + kh) % 2]
                    S.dma("sp" if kh == 0 else "act",
                          [(sg[:, 0:6, :], I["wout0"].ap()[kh * 768:(kh + 1) * 768, c0:c0 + 256].rearrange("(k p) c -> p k c", p=128))], W=[sg])
                    if kh == 0:
                        S.op("dve", lambda e, sg=sg, c0=c0, kh=kh: e.tensor_copy(
                            out=wo[:, kh * 6:(kh + 1) * 6, c0:c0 + 256], in_=sg[:, 0:6, :]), R=[sg], W=[wo])
                    else:
                        S.op("act", lambda e, sg=sg, c0=c0, kh=kh: e.activation(
                            out=wo[:, kh * 6:(kh + 1) * 6, c0:c0 + 256], in_=sg[:, 0:6, :], func=AF.Copy), R=[sg], W=[wo])
            ldw(w1, 8, 1440, I["win1"].ap(), eng="act")
            ldw(wkr, 8, 96, I["wkr"].ap())
            ldw(wkrp, 8, 96, I["wkrp"].ap())
            ldw(wuq, 2, 1536, I["wuq"].ap(), eng="act")
            ldw(wuqp, 2, 1536, I["wuqp"].ap())
            S.barrier()
            S.flush()
        S.stack = stB
        ps = [S.psum("psB_%d" % i, [128, 512], F32) for i in range(8)]
        lng0 = ld("lng0", [128, D], bcast_rows(I["ln_g"], 0, D))
        lnb0 = ld("lnb0", [128, D], bcast_rows(I["ln_b"], 0, D), q="pool")
        qng = ld("qng", [128, 2], I["qng"].ap())
        kvg = ld("kvg", [128, 1], I["kvg"].ap())
        ropec = ld("ropec", [96, TOK], I["ropec"].ap(), dt=BF16)
        ropes = ld("ropes", [96, TOK], I["ropes"].ap(), dt=BF16, q="pool")
        ones = S.sbuf("onesB", [128, 128], BF16)
        S.op("pool", lambda e: e.memset(ones[:], 1.0), W=[ones])
        idxt = []
        S.group_begin()
        for i in range(32):
            idxt.append(ld("idx%d" % i, [128, 1], I["idx"].ap()[i], dt=I32, q="pool"))
        S.group_end()
        mblk = [S.sbuf("mblk%d" % i, [128, 12, 512], BF16) for i in range(2)]
        xres = [S.sbuf("xres%d" % i, [128, D], F32) for i in range(2)]
        h1t = [S.sbuf("h1t%d" % i, [128, D], F32) for i in range(2)]
        st6 = [S.sbuf("st6_%d" % i, [128, 2, 6], F32) for i in range(2)]
        mv = [S.sbuf("mv%d" % i, [128, 2], F32) for i in range(2)]
        rs = [S.sbuf("rs%d" % i, [128, 1], F32) for i in range(2)]
        u1t = [S.sbuf("u1T%d" % i, [128, 8, 512], BF16) for i in range(2)]
        u1B = [[S.view(u1t[i].t, "u1_%d_%d" % (i, j)) for j in range(4)] for i in range(2)]
        qcs = S.sbuf("qcs", [128, 3, 512], F32)
        sqs = S.sbuf("sqs", [128, 3, 512], BF16)
        rsq = S.sbuf("rsq", [128, 2, 512], F32)
        qnT = S.sbuf("qnT", [128, 2, 512], BF16)
        kvo = [S.sbuf("kvo%d" % i, [128, 512], BF16) for i in range(1)]
        kro = [S.sbuf("kro%d" % i, [96, 512], BF16) for i in range(1)]
        krt = S.sbuf("krt", [96, 2, 512], F32)
        gso = [S.sbuf("gso%d" % i, [128, 8, 512], BF16) for i in range(1)]
        qo = [S.sbuf("qo%d" % i, [96, 512], BF16) for i in range(2)]
        qrts = [S.sbuf("qrt%d" % i, [96, 2, 512], F32) for i in range(2)]
        rows_lat = cc_out_lat.ap().rearrange("i q (b x) -> (i q b) x", x=1536)
        psY = [ps[0:2], ps[0:2]]
        psT = ps[2:4]
        psW = ps[4:8]

        def proj_block(bi, ntok, tok0_own, is_ctx):
            u = u1t[bi % 2]
            uB = u1B[bi % 2]
            cnt = [0]
            def mm(wt, c0, ncol, pb):
                for k in range(8):
                    S.op("pe", lambda e, k=k: e.matmul(pb[0:ncol, 0:ntok], lhsT=wt[:, k, c0:c0 + ncol], rhs=u[:, k, 0:ntok],
                                                       start=(k == 0), stop=(k == 7)), R=[wt] + uB, W=[pb])
            def nextps():
                cnt[0] += 1
                return psW[cnt[0] % 4]
            pb = nextps(); mm(w1, 256, 128, pb)
            S.op("act", lambda e, pb=pb: e.activation(out=qcs[:, 2, 0:ntok], in_=pb[:, 0:ntok], func=AF.Copy), R=[pb], W=[qcs])
            S.op("act", lambda e, pb=pb: e.activation(out=sqs[:, 2, 0:ntok], in_=pb[:, 0:ntok], func=AF.Square), R=[pb], W=[sqs])
            pb = nextps()
            S.op("pe", lambda e, pb=pb: e.matmul(pb[:, 0:ntok], lhsT=ones[:], rhs=sqs[:, 2, 0:ntok], start=True, stop=True), R=[ones, sqs], W=[pb])
            S.op("act", lambda e, pb=pb: e.activation(out=rsq[:, 1, 0:ntok], in_=pb[:, 0:ntok], func=AF.Sqrt, bias=RMS_EPS, scale=1.0 / 128.0), R=[pb], W=[rsq])
            S.op("dve", lambda e: e.reciprocal(out=rsq[:, 1, 0:ntok], in_=rsq[:, 1, 0:ntok]), R=[rsq], W=[rsq])
            ko = kvo[0]
            S.op("dve", lambda e: e.scalar_tensor_tensor(out=ko[:, 0:ntok], in0=qcs[:, 2, 0:ntok], scalar=kvg[:, 0:1], in1=rsq[:, 1, 0:ntok],
                                                         op0=ALU.mult, op1=ALU.mult), R=[qcs, kvg, rsq], W=[ko])
            yield
            pb = nextps(); mm(wkr, 0, 96, pb)
            kr_ = kro[0]
            if is_ctx:
                S.op("dve", lambda e, pb=pb: e.tensor_copy(out=kr_[64:96, 0:ntok], in_=pb[64:96, 0:ntok]), R=[pb], W=[kr_])
            else:
                pb2 = nextps(); mm(wkrp, 0, 96, pb2)
                S.op("dve", lambda e, pb=pb: e.tensor_tensor(out=krt[64:96, 0, 0:ntok], in0=pb[64:96, 0:ntok],
                                                             in1=ropec[64:96, tok0_own:tok0_own + ntok], op=ALU.mult), R=[pb, ropec], W=[krt])
                S.op("dve", lambda e, pb2=pb2: e.tensor_tensor(out=krt[64:96, 1, 0:ntok], in0=pb2[64:96, 0:ntok],
                                                               in1=ropes[64:96, tok0_own:tok0_own + ntok], op=ALU.mult), R=[pb2, ropes, krt], W=[krt])
                S.op("pool", lambda e: e.tensor_tensor(out=kr_[64:96, 0:ntok], in0=krt[64:96, 0, 0:ntok], in1=krt[64:96, 1, 0:ntok], op=ALU.add),
                     R=[krt], W=[kr_])
            if is_ctx:
                dk = cc2_in[0].ap()[:, 0:ntok]
            else:
                dk = cc2_in[1].ap()[tok0_own // 2048][:, tok0_own % 2048:tok0_own % 2048 + ntok]
            S.dma("sp", [(dk[0:128, :], ko[:, 0:ntok])], R=[ko], owner=ko)
            S.dma("sp", [(dk[128:160, :], kr_[64:96, 0:ntok])], R=[kr_], owner=kr_)
            yield
            if is_ctx:
                return
            go = gso[0]
            for c in range(8):
                pb = nextps(); mm(w1, 416 + c * 128, 128, pb)
                S.op("act", lambda e, pb=pb, c=c: e.activation(out=go[:, c, 0:ntok], in_=pb[:, 0:ntok], func=AF.Silu), R=[pb], W=[go])
                yield
            S.dma("pool", [(gsc1.ap()[:, :, tok0_own:tok0_own + ntok].rearrange("c p t -> p c t"), go[:, :, 0:ntok])], R=[go], owner=go)
            for c in range(2):
                pb = nextps(); mm(w1, c * 128, 128, pb)
                S.op("act", lambda e, pb=pb, c=c: e.activation(out=qcs[:, c, 0:ntok], in_=pb[:, 0:ntok], func=AF.Copy), R=[pb], W=[qcs])
                S.op("act", lambda e, pb=pb, c=c: e.activation(out=sqs[:, c, 0:ntok], in_=pb[:, 0:ntok], func=AF.Square), R=[pb], W=[sqs])
            pb = nextps()
            for c in range(2):
                S.op("pe", lambda e, pb=pb, c=c: e.matmul(pb[:, 0:ntok], lhsT=ones[:], rhs=sqs[:, c, 0:ntok], start=(c == 0), stop=(c == 1)),
                     R=[ones, sqs], W=[pb])
            S.op("act", lambda e, pb=pb: e.activation(out=rsq[:, 0, 0:ntok], in_=pb[:, 0:ntok], func=AF.Sqrt, bias=RMS_EPS, scale=1.0 / 256.0), R=[pb], W=[rsq])
            S.op("dve", lambda e: e.reciprocal(out=rsq[:, 0, 0:ntok], in_=rsq[:, 0, 0:ntok]), R=[rsq], W=[rsq])
            for c in range(2):
                S.op("dve", lambda e, c=c: e.scalar_tensor_tensor(out=qnT[:, c, 0:ntok], in0=qcs[:, c, 0:ntok], scalar=qng[:, c:c + 1],
                                                                  in1=rsq[:, 0, 0:ntok], op0=ALU.mult, op1=ALU.mult), R=[qcs, qng, rsq], W=[qnT])
            yield
            for h in range(NH):
                pa = nextps()
                for c in range(2):
                    S.op("pe", lambda e, pa=pa, c=c, h=h: e.matmul(pa[0:96, 0:ntok], lhsT=wuq[:, c, h * 96:(h + 1) * 96], rhs=qnT[:, c, 0:ntok],
                                                                 start=(c == 0), stop=(c == 1)), R=[wuq, qnT], W=[pa])
                pp = nextps()
                for c in range(2):
                    S.op("pe", lambda e, pp=pp, c=c, h=h: e.matmul(pp[0:96, 0:ntok], lhsT=wuqp[:, c, h * 96:(h + 1) * 96], rhs=qnT[:, c, 0:ntok],
                                                                 start=(c == 0), stop=(c == 1)), R=[wuqp, qnT], W=[pp])
                q_ = qo[h % 2]
                qrt = qrts[h % 2]
                S.op("act", lambda e, pa=pa, q_=q_: e.activation(out=q_[0:64, 0:ntok], in_=pa[0:64, 0:ntok], func=AF.Copy), R=[pa], W=[q_])
                S.op("dve", lambda e, pa=pa, qrt=qrt: e.tensor_tensor(out=qrt[64:96, 0, 0:ntok], in0=pa[64:96, 0:ntok],
                                                             in1=ropec[64:96, tok0_own:tok0_own + ntok], op=ALU.mult), R=[pa, ropec], W=[qrt])
                S.op("dve", lambda e, pp=pp, qrt=qrt: e.tensor_tensor(out=qrt[64:96, 1, 0:ntok], in0=pp[64:96, 0:ntok],
                                                             in1=ropes[64:96, tok0_own:tok0_own + ntok], op=ALU.mult), R=[pp, ropes, qrt], W=[qrt])
                S.op("pool", lambda e, q_=q_, qrt=qrt: e.tensor_tensor(out=q_[64:96, 0:ntok], in0=qrt[64:96, 0, 0:ntok], in1=qrt[64:96, 1, 0:ntok], op=ALU.add),
                     R=[qrt, q_], W=[q_])
                S.dma("sp" if h % 2 == 0 else "pool", [(qsc.ap()[h, :, tok0_own:tok0_own + ntok], q_[0:96, 0:ntok])], R=[q_], owner=q_)
                yield

        mctx = S.sbuf("mctx", [128, 12, SC], BF16)
        S.dma("sp", [(mctx[:], cc_out_ctx.ap().rearrange("(k p) t -> p k t", p=128))], R=[ccocB], W=[mctx])
        ntile = 2 + TOK // 128

        def b_info(ti):
            is_ctx = ti < 2
            if is_ctx:
                return is_ctx, 1, mctx, ti * 128, I["ctx"].ap()[ti * 128:(ti + 1) * 128, :]
            li = ti - 2
            return is_ctx, 0, mblk[(li // 4) % 2], (li % 4) * 128, I["x_own"].ap()[li * 128:(li + 1) * 128, :]

        def b_gather(blk):
            mb = mblk[blk % 2]
            for r in range(4):
                S.special("pool", lambda e, r=r: e.indirect_dma_start(
                    out=mb[:, 3 * r:3 * r + 3, :].rearrange("p a b -> p (a b)"), out_offset=None, in_=rows_lat,
                    in_offset=bass.IndirectOffsetOnAxis(ap=idxt[blk * 4 + r][:, :], axis=0),
                    bounds_check=16 * 512 * 2 - 1, oob_is_err=False), 16, R=[ccoB, idxt[blk * 4 + r]], W=[mb])

        def b_y(ti):
            is_ctx, j, msrc, mcol, xsrc = b_info(ti)
            if ti == 0:
                b_gather(0)
                b_gather(1)
            if not is_ctx and (ti - 2) % 4 == 0:
                blk = (ti - 2) // 4
                if 1 <= blk and blk + 1 < TOK // 512:
                    b_gather(blk + 1)
            xr = xres[ti % 2]
            S.dma("sp", [(xr[:], xsrc)], W=[xr])
            yps = psY[ti % 2]
            for hf in range(2):
                for kc in range(12):
                    S.op("pe", lambda e, hf=hf, kc=kc: e.matmul(
                        yps[hf][:, :], lhsT=msrc[:, kc, mcol:mcol + 128], rhs=wo[:, kc, hf * 512:(hf + 1) * 512],
                        start=(kc == 0), stop=(kc == 11)), R=[msrc, wo], W=[yps[hf]])

        def b_ep(ti):
            is_ctx, j, msrc, mcol, xsrc = b_info(ti)
            xr, yps = xres[ti % 2], psY[ti % 2]
            hh = h1t[ti % 2]
            hv = V(lambda: hh[:, :], hh)
            ln_epilogue(S, "l0", yps, V(lambda: xr[:, :], xr), V(lambda: gbc0[:, j, :], gbc0), lng0, lnb0,
                        hv, hv, (st6[ti % 2], mv[ti % 2], rs[ti % 2]))
            if not is_ctx:
                S.dma("pool", [(h1sc.ap()[(ti - 2) * 128:(ti - 1) * 128, :], hh[:])], R=[hh], owner=hh)

        def b_tr(ti):
            is_ctx, j, msrc, mcol, xsrc = b_info(ti)
            hh = h1t[ti % 2]
            bi = 0 if is_ctx else 1 + (ti - 2) // 4
            sub = ti if is_ctx else (ti - 2) % 4
            u = u1t[bi % 2]
            for k in range(8):
                pb = psT[k % 2]
                S.op("pe", lambda e, pb=pb, k=k: e.transpose(out=pb[:, 0:128], in_=hh[:, k * 128:(k + 1) * 128], identity=ident[:]),
                     R=[hh, ident], W=[pb])
                S.op("act", lambda e, pb=pb, k=k: e.activation(
                    out=u[:, k, sub * 128:(sub + 1) * 128], in_=pb[:, 0:128], func=AF.Identity,
                    scale=mod1[:, 8 + k, j:j + 1], bias=mod1[:, k, j:j + 1]), R=[pb, mod1], W=[u1B[bi % 2][sub]])
            if is_ctx and ti == 1:
                pgen.append(proj_block(0, 256, 0, True))
            elif (not is_ctx) and sub == 3:
                pgen.append(proj_block(bi, 512, ((ti - 2) // 4) * 512, False))

        pgen = []

        def pump(k):
            while k > 0 and pgen:
                try:
                    next(pgen[0])
                    k -= 1
                except StopIteration:
                    pgen.pop(0)

        b_y(0)
        for ti in range(ntile):
            b_ep(ti)
            pump(3)
            if ti + 1 < ntile:
                b_y(ti + 1)
            pump(3)
            b_tr(ti)
            pump(3)
        pump(1000)
        S.barrier()
        S.flush()


def phase_C(S, P, cc2_in, cc2_out, qsc, gsc1, osc):
    I = P.ins
    NKT = TA // 128
    RG = [[0, 1, 2, 3], [4, 5, 6, 7]]
    c2o = S.view(cc2_out[0], "cc2o")
    S.special("pool", lambda e: e.collective_compute("AllGather", ALU.bypass, replica_groups=RG,
                                                     ins=[cc2_in[0].ap()], outs=[cc2_out[0].ap()]), 1, W=[c2o])
    for i in range(2):
        S.special("pool", lambda e, i=i: e.collective_compute("AllGather", ALU.bypass, replica_groups=RG,
                                                            ins=[cc2_in[1].ap()[i]], outs=[cc2_out[1].ap()[i]]), 1, W=[c2o])
    with contextlib.ExitStack() as st:
        S.stack = st
        psS = [S.psum("psS%d" % i, [128, 1024], F32) for i in range(3)]
        psO = [S.psum("psO%d" % i, [128, 512], F32) for i in range(2)]
        kvn = S.sbuf("kvn", [128, TA], BF16)
        KTt = [S.sbuf("KT%d" % i, [128, TA], BF16).t for i in range(2)]
        KTn = [S.view(KTt[i], "KTn%d" % i) for i in range(2)]
        KTr = [S.view(KTt[i], "KTr%d" % i) for i in range(2)]
        Va = [S.sbuf("Va%d" % i, [128, NKT, 128], BF16) for i in range(2)]
        qT = [S.sbuf("qT%d" % i, [128, TOK], BF16) for i in range(2)]
        pk = [(kvn[:, 0:SC], cc2_out[0].ap()[0:128, :])]
        pr = [[(KTt[i][64:96, 0:SC], cc2_out[0].ap()[128:160, :])] for i in range(2)]
        for r in range(4):
            for hf in range(2):
                c0 = SC + r * TOK + hf * 2048
                pk.append((kvn[:, c0:c0 + 2048], cc2_out[1].ap()[hf, 160 * r:160 * r + 128, :]))
                for i in range(2):
                    pr[i].append((KTt[i][64:96, c0:c0 + 2048], cc2_out[1].ap()[hf, 160 * r + 128:160 * r + 160, :]))
        S.dma("sp", pk, R=[c2o], W=[kvn])
        for i in range(2):
            S.dma("pool", pr[i], R=[c2o], W=[KTr[i]])
            S.op("pool", lambda e, i=i: e.memset(KTt[i][96:128, :], 0.0), W=[KTr[i]])
            S.op("pool", lambda e, i=i: e.memset(KTt[i][96:97, :], 1.0), W=[KTr[i]])
            S.op("pool", lambda e, i=i: e.memset(qT[i][96:128, :], 0.0), W=[qT[i]])
        S.op("pool", lambda e: e.memset(Va[0][:, :, 64:128], 1.0), W=[Va[0]])
        S.op("pool", lambda e: e.memset(Va[1][:, :, 0:64], 1.0), W=[Va[1]])
        wst = S.sbuf("wukvf", [128, 512], F32)
        wukv = S.sbuf("wukv", [128, 2048], BF16)
        for c0 in range(0, 2048, 512):
            S.dma("sp", [(wst[:], I["wukv"].ap()[:, c0:c0 + 512])], W=[wst])
            S.op("dve", lambda e, c0=c0: e.tensor_copy(out=wukv[:, c0:c0 + 512], in_=wst[:]), R=[wst], W=[wukv])
        ones = S.sbuf("onesC", [96, 128], BF16)
        S.op("pool", lambda e: e.memset(ones[:], 1.0), W=[ones])
        PT = [S.sbuf("PT%d" % i, [128, 1024], BF16) for i in range(3)]
        sq = [S.sbuf("sqC%d" % i, [96, 512], BF16) for i in range(2)]
        mx = S.sbuf("mxC", [128, 1], F32)
        qm = [S.sbuf("qmC%d" % i, [128, 1], F32) for i in range(2)]
        km = [S.sbuf("kmC%d" % i, [128, 1], F32) for i in range(2)]
        negc = [S.sbuf("negc%d" % i, [128, 1], F32) for i in range(2)]
        ot = [S.sbuf("otC%d" % i, [128, 512], F32) for i in range(2)]
        dn = [S.sbuf("dnC%d" % i, [128, 512], F32) for i in range(2)]
        gt = [S.sbuf("gtC%d" % i, [128, 512], BF16) for i in range(2)]
        oo = [S.sbuf("ooC%d" % i, [128, 512], BF16) for i in range(2)]
        cnt = [0]

        def build_units(h, split=False):
            par = h % 2
            v0 = 0 if par == 0 else 64
            S.dma("sp", [(qT[par][0:96, :], qsc.ap()[h])], W=[qT[par]])
            S.op("pool", lambda e: e.memset(qm[par][:], 0.0), W=[qm[par]])
            S.op("pool", lambda e: e.memset(km[par][:], 0.0), W=[km[par]])
            yield
            for kb in range(0, TA, 512):
                n = min(512, TA - kb)
                pb = bank[0]
                S.op("pe", lambda e, pb=pb, kb=kb, n=n: e.matmul(pb[0:64, 0:n], lhsT=wukv[:, h * 128:h * 128 + 64], rhs=kvn[:, kb:kb + n],
                                                                 start=True, stop=True), R=[wukv, kvn], W=[pb])
                S.op("act", lambda e, pb=pb, kb=kb, n=n: e.activation(out=KTt[par][0:64, kb:kb + n], in_=pb[0:64, 0:n], func=AF.Copy),
                     R=[pb], W=[KTn[par]])
                yield
            for k0 in range(0, NKT, 8):
                nk = min(8, NKT - k0)
                pb = bank[0]
                for i in range(nk):
                    S.op("pe", lambda e, pb=pb, i=i, k0=k0: e.matmul(pb[:, i * 64:(i + 1) * 64], lhsT=kvn[:, (k0 + i) * 128:(k0 + i + 1) * 128],
                                                                    rhs=wukv[:, h * 128 + 64:h * 128 + 128], start=True, stop=True),
                         R=[wukv, kvn], W=[pb])
                if (k0 // 8) % 2 == 0:
                    S.op("dve", lambda e, pb=pb, k0=k0, nk=nk: e.tensor_copy(
                        out=Va[par][:, k0:k0 + nk, v0:v0 + 64], in_=pb[:, 0:nk * 64].rearrange("p (a b) -> p a b", b=64)), R=[pb], W=[Va[par]])
                else:
                    S.op("act", lambda e, pb=pb, k0=k0, nk=nk: e.activation(
                        out=Va[par][:, k0:k0 + nk, v0:v0 + 64], in_=pb[:, 0:nk * 64].rearrange("p (a b) -> p a b", b=64), func=AF.Copy),
                        R=[pb], W=[Va[par]])
                yield
            if not split:
                for _ in norm_units(h):
                    yield

        def norm_units(h):
            par = h % 2
            work = []
            for (src, srcB, tot, acc) in ((qT[par], [qT[par]], TOK, qm[par]), (KTt[par], [KTn[par], KTr[par]], TA, km[par])):
                for b0 in range(0, tot, 512):
                    work.append((src, srcB, min(512, tot - b0), b0, acc))

            def stage_a(u):
                src, srcB, n, b0, acc = work[u]
                s_ = sq[u % 2]
                S.op("pool", lambda e: e.tensor_tensor(out=s_[:, 0:n], in0=src[0:96, b0:b0 + n], in1=src[0:96, b0:b0 + n], op=ALU.mult),
                     R=srcB, W=[s_])

            def stage_b(u):
                src, srcB, n, b0, acc = work[u]
                s_ = sq[u % 2]
                pb = bank[0]
                S.op("pe", lambda e: e.matmul(pb[:, 0:n], lhsT=ones[:], rhs=s_[:, 0:n], start=True, stop=True), R=[ones, s_], W=[pb])
                S.op("dve", lambda e: e.tensor_reduce(out=mx[:], in_=pb[:, 0:n], op=ALU.max, axis=mybir.AxisListType.X), R=[pb], W=[mx])
                S.op("dve", lambda e: e.tensor_tensor(out=acc[:], in0=acc[:], in1=mx[:], op=ALU.max), R=[acc, mx], W=[acc])

            stage_a(0)
            yield
            for u in range(len(work)):
                if u + 1 < len(work):
                    stage_a(u + 1)
                stage_b(u)
                yield
            nb = negc[par]
            S.op("dve", lambda e: e.tensor_tensor(out=nb[:], in0=qm[par][:], in1=km[par][:], op=ALU.mult), R=[qm[par], km[par]], W=[nb])
            S.op("act", lambda e: e.activation(out=nb[:], in_=nb[:], func=AF.Sqrt), R=[nb], W=[nb])
            S.op("dve", lambda e: e.tensor_scalar(out=qT[par][96:97, :], in0=KTt[par][96:97, 0:TOK], scalar1=nb[96:97, 0:1], scalar2=-1.0,
                                                  op0=ALU.mult, op1=ALU.mult), R=[nb, KTr[par], qT[par]], W=[qT[par]])

        bank = [psO[0]]
        for i_, _ in enumerate(build_units(0)):
            bank[0] = psO[i_ % 2]
        gen = [None]
        GK = 2
        groups = [list(range(k0, min(k0 + GK, NKT))) for k0 in range(0, NKT, GK)]
        NP_ = len(groups)
        NQB = TOK // 512
        pairs = [(h, qb, kp) for h in range(NH) for qb in range(NQB) for kp in range(NP_)]
        pobuf = {}

        def emit_qk(n):
            h, qb, kp = pairs[n]
            par = h % 2
            q0 = qb * 512
            if kp == 0:
                pobuf[(h, qb)] = psO[(h * NQB + qb) % 2]
                if qb == 0 and gen[0] is not None:
                    raise RuntimeError("norm units of this head were not finished in time")
            if gen[0] is not None and qb >= 3 and 4 <= kp <= NP_ - 5 and kp % 4 == 0:
                bank[0] = psO[(h * NQB + qb + 1) % 2]
                try:
                    next(gen[0])
                except StopIteration:
                    gen[0] = None
            sp_ = psS[n % 3]
            kts = groups[kp]
            for i, kt in enumerate(kts):
                S.op("pe", lambda e, i=i, kt=kt: e.matmul(
                    sp_[:, i * 512:(i + 1) * 512], lhsT=KTt[par][:, kt * 128:(kt + 1) * 128], rhs=qT[par][:, q0:q0 + 512],
                    start=True, stop=True), R=[KTn[par], KTr[par], qT[par]], W=[sp_])
            pt = PT[n % 3]
            w = 512 * len(kts)
            S.op("act", lambda e: e.activation(out=pt[:, 0:w], in_=sp_[:, 0:w], func=AF.Exp, scale=ATTN_SCALE), R=[sp_], W=[pt])

        def emit_pv(n):
            h, qb, kp = pairs[n]
            par = h % 2
            q0 = qb * 512
            po = pobuf[(h, qb)]
            pt = PT[n % 3]
            for i, kt in enumerate(groups[kp]):
                S.op("pe", lambda e, i=i, kt=kt: e.matmul(
                    po[:, :], lhsT=Va[par][:, kt, :], rhs=pt[:, i * 512:(i + 1) * 512],
                    start=(kt == 0), stop=(kt == NKT - 1)), R=[Va[par], pt], W=[po])
            if kp != NP_ - 1:
                return
            num = slice(0, 64) if par == 0 else slice(64, 128)
            den = slice(64, 128) if par == 0 else slice(0, 64)
            o_, d_, g_, oo_ = ot[qb % 2], dn[qb % 2], gt[qb % 2], oo[qb % 2]
            S.op("dve", lambda e: e.tensor_copy(out=o_[:, :], in_=po[:, :]), R=[po], W=[o_])
            S.dma("sp", [(d_[num, :], o_[den, :])], R=[o_], W=[d_])
            S.dma("pool", [(g_[num, :], gsc1.ap()[h // 2, num, q0:q0 + 512])], W=[g_])
            S.op("dve", lambda e: e.reciprocal(out=d_[num, :], in_=d_[num, :]), R=[d_], W=[d_])
            S.op("dve", lambda e: e.tensor_tensor(out=o_[num, :], in0=o_[num, :], in1=d_[num, :], op=ALU.mult), R=[o_, d_], W=[o_])
            S.op("pool", lambda e: e.tensor_tensor(out=oo_[num, :], in0=o_[num, :], in1=g_[num, :], op=ALU.mult), R=[o_, g_], W=[oo_])
            S.dma("sp", [(osc.ap()[h // 2, num, q0:q0 + 512], oo_[num, :])], R=[oo_], owner=oo_)
            if qb == 2 and h + 1 < NH:
                par_ = psS[n % 3]
                halves = [Buf(par_.t[:, 0:512], "psSh0"), Buf(par_.t[:, 512:1024], "psSh1")]
                for hb in halves:
                    hb.lw = dict(par_.lw)
                    hb.rd = dict(par_.rd)
                banks4 = [psO[0], psO[1]] + halves
                bank[0] = banks4[0]
                for i_, _ in enumerate(build_units(h + 1, split=True)):
                    bank[0] = banks4[(i_ + 1) % 4]
                for hb in halves:
                    for dd in (hb.lw, hb.rd):
                        for k_, v_ in dd.items():
                            if par_.rd.get(k_, 0) < v_:
                                par_.rd[k_] = v_
                gen[0] = norm_units(h + 1)

        LOOK = 2
        for n in range(len(pairs) + LOOK):
            if n < len(pairs):
                emit_qk(n)
            if n - LOOK >= 0:
                emit_pv(n - LOOK)
        S.barrier()
        S.flush()


def phase_D(S, P, osc, h1sc, gbc1, out):
    I = P.ins
    with contextlib.ExitStack() as st:
        S.stack = st
        ps = [S.psum("psD_%d" % i, [128, 512], F32) for i in range(4)]
        stg = S.sbuf("stgD", [128, 8, 256], F32)
        wo = S.sbuf("wo1", [128, 8, D], BF16)
        for c0 in range(0, D, 256):
            S.dma("sp", [(stg[:], I["wout1"].ap()[:, c0:c0 + 256].rearrange("(k p) c -> p k c", p=128))], W=[stg])
            S.op("dve", lambda e, c0=c0: e.tensor_copy(out=wo[:, :, c0:c0 + 256], in_=stg[:]), R=[stg], W=[wo])
        g1 = S.sbuf("gbc1D", [128, D], F32); S.dma("sp", [(g1[:], gbc1.ap())], W=[g1])
        lng = S.sbuf("lng1", [128, D], F32); S.dma("sp", [(lng[:], bcast_rows(I["ln_g"], D, D))], W=[lng])
        lnb = S.sbuf("lnb1", [128, D], F32); S.dma("pool", [(lnb[:], bcast_rows(I["ln_b"], D, D))], W=[lnb])
        oT = S.sbuf("oT", [128, 8, TOK], BF16)
        S.dma("sp", [(oT[:], osc.ap().rearrange("c p t -> p c t"))], W=[oT])
        hres = [S.sbuf("hres%d" % i, [128, D], F32) for i in range(3)]
        tmpt = [S.sbuf("tmpD%d" % i, [128, D], F32) for i in range(2)]
        outt = [S.sbuf("outD%d" % i, [128, D], F32) for i in range(2)]
        st6 = [S.sbuf("st6D%d" % i, [128, 2, 6], F32) for i in range(2)]
        mv = [S.sbuf("mvD%d" % i, [128, 2], F32) for i in range(2)]
        rs = [S.sbuf("rsD%d" % i, [128, 1], F32) for i in range(2)]
        psY = [ps[0:2], ps[2:4]]
        def d_load(ti):
            S.dma("act", [(hres[ti % 3][:], h1sc.ap()[ti * 128:(ti + 1) * 128, :])], W=[hres[ti % 3]])

        d_load(0)
        d_load(1)
        for ti in range(TOK // 128):
            hr = hres[ti % 3]
            if ti + 2 < TOK // 128:
                d_load(ti + 2)
            yps = psY[ti % 2]
            for hf in range(2):
                for kc in range(8):
                    S.op("pe", lambda e, hf=hf, kc=kc, ti=ti, yps=yps: e.matmul(
                        yps[hf][:, :], lhsT=oT[:, kc, ti * 128:(ti + 1) * 128], rhs=wo[:, kc, hf * 512:(hf + 1) * 512],
                        start=(kc == 0), stop=(kc == 7)), R=[oT, wo], W=[yps[hf]])
            tt, ou = tmpt[ti % 2], outt[ti % 2]
            ln_epilogue(S, "l1", yps, V(lambda hr=hr: hr[:, :], hr), V(lambda: g1[:, :], g1), lng, lnb,
                        V(lambda ou=ou: ou[:, :], ou), V(lambda tt=tt: tt[:, :], tt), (st6[ti % 2], mv[ti % 2], rs[ti % 2]))
            S.dma("sp", [(out.ap()[ti * 128:(ti + 1) * 128, :], ou[:])], R=[ou], owner=ou)
        S.barrier()
        S.flush()


def build(stage="full"):
    P = Prog(stage)
    nc = P.nc
    declare_inputs(P)
    I = P.ins
    zsc = P.scratch("zsc", [6, 128, TA], BF16)
    cc_in_lat = P.scratch("cc_in_lat", [16, 128, 3072], BF16)
    cc_in_ctx = P.scratch("cc_in_ctx", [384, SC], BF16)
    cc_out_lat = P.scratch("cc_out_lat", [16, 512, 3072], BF16)
    cc_out_ctx = P.scratch("cc_out_ctx", [1536, SC], BF16)
    if stage == "A":
        dbg_lat = P.out("dbg_m_lat", [384, SL], BF16)
        dbg_ctx = P.out("dbg_m_ctx", [384, SC], BF16)
        dbg_z = P.out("dbg_z", [6, 128, TA], BF16)
        dbg_mod = P.out("dbg_mod", [128, 16, 2], F32)
        dbg_gbc = P.out("dbg_gbc", [128, 2, D], F32)

    with contextlib.ExitStack() as top:
        S = Sync(nc, top)
        gbc1 = P.scratch("gbc1sc", [128, D], F32)
        zscB = S.view(zsc, "zsc")
        ccinB = S.view(cc_in_lat, "ccin")
        ccincB = S.view(cc_in_ctx, "ccinc")

        with contextlib.ExitStack() as st:
            S.stack = st
            ps = [S.psum("psA1_%d" % i, [128, 512], F32) for i in range(6)]
            psTb = [S.psum("psA1T_%d" % i, [128, 512], BF16) for i in range(2)]
            ident = S.sbuf("ident", [128, 128], BF16)
            S.dma("pool", [(ident[:], I["ident"].ap())], W=[ident])
            const = adaln_setup(S, P, ps)
            mod0 = S.sbuf("mod0", [128, 16, 2], F32)
            gbc0 = S.sbuf("gbc0", [128, 2, D], F32)
            adaln(S, P, 0, const, mod0, gbc0)
            if stage == "A":
                S.dma("sp", [(dbg_mod.ap(), mod0[:])], R=[mod0])
                S.dma("sp", [(dbg_gbc.ap(), gbc0[:])], R=[gbc0])

            win = S.sbuf("win", [128, 8, 768], BF16)
            for hf in range(2):
                ws = S.sbuf("wst%d" % hf, [128, 8, 384], F32)
                S.dma("sp" if hf == 0 else "pool",
                      [(ws[:], I["win0"].ap()[:, hf * 384:(hf + 1) * 384].rearrange("(k p) c -> p k c", p=128))], W=[ws])
                S.op("dve" if hf == 0 else "pool", lambda e, ws=ws, hf=hf: e.tensor_copy(
                    out=win[:, :, hf * 384:(hf + 1) * 384], in_=ws[:]), R=[ws], W=[win])

            xin = [S.sbuf("xin%d" % i, [128, 4, D], BF16) for i in range(3)]
            uTt = [S.sbuf("uT%d" % i, [128, 8, 512], BF16).t for i in range(2)]
            uT = [[S.view(uTt[i], "uT%d_%d" % (i, k)) for k in range(8)] for i in range(2)]
            zstt = [S.sbuf("zst%d" % i, [128, 6, 512], BF16).t for i in range(2)]
            zst = [[S.view(zstt[i], "zst%d_%d" % (i, c)) for c in range(6)] for i in range(2)]
            psT = psTb
            psZ = ps[3:6]
            ntiles = 1 + SL // 512

            def tinfo(t):
                if t == 0:
                    return SC, 0, 1, I["ctx"].ap().rearrange("(j p) d -> p j d", p=128)
                return 512, SC + (t - 1) * 512, 0, I["x_all"].ap()[(t - 1) * 512:t * 512, :].rearrange("(j p) d -> p j d", p=128)

            def a1_load(t):
                ntok, tok0, j, src = tinfo(t)
                xi = xin[t % 3]
                S.dma("pool", [(xi[:, 0:ntok // 128, :], src)], W=[xi])

            def a1_T(t, k):
                ntok, tok0, j, src = tinfo(t)
                xi = xin[t % 3]
                bank = psT[k % 2]
                for jj in range(ntok // 128):
                    S.op("pe", lambda e, jj=jj: e.transpose(
                        out=bank[:, jj * 128:(jj + 1) * 128], in_=xi[:, jj, k * 128:(k + 1) * 128], identity=ident[:]),
                        R=[xi, ident], W=[bank])
                u = uT[t % 2][k]
                if k % 2 == 0:
                    S.op("act", lambda e: e.activation(
                        out=u[:, k, 0:ntok], in_=bank[:, 0:ntok], func=AF.Identity,
                        scale=mod0[:, 8 + k, j:j + 1], bias=mod0[:, k, j:j + 1]), R=[bank, mod0], W=[u])
                else:
                    S.op("dve", lambda e: e.tensor_scalar(
                        out=u[:, k, 0:ntok], in0=bank[:, 0:ntok], scalar1=mod0[:, 8 + k, j:j + 1],
                        scalar2=mod0[:, k, j:j + 1], op0=ALU.mult, op1=ALU.add), R=[bank, mod0], W=[u])

            def a1_Z(t, c):
                ntok, tok0, j, src = tinfo(t)
                zb = psZ[c % 3]
                for k in range(8):
                    u = uT[t % 2][k]
                    S.op("pe", lambda e, u=u, k=k: e.matmul(
                        zb[:, 0:ntok], lhsT=win[:, k, c * 128:(c + 1) * 128], rhs=u[:, k, 0:ntok],
                        start=(k == 0), stop=(k == 7)), R=[win, u], W=[zb])
                zs = zst[t % 2][c]
                if c < 3:
                    S.op("dve", lambda e: e.tensor_copy(out=zs[:, c, 0:ntok], in_=zb[:, 0:ntok]), R=[zb], W=[zs])
                else:
                    S.op("act", lambda e: e.activation(out=zs[:, c, 0:ntok], in_=zb[:, 0:ntok], func=AF.Silu), R=[zb], W=[zs])
                if c == 5:
                    zt = zst[t % 2][0].t
                    S.dma("sp" if t % 2 == 1 else "act",
                          [(zsc.ap()[:, :, tok0:tok0 + ntok].rearrange("c p t -> p c t"), zt[:, :, 0:ntok])],
                          R=zst[t % 2], W=[], owner=zst[t % 2][0])

            a1_load(0)
            a1_load(1)
            a1_load(2)
            for k in range(8):
                a1_T(0, k)
            for t in range(ntiles):
                for k in range(8):
                    if t + 1 < ntiles:
                        a1_T(t + 1, k)
                    if k < 6:
                        a1_Z(t, k)
                if t + 3 < ntiles:
                    a1_load(t + 3)
            S.barrier()
            S.flush()
        phase_A2(S, P, zsc, cc_in_lat, cc_in_ctx)
        phase_A3(S, P, zsc, cc_in_lat, cc_in_ctx)
        if stage == "A":
            with contextlib.ExitStack() as st:
                S.stack = st
                for c in range(3):
                    bt = S.sbuf("dbgm%d" % c, [128, SC], BF16)
                    S.dma("sp", [(bt[:], cc_in_ctx.ap()[c * 128:(c + 1) * 128, :])], W=[bt])
                    S.dma("sp", [(dbg_ctx.ap()[c * 128:(c + 1) * 128, :], bt[:])], R=[bt])
                    bl = S.sbuf("dbgl%d" % c, [128, SL], BF16)
                    S.dma("sp", [(bl[:, i * 1024:(i + 1) * 1024].rearrange("p (b t) -> p b t", t=512),
                                  cc_in_lat.ap()[i].rearrange("p (b c t) -> p b c t", b=2, c=3)[:, :, c, :]) for i in range(16)], W=[bl])
                    S.dma("sp", [(dbg_lat.ap()[c * 128:(c + 1) * 128, :], bl[:])], R=[bl])
                S.barrier()
                S.flush()
        if stage == "A":
            with contextlib.ExitStack() as st:
                S.stack = st
                for c in range(6):
                    bt = S.sbuf("dbgb%d" % c, [128, TA], BF16)
                    S.dma("sp", [(bt[:], zsc.ap()[c])], W=[bt])
                    S.dma("sp", [(dbg_z.ap()[c], bt[:])], R=[bt])
                S.barrier()
                S.flush()
        if stage == "A":
            return P
        S.stack = top
        h1sc = P.scratch("h1sc", [TOK, D], F32)
        gsc1 = P.scratch("gsc1", [8, 128, TOK], BF16)
        qsc = P.scratch("qsc", [NH, 96, TOK], BF16)
        cc2_in = (P.scratch("cc2c_in", [160, SC], BF16), P.scratch("cc2l_in", [2, 160, 2048], BF16))
        cc2_out = (P.scratch("cc2c_out", [640, SC], BF16), P.scratch("cc2l_out", [2, 640, 2048], BF16))
        osc = P.scratch("osc", [8, 128, TOK], BF16)
        outT = P.out("out", [TOK, D], F32)
        phase_B(S, P, (cc_in_lat, cc_in_ctx, cc_out_lat, cc_out_ctx), h1sc, gsc1, qsc, cc2_in, gbc1)
        if stage == "B":
            dbg_h1 = P.out("dbg_h1", [TOK, D], F32)
            dbg_q = P.out("dbg_q", [NH, 96, TOK], BF16)
            dbg_kv = P.out("dbg_kv", [160, SC + TOK], BF16)
            with contextlib.ExitStack() as st:
                S.stack = st
                bts = [S.sbuf("dbh%d" % i, [128, 4, D], F32) for i in range(2)]
                for i in range(TOK // 512):
                    bt = bts[i % 2]
                    S.dma("sp", [(bt[:], h1sc.ap()[i * 512:(i + 1) * 512, :].rearrange("(j p) d -> p j d", p=128))], W=[bt])
                    S.dma("sp", [(dbg_h1.ap()[i * 512:(i + 1) * 512, :].rearrange("(j p) d -> p j d", p=128), bt[:])], R=[bt])
                bqs = [S.sbuf("dbq%d" % i, [96, TOK], BF16) for i in range(2)]
                for h in range(NH):
                    bt = bqs[h % 2]
                    S.dma("sp", [(bt[:], qsc.ap()[h])], W=[bt])
                    S.dma("sp", [(dbg_q.ap()[h], bt[:])], R=[bt])
                for (r0, n) in ((0, 128), (128, 32)):
                    bt = S.sbuf("dbk%d" % r0, [n, SC + TOK], BF16)
                    S.dma("sp", [(bt[:, 0:SC], cc2_in[0].ap()[r0:r0 + n, :]),
                                 (bt[:, SC:SC + 2048], cc2_in[1].ap()[0, r0:r0 + n, :]),
                                 (bt[:, SC + 2048:SC + 4096], cc2_in[1].ap()[1, r0:r0 + n, :])], W=[bt])
                    S.dma("sp", [(dbg_kv.ap()[r0:r0 + n, :], bt[:])], R=[bt])
                S.barrier()
                S.flush()
            return P
        phase_C(S, P, cc2_in, cc2_out, qsc, gsc1, osc)
        phase_D(S, P, osc, h1sc, gbc1, outT)
        return P


def _fm(v, nchunk):
    return np.ascontiguousarray(np.asarray(v, np.float32).reshape(nchunk, 128).T)


def _consts():
    n = np.arange(128)
    ang = 2.0 * np.pi * np.outer(n, n) / 128.0
    c128 = np.cos(ang).astype(np.float32)
    s128 = np.sin(ang).astype(np.float32)
    angt = 2.0 * np.pi * np.outer(n, n) / 16384.0
    tr = np.cos(angt).astype(np.float32)
    sn = np.sin(angt).astype(np.float32)
    m = np.arange(256)
    ang256 = 2.0 * np.pi * np.outer(m, m) / 256.0
    c256 = np.cos(ang256).astype(np.float32).reshape(2, 128, 256).transpose(1, 0, 2)
    s256 = np.sin(ang256).astype(np.float32).reshape(2, 128, 256).transpose(1, 0, 2)
    return dict(
        ident=np.eye(128, dtype=np.float32), c128=c128, s128=s128, s128n=-s128,
        cs1=np.concatenate([c128, -s128], 1), cs2=np.concatenate([s128, c128], 1),
        tt1=np.concatenate([tr, tr], 1), tt2=np.concatenate([sn, -sn], 1),
        c256=np.ascontiguousarray(c256), s256=np.ascontiguousarray(s256))


def _rope_tables(tc):
    t = np.arange(TOK) + TOK * tc
    inv = (10000.0 ** (-np.arange(8, dtype=np.float32) / 8.0)).astype(np.float32)
    row = (t // 64).astype(np.float32)
    col = (t % 64).astype(np.float32)
    ar = row[None, :] * inv[:, None]
    ac = col[None, :] * inv[:, None]
    cosf = np.concatenate([np.cos(ar), np.cos(ar), np.cos(ac), np.cos(ac)], 0)
    sinf = np.concatenate([-np.sin(ar), np.sin(ar), -np.sin(ac), np.sin(ac)], 0)
    c = np.zeros((96, TOK), np.float32)
    s = np.zeros((96, TOK), np.float32)
    c[64:96] = cosf
    s[64:96] = sinf
    return c, s


_ROPE_PERM = np.concatenate([np.arange(8, 16), np.arange(0, 8), np.arange(24, 32), np.arange(16, 24)])


def prep_inputs(inp):
    f = lambda a: np.ascontiguousarray(np.asarray(a, np.float32))
    x, c, ctx, c_ctx = f(inp["x"]), f(inp["c"]), f(inp["ctx"]), f(inp["c_ctx"])
    ada_w, ada_b = f(inp["ada_w"]), f(inp["ada_b"])
    w_in_rf = f(inp["w_in_rf"])[0]
    conv_w, conv_b = f(inp["conv_w"])[0], f(inp["conv_b"])[0]
    gw, gb, lam = f(inp["lru_gate_w"])[0], f(inp["lru_gate_b"])[0], f(inp["lru_lambda"])[0]
    fnw, fnb = f(inp["fnet_w"])[0], f(inp["fnet_b"])[0]
    w_out_rf = f(inp["w_out_rf"])[0]
    w_in_mla = f(inp["w_in_mla"])[0]
    qng, kvg = f(inp["q_norm_g"])[0], f(inp["kv_norm_g"])[0]
    w_uq, w_ukv, w_out_mla = f(inp["w_uq"])[0], f(inp["w_ukv"])[0], f(inp["w_out_mla"])[0]
    cs = _consts()
    ada_bf = np.ascontiguousarray(ada_b.reshape(2, 24, 128).transpose(2, 0, 1))
    ada_bg = np.ascontiguousarray(ada_b[:, 2048:3072])
    perm_rows = np.concatenate([np.concatenate([np.arange(256 * r, 256 * r + 256),
                                                1024 + np.arange(128 * r, 128 * r + 128)]) for r in range(4)])
    wout0 = np.ascontiguousarray(w_out_rf[perm_rows])
    kr_cols = 384 + _ROPE_PERM
    wkr = np.ascontiguousarray(np.concatenate([w_in_mla[:, 256:320], w_in_mla[:, 384:416]], 1))
    wkrp = np.ascontiguousarray(np.concatenate([w_in_mla[:, 256:320], w_in_mla[:, kr_cols]], 1))
    qperm = np.arange(1536).reshape(16, 96)
    qperm[:, 64:96] = qperm[:, 64:96][:, _ROPE_PERM]
    wuqp = np.ascontiguousarray(w_uq[:, qperm.reshape(-1)])
    maps = []
    for core in range(8):
        b, g = core // 4, core % 4
        cols = np.concatenate([np.arange(256 * g, 256 * g + 256), 1024 + np.arange(128 * g, 128 * g + 128),
                               1536 + np.arange(256 * g, 256 * g + 256), 2560 + np.arange(128 * g, 128 * g + 128)])
        ch = np.arange(256 * g, 256 * g + 256)
        rc, rs = _rope_tables(g)
        idx = np.zeros((32, 128, 1), np.int32)
        for blk in range(8):
            for r in range(4):
                gblk = g * 8 + blk
                idx[blk * 4 + r, :, 0] = ((gblk // 2) * 512 + r * 128 + np.arange(128)) * 2 + gblk % 2
        m = dict(
            x_all=x[b], x_own=np.ascontiguousarray(x[b, TOK * g:TOK * (g + 1)]), ctx=ctx[b],
            cvec=np.ascontiguousarray(np.stack([c[b], c_ctx], -1).reshape(8, 128, 2).transpose(1, 0, 2)),
            ada_w=ada_w, ada_bf=ada_bf, ada_bg=ada_bg, ln_g=f(inp["ln_g"]), ln_b=f(inp["ln_b"]),
            win0=np.ascontiguousarray(w_in_rf[:, cols]),
            convw=np.ascontiguousarray(conv_w[:, ch].reshape(4, 2, 128).transpose(2, 1, 0)),
            convb=_fm(conv_b[ch], 2),
            gatew=np.ascontiguousarray(gw[:, :, 4 * g:4 * g + 4]),
            gateb=np.ascontiguousarray(gb[:, :, ch].reshape(2, 2, 2, 128).transpose(3, 0, 1, 2)),
            lam=np.ascontiguousarray(lam[:, ch].reshape(2, 2, 128).transpose(2, 0, 1)),
            fw=fnw[g], fb=np.ascontiguousarray(fnb[128 * g:128 * g + 128, None]),
            wout0=wout0, idx=np.ascontiguousarray(idx),
            win1=w_in_mla, qng=_fm(qng, 2), kvg=_fm(kvg, 1),
            wuq=w_uq, wuqp=wuqp, wukv=w_ukv, wout1=w_out_mla,
            ropec=rc.astype(ml_dtypes.bfloat16), ropes=rs.astype(ml_dtypes.bfloat16), wkr=wkr, wkrp=wkrp, **cs)
        maps.append(m)
    return maps


_CACHE = {}


def kernel(**inputs):
    maps = prep_inputs(inputs)
    if "full" not in _CACHE:
        _CACHE["full"] = build("full")
    P = _CACHE["full"]
    res = run_bass_kernel_spmd(P.nc, maps, core_ids=list(range(8)))
    out = np.zeros((2, SL, D), np.float32)
    for core in range(8):
        b, g = core // 4, core % 4
        out[b, TOK * g:TOK * (g + 1)] = res.results[core]["out"]
    return out
```
